# Optimizing a Trainium2 kernel written in Bass

```python
import jax, jax.numpy as jnp
from jax import lax
import numpy as np

D_MODEL = 1024
BATCH = 2
SEQ = 8192
DEPTH = 4
DEC_BATCH = 128
DEC_SEQ = 8
PAST_LEN = 8192
PAGE_SIZE = 128

N_A_LAYERS = DEPTH // 2
N_B_LAYERS = DEPTH - N_A_LAYERS
CHUNK = 128
SGU_WIDTH = D_MODEL
SGU_GROUPS = 8
SGU_GROUP_DIM = SGU_WIDTH // SGU_GROUPS
HEAD_DIM = 64
N_Q_HEADS = D_MODEL // HEAD_DIM
N_KV_HEADS = 2
Q_PER_KV = N_Q_HEADS // N_KV_HEADS
WINDOW = 128
D_FF = -(-8 * D_MODEL // (3 * 256)) * 256
RMS_EPS = 1e-6
LN_EPS = 1e-5

kernel_name = "yoco_gmlp_swa_sink_decoder_step"


def rms_norm(x, g):
    xf = x.astype(jnp.float32)
    y = xf * lax.rsqrt(jnp.mean(xf * xf, axis=-1, keepdims=True) + RMS_EPS)
    return (y * g.astype(jnp.float32)).astype(x.dtype)


def layer_norm(x, g, b):
    xf = x.astype(jnp.float32)
    mu = jnp.mean(xf, axis=-1, keepdims=True)
    var = jnp.mean(jnp.square(xf - mu), axis=-1, keepdims=True)
    y = (xf - mu) * lax.rsqrt(var + LN_EPS)
    return (y * g.astype(jnp.float32) + b.astype(jnp.float32)).astype(x.dtype)


def swiglu_ffn(h, w_gate, w_up, w_down):
    return (jax.nn.silu(h @ w_gate) * (h @ w_up)) @ w_down


def gmlp_mixer(h, n, w_in, ln_g, ln_b, w_s, b_s, w_out):
    B, S, _ = h.shape
    z = jax.nn.gelu(h @ w_in, approximate=False)
    u, v = jnp.split(z, 2, axis=-1)
    v = layer_norm(v, ln_g, ln_b)
    vc = v.reshape(B, S // n, n, SGU_GROUPS, SGU_GROUP_DIM)
    causal = jnp.tril(jnp.ones((n, n), dtype=bool))
    w = jnp.where(causal, w_s[:, :n, :n], 0.0)
    mixed = jnp.einsum('gts,bcsgd->bctgd', w, vc) + b_s[:, :n].T[:, :, None]
    y = u * mixed.reshape(B, S, SGU_WIDTH)
    v_rows = vc[:, -1].reshape(B, n, SGU_WIDTH)
    return y @ w_out, v_rows


def sink_softmax(scores, sink):
    s = jnp.broadcast_to(sink.astype(jnp.float32)[:, :, None, None], scores.shape[:-1] + (1,))
    probs = jax.nn.softmax(jnp.concatenate([scores, s], axis=-1), axis=-1)
    return probs[..., :-1]


def swa_prompt(q, k, v, sinks):
    B, S = q.shape[:2]
    nb = S // WINDOW
    qb = q.reshape(B, nb, WINDOW, N_KV_HEADS, Q_PER_KV, HEAD_DIM)
    kb = k.reshape(B, nb, WINDOW, N_KV_HEADS, HEAD_DIM)
    vb = v.reshape(B, nb, WINDOW, N_KV_HEADS, HEAD_DIM)
    pad = ((0, 0), (1, 0), (0, 0), (0, 0), (0, 0))
    kk = jnp.concatenate([jnp.pad(kb, pad)[:, :-1], kb], axis=2)
    vv = jnp.concatenate([jnp.pad(vb, pad)[:, :-1], vb], axis=2)
    qpos = jnp.arange(WINDOW)[:, None] + WINDOW
    kpos = jnp.arange(2 * WINDOW)[None, :]
    band = (kpos <= qpos) & (qpos - kpos < WINDOW)
    blk = jnp.arange(nb)[:, None, None]
    valid = band[None] & ((blk > 0) | (kpos[None] >= WINDOW))
    scores = jnp.einsum('bnqkgd,bnskd->bnkgqs', qb, kk,
                        preferred_element_type=jnp.float32) * (HEAD_DIM ** -0.5)
    scores = jnp.where(valid[None, :, None, None], scores, -jnp.inf)
    probs = sink_softmax(scores, sinks.reshape(N_KV_HEADS, Q_PER_KV))
    out = jnp.einsum('bnkgqs,bnskd->bnqkgd', probs.astype(vv.dtype), vv)
    return out.reshape(B, S, N_Q_HEADS * HEAD_DIM)


def swa_sample(q, k_new, v_new, cache_k, cache_v, sinks):
    DB, T = q.shape[:2]
    cw = cache_k.shape[1]
    kk = jnp.concatenate([cache_k.astype(k_new.dtype), k_new], axis=1)
    vv = jnp.concatenate([cache_v.astype(v_new.dtype), v_new], axis=1)
    qpos = PAST_LEN + jnp.arange(T)
    kpos = jnp.concatenate([PAST_LEN - cw + jnp.arange(cw), PAST_LEN + jnp.arange(T)])
    valid = (kpos[None, :] <= qpos[:, None]) & (qpos[:, None] - kpos[None, :] < WINDOW)
    qg = q.reshape(DB, T, N_KV_HEADS, Q_PER_KV, HEAD_DIM)
    scores = jnp.einsum('btkgd,bskd->bkgts', qg, kk,
                        preferred_element_type=jnp.float32) * (HEAD_DIM ** -0.5)
    scores = jnp.where(valid, scores, -jnp.inf)
    probs = sink_softmax(scores, sinks.reshape(N_KV_HEADS, Q_PER_KV))
    out = jnp.einsum('bkgts,bskd->btkgd', probs.astype(vv.dtype), vv)
    return out.reshape(DB, T, N_Q_HEADS * HEAD_DIM)


def trunk(x, is_prompt, cache_k, cache_v, sg_norm_pre, sg_w_in, sg_ln_g, sg_ln_b, sg_w_s, sg_b_s,
          sg_w_out, sg_norm_post, kv_norm, w_kv, b_kv, sw_norm_pre, sw_w_q, sw_b_q, sw_sinks,
          sw_w_o, sw_b_o, sw_norm_post, f_norm_pre, f_w_gate, f_w_up, f_w_down, f_norm_post):
    B, S, _ = x.shape
    n = CHUNK if is_prompt else S
    v_rows = []
    k_sh = v_sh = None
    for layer in range(DEPTH):
        if layer < N_A_LAYERS:
            i = layer
            h = rms_norm(x, sg_norm_pre[i])
            out, vr = gmlp_mixer(h, n, sg_w_in[i], sg_ln_g[i], sg_ln_b[i], sg_w_s[i], sg_b_s[i], sg_w_out[i])
            v_rows.append(vr)
        else:
            i = layer - N_A_LAYERS
            if i == 0:
                kv = rms_norm(x, kv_norm) @ w_kv + b_kv
                k_sh, v_sh = jnp.split(kv.reshape(B, S, 2 * N_KV_HEADS, HEAD_DIM), 2, axis=2)
            h = rms_norm(x, sw_norm_pre[i])
            q = (h @ sw_w_q[i] + sw_b_q[i]).reshape(B, S, N_Q_HEADS, HEAD_DIM)
            if is_prompt:
                att = swa_prompt(q, k_sh, v_sh, sw_sinks[i])
            else:
                att = swa_sample(q, k_sh, v_sh, cache_k, cache_v, sw_sinks[i])
            out = att @ sw_w_o[i] + sw_b_o[i]
            norm_post = sw_norm_post[i]
        x = x + rms_norm(out, sg_norm_post[layer] if layer < N_A_LAYERS else sw_norm_post[layer - N_A_LAYERS])
        h = rms_norm(x, f_norm_pre[layer])
        x = x + rms_norm(swiglu_ffn(h, f_w_gate[layer], f_w_up[layer], f_w_down[layer]), f_norm_post[layer])
    return x, jnp.stack(v_rows, axis=0), k_sh, v_sh


def setup_inputs(seed: int = 0) -> dict:
    key = jax.random.key(seed)
    ks = iter(jax.random.split(key, 40))
    nrm = lambda shape, scale: jax.random.normal(next(ks), shape, jnp.float32) * scale
    cache_rows = min(WINDOW, PAST_LEN)
    qw = N_Q_HEADS * HEAD_DIM
    kvw = N_KV_HEADS * HEAD_DIM
    na, nb_, d = N_A_LAYERS, N_B_LAYERS, D_MODEL
    return {
        'x_prompt': nrm((BATCH, SEQ, d), 1.0),
        'x_sample': nrm((DEC_BATCH, DEC_SEQ, d), 1.0),
        'cache_k': nrm((DEC_BATCH, cache_rows, N_KV_HEADS, HEAD_DIM), 1.0),
        'cache_v': nrm((DEC_BATCH, cache_rows, N_KV_HEADS, HEAD_DIM), 1.0),
        'sg_norm_pre': 1.0 + nrm((na, d), 0.05),
        'sg_w_in': nrm((na, d, 2 * SGU_WIDTH), d ** -0.5),
        'sg_ln_g': 1.0 + nrm((na, SGU_WIDTH), 0.05),
        'sg_ln_b': nrm((na, SGU_WIDTH), 0.02),
        'sg_w_s': nrm((na, SGU_GROUPS, CHUNK, CHUNK), 0.5 * CHUNK ** -0.5),
        'sg_b_s': 1.0 + nrm((na, SGU_GROUPS, CHUNK), 0.1),
        'sg_w_out': nrm((na, SGU_WIDTH, d), SGU_WIDTH ** -0.5),
        'sg_norm_post': 1.0 + nrm((na, d), 0.05),
        'kv_norm': 1.0 + nrm((d,), 0.05),
        'w_kv': nrm((d, 2 * kvw), d ** -0.5),
        'b_kv': nrm((2 * kvw,), 0.02),
        'sw_norm_pre': 1.0 + nrm((nb_, d), 0.05),
        'sw_w_q': nrm((nb_, d, qw), d ** -0.5),
        'sw_b_q': nrm((nb_, qw), 0.02),
        'sw_sinks': nrm((nb_, N_Q_HEADS), 0.5),
        'sw_w_o': nrm((nb_, qw, d), qw ** -0.5),
        'sw_b_o': nrm((nb_, d), 0.02),
        'sw_norm_post': 1.0 + nrm((nb_, d), 0.05),
        'f_norm_pre': 1.0 + nrm((DEPTH, d), 0.05),
        'f_w_gate': nrm((DEPTH, d, D_FF), d ** -0.5),
        'f_w_up': nrm((DEPTH, d, D_FF), d ** -0.5),
        'f_w_down': nrm((DEPTH, D_FF, d), D_FF ** -0.5),
        'f_norm_post': 1.0 + nrm((DEPTH, d), 0.05),
    }


def reference(x_prompt, x_sample, cache_k, cache_v, sg_norm_pre, sg_w_in, sg_ln_g, sg_ln_b, sg_w_s,
              sg_b_s, sg_w_out, sg_norm_post, kv_norm, w_kv, b_kv, sw_norm_pre, sw_w_q, sw_b_q,
              sw_sinks, sw_w_o, sw_b_o, sw_norm_post, f_norm_pre, f_w_gate, f_w_up, f_w_down,
              f_norm_post):
    weights = (sg_norm_pre, sg_w_in, sg_ln_g, sg_ln_b, sg_w_s, sg_b_s, sg_w_out, sg_norm_post,
               kv_norm, w_kv, b_kv, sw_norm_pre, sw_w_q, sw_b_q, sw_sinks, sw_w_o, sw_b_o,
               sw_norm_post, f_norm_pre, f_w_gate, f_w_up, f_w_down, f_norm_post)
    y_prompt, v_rows_prompt, k_p, v_p = trunk(x_prompt, True, None, None, *weights)
    y_sample, v_rows_sample, k_s, v_s = trunk(x_sample, False, cache_k, cache_v, *weights)
    k_prompt = k_p[:, -WINDOW:]
    v_prompt = v_p[:, -WINDOW:]
    return (y_prompt, y_sample, v_rows_prompt, v_rows_sample, k_prompt, v_prompt, k_s, v_s)
```

```python
import numpy as np
import concourse.bass as bass
import concourse.mybir as mybir
from concourse.ap import AP
from concourse.bass_utils import run_bass_kernel_spmd

F32 = mybir.dt.float32
F32R = mybir.dt.float32r
AF = mybir.ActivationFunctionType
ALU = mybir.AluOpType
AX = mybir.AxisListType

D = 1024
DFF = 2816
NF = 22
NTILE = 18
GT = 6
T = GT * 128
NSLOT = 5
SLOTF = 2048
RMS_EPS = 1e-6
LN_EPS = 1e-5
NEG = -30000.0

C_SGPRE = 0
C_KVN = 16
C_SWPRE = 24
C_FPRE = 40
C_BQ = 72
C_SINKS = 88
C_EPS_RMS = 90
C_EPS_LN = 91
NCOL = 92
R_SGPOST = 0
R_SWPOST = 2
R_FPOST = 4
R_LNG = 8
R_LNB = 10
R_BO = 12
R_BKV = 14
R_SINK = 15
NROW = 16
K_ID = 0
K_TRIL = 128
K_MASKP = 256
K_MASKSC = 512
K_MWIDE = 640
K_ZERO = 888
K_VPAT = 1016
NCONST = 1016 + 132


class Buf:
    def __init__(self, name, handle, row, gran, r=False):
        self.name, self.h, self.row, self.gran, self.r = name, handle, row, gran, r

    def v(self, off=0, dims=None, p0=0, npart=128):
        if dims is None:
            dims = ((1, self.row - off),)
        return View(self, p0, npart, off, tuple(dims))


class View:
    def __init__(self, buf, p0, npart, off, dims):
        self.buf, self.p0, self.npart, self.off, self.dims = buf, p0, npart, off, dims
        self.blk = None
        self.r = buf.r

    def _cp(self, v):
        v.blk = self.blk
        v.r = self.r
        return v

    def asr(self, flag=True):
        v = View(self.buf, self.p0, self.npart, self.off, self.dims)
        v.blk = self.blk
        v.r = flag
        return v

    def wap(self):
        return self.ap(r=self.r)

    def part(self, p0, n):
        return self._cp(View(self.buf, self.p0 + p0, n, self.off, self.dims))

    def __getitem__(self, idx):
        if not isinstance(idx, tuple):
            idx = (idx,)
        off = self.off
        dims = []
        for i, (s, n) in enumerate(self.dims):
            if i < len(idx):
                ix = idx[i]
                if isinstance(ix, int):
                    assert 0 <= ix < n
                    off += ix * s
                else:
                    a = 0 if ix.start is None else ix.start
                    b = n if ix.stop is None else ix.stop
                    assert 0 <= a < b <= n, (a, b, n)
                    off += a * s
                    dims.append((s, b - a))
            else:
                dims.append((s, n))
        return self._cp(View(self.buf, self.p0, self.npart, off, tuple(dims)))

    def reshape(self, *shape):
        assert len(self.dims) == 1 and self.dims[0][0] == 1
        tot = 1
        for s in shape:
            tot *= s
        assert tot == self.dims[0][1]
        dims = []
        st = tot
        for s in shape:
            st //= s
            dims.append((st, s))
        return self._cp(View(self.buf, self.p0, self.npart, self.off, tuple(dims)))

    def bcast_flat(self, n):
        return self._cp(View(self.buf, self.p0, self.npart, self.off, ((0, n),)))

    def bcast_last(self, n):
        return self._cp(View(self.buf, self.p0, self.npart, self.off, self.dims + ((0, n),)))

    def ap(self, r=False):
        a = AP(self.buf.h, self.p0 * self.buf.row + self.off,
               [[self.buf.row, self.npart]] + [[s, n] for s, n in self.dims])
        return a.bitcast(F32R) if r else a

    def keys(self):
        offs = np.array([self.off])
        for s, n in self.dims:
            if s == 0:
                continue
            offs = (offs[:, None] + (np.arange(n) * s)[None, :]).reshape(-1)
        g = self.buf.gran
        blks = np.unique(offs // g)
        return [(self.buf.name, int(b)) for b in blks]


class Ins:
    __slots__ = ("eng", "fn", "rk", "wk", "deps", "dma", "lane", "ms", "ev", "idx", "tag")


ENGS = ("pe", "act", "dve", "pool", "sp")


class Prog:
    def __init__(self, plan=None):
        self.ins = []
        self.plan = plan
        self.wdesc = []
        self.wn = 0
        self.wissued = 0
        self.blk_lastread = {}
        self.blk_dmapos = {}
        self.psum_ptr = 0
        self.tmp_ptr = 0
        self.ptb_ptr = 0
        self.gb_ptr = 0
        self.st_ptr = 0
        self.outkeys = []
        self.nout = 0
        self.tag = ""
        self.live = set()
        self.defer_map = {}
        self.deferred = []

    def add(self, eng, fn, reads, writes, dma=False):
        i = Ins()
        i.eng, i.fn, i.dma = eng, fn, dma
        rk = []
        for v in reads:
            if isinstance(v, View):
                rk.extend(v.keys())
                if v.blk is not None:
                    self.blk_lastread[v.blk] = len(self.ins)
            else:
                rk.append(v)
        wk = []
        for v in writes:
            if isinstance(v, View):
                wk.extend(v.keys())
            else:
                wk.append(v)
        i.rk, i.wk = rk, wk
        i.tag = self.tag
        i.idx = len(self.ins)
        self.ins.append(i)
        return i


def build_program():
    nc = bass.Bass("TRN2", target_bir_lowering=False)

    def din(name, shape, dt=F32):
        return nc.dram_tensor(name, list(shape), dt, kind="ExternalInput").ap()

    def dout(name, shape):
        return nc.dram_tensor(name, list(shape), F32, kind="ExternalOutput").ap()

    xin = din("xin", [NTILE, 128, D])
    ck = din("ck", [16, 128, 128])
    cv = din("cv", [16, 128, 128])
    w_in = din("w_in", [2, D, 2 * D])
    w_out = din("w_out", [2, D, D])
    wst = din("wst", [2, 2, 128, 1024])
    bsb = din("bsb", [2, 2, 1024])
    w_kv = din("w_kv", [D, 256])
    w_q = din("w_q", [2, D, D])
    w_o = din("w_o", [2, D, D])
    wg = din("wg", [4, D, DFF])
    wu = din("wu", [4, D, DFF])
    wd = din("wd", [4, DFF, D])
    colv = din("colv", [128, NCOL])
    rowv = din("rowv", [NROW, D])
    consts = din("consts", [128, NCONST])
    maskh = din("maskh", [128, 256])
    y_o = dout("y", [17, 128, D])
    sv_o = dout("sv", [2, 2, 128, D])
    kv_o = dout("kvo", [2, 128, 256])

    from contextlib import ExitStack
    es = ExitStack()

    def sb(name, n, gran=128, r=False):
        h = es.enter_context(nc.sbuf_tensor(name, [128, n], F32))
        return Buf(name, h, n, gran, r)

    with es:
        BX = sb("X", GT * D)
        BHT = sb("HT", 8 * T, r=True)
        BBIG = sb("BIG", NF * T, r=True)
        BWS = sb("WS", NSLOT * SLOTF, gran=SLOTF, r=True)
        BKT = sb("KT", 2 * 7 * 128, r=True)
        BVV = sb("VV", 7 * 132, r=True)
        BCON = sb("CON", NCONST, gran=8, r=True)
        BMH = sb("MH", 256, gran=256, r=True)
        BGB = sb("GB", 2 * D, gran=D)
        BTMP = sb("TMP", 3 * D, gran=128)
        BPTB = sb("PTB", 2 * D, gran=128, r=True)
        BSTASH = sb("STASH", D, gran=128)
        BCOL = sb("COL", NCOL, gran=1)
        BSK = sb("SK", 92, gran=1)
        BST = sb("ST", 512, gran=1)
        hps = es.enter_context(nc.psum_tensor("PS", [128, 4096], F32))
        BPS = Buf("PS", hps, 4096, 512)

        sems = {}
        for e in ("pe", "act", "dve", "pool"):
            sems[e] = es.enter_context(nc.semaphore("s_" + e))
        NLANE = 12
        lanes = [es.enter_context(nc.semaphore("l%d" % i)) for i in range(NLANE)]

        X = BX.v().reshape(GT, D)
        HT = BHT.v().reshape(8, T)
        OB2 = BHT.v().reshape(GT, D)
        BIG = BBIG.v().reshape(NF, T)
        UT = BBIG.v(0, ((1, 8 * T),)).reshape(8, T)
        VB = BBIG.v(8 * T, ((1, 8 * T),)).reshape(GT, D)
        ATT_T = BBIG.v(8 * T, ((1, 8 * T),)).reshape(8, T)
        SAMP = BBIG.v(16 * T, ((1, 4096),))
        QBD = SAMP[0:2048].reshape(16, 128)
        KTC = SAMP[2048:4096].reshape(16, 128)
        VC = BHT.v(0, ((1, 2048),)).reshape(16, 128)
        WSTV = BHT.v(0, ((1, 2048),)).reshape(2, 8, 128)
        BSBV = BHT.v(2048, ((1, 2048),)).reshape(2, 8, 128)
        KTZ = BKT.v().reshape(2, 7, 128)
        VV2 = BVV.v().reshape(7, 132)

        def VVd(slot):
            return View(BVV, 0, 128, slot * 132, ((66, 2), (1, 64)))
        IDENT = BCON.v(K_ID, ((1, 128),))
        TRIL = BCON.v(K_TRIL, ((1, 128),))
        MASKP = BCON.v(K_MASKP, ((1, 256),))
        MASKSC = BCON.v(K_MASKSC, ((1, 128),))
        MWIDE = BCON.v(K_MWIDE, ((1, 248),))
        MASKH = BMH.v()
        COL = BCOL.v()
        SK = BSK.v()

        def run_body(P):
            def psum(nb):
                p = P.psum_ptr
                for _ in range(16):
                    if p % nb:
                        p += nb - p % nb
                    if p + nb > 8:
                        p = 0
                    if not any((p + j) in P.live for j in range(nb)):
                        break
                    p += nb
                else:
                    raise RuntimeError("no free PSUM bank")
                P.psum_ptr = p + nb
                return BPS.v(p * 512, ((1, nb * 512),))

            def bank_of(v):
                return v.off // 512

            def bank(b, nb=1):
                return BPS.v(b * 512, ((1, nb * 512),))

            def temp():
                return talloc(D)

            def ptbuf():
                return palloc(D)

            def gbt():
                t = P.gb_ptr
                P.gb_ptr = (t + 1) % 2
                return BGB.v(t * D, ((1, D),))

            def stat(n):
                if P.st_ptr + n > 512:
                    P.st_ptr = 0
                v = BST.v(P.st_ptr, ((1, n),))
                P.st_ptr += n
                return v

            def dma(out, in_, reads=(), writes=(), q=None):
                o = out.wap() if isinstance(out, View) else out
                i = in_.ap() if isinstance(in_, View) else in_
                rd = list(reads) + ([in_] if isinstance(in_, View) else [])
                wr = list(writes) + ([out] if isinstance(out, View) else [])
                if q is None:
                    q = "pool" if (isinstance(out, View) and out.r and not isinstance(in_, View)) else "sp"
                return P.add(q, lambda e: e.dma_start(out=o, in_=i), rd, wr, dma=True)

            def dma_out(dram_ap, src):
                key = ("OUT", P.nout)
                P.nout += 1
                P.outkeys.append(key)
                dma(dram_ap, src, writes=[key])

            def mm(out, lhsT, rhs, start, stop):
                o, l, r = out.ap(), lhsT.ap(r=True), rhs.ap(r=True)
                P.add("pe", lambda e: e.matmul(o, l, r, start=start, stop=stop), [lhsT, rhs], [out])

            def tr(out, in_):
                o, i, idn = out.ap(), in_.ap(), IDENT.ap()

                def f(e):
                    try:
                        return e.transpose(o, i, idn)
                    except Exception:
                        print("TRANSPOSE FAIL", P.tag, o, i)
                        raise
                P.add("pe", f, [in_, IDENT], [out])

            def act(out, in_, func, bias=None, scale=None, accum=None, eng="act", xw=()):
                o, i = out.wap(), in_.ap()
                kw = {}
                rd = [in_]
                wr = [out] + list(xw)
                if bias is not None:
                    if isinstance(bias, View):
                        kw["bias"] = bias.ap()
                        rd.append(bias)
                    else:
                        kw["bias"] = bias
                if scale is not None:
                    if isinstance(scale, View):
                        kw["scale"] = scale.ap()
                        rd.append(scale)
                    else:
                        kw["scale"] = scale
                if accum is not None:
                    kw["accum_out"] = accum.ap()
                    wr.append(accum)
                P.add("act", lambda e: e.activation(o, i, func, **kw), rd, wr)

            def tt(eng, out, a, b, op, xw=()):
                o, x, y_ = out.wap(), a.ap(), b.ap()
                P.add(eng, lambda e: e.tensor_tensor(o, x, y_, op), [a, b], [out] + list(xw))

            def ts(eng, out, a, s1, s2, op0, op1=None):
                o, x = out.wap(), a.ap()
                rd = [a]
                if isinstance(s1, View):
                    rd.append(s1)
                    s1 = s1.ap()
                if isinstance(s2, View):
                    rd.append(s2)
                    s2 = s2.ap()
                if op1 is None:
                    P.add(eng, lambda e: e.tensor_scalar(o, x, s1, None, op0), rd, [out])
                else:
                    P.add(eng, lambda e: e.tensor_scalar(o, x, s1, s2, op0, op1), rd, [out])

            def stt(eng, out, a, sc, b, op0, op1):
                o, x, y_ = out.wap(), a.ap(), b.ap()
                rd = [a, b]
                if isinstance(sc, View):
                    rd.append(sc)
                    sc = sc.ap()
                P.add(eng, lambda e: e.scalar_tensor_tensor(o, x, sc, y_, op0, op1), rd, [out])

            def cp(eng, out, in_):
                o, i = out.wap(), in_.ap()
                P.add(eng, lambda e: e.tensor_copy(o, i), [in_], [out])

            def reduce_max(out, in_):
                o, i = out.ap(), in_.ap()
                P.add("dve", lambda e: e.tensor_reduce(o, i, AX.X, ALU.max), [in_], [out])

            def wget(dram_ap, shape, look=None):
                n = P.wn
                P.wn += 1
                nfl = 1
                for s in shape:
                    nfl *= s
                assert nfl <= SLOTF
                if P.plan is None:
                    P.wdesc.append((dram_ap, tuple(shape)))
                else:
                    if look is None:
                        look = NSLOT - 2
                    while P.wissued < len(P.plan) and P.wissued <= n + look:
                        m = P.wissued
                        dap, shp = P.plan[m]
                        tot = 1
                        for s in shp:
                            tot *= s
                        sv = BWS.v((m % NSLOT) * SLOTF, ((1, tot),)).reshape(*shp)
                        P.blk_dmapos[m] = len(P.ins)
                        dma(sv, dap)
                        P.wissued += 1
                v = BWS.v((n % NSLOT) * SLOTF, ((1, nfl),)).reshape(*shape)
                v.blk = n
                return v

            def w_o1(w2d, c0):
                return wget(w2d.rearrange("(k p) c -> p k c", p=128)[:, :, c0:c0 + 256], (8, 256))

            def w_o2(w2d, kh, c0):
                return wget(w2d[kh * 512:(kh + 1) * 512, :].rearrange("(k p) c -> p k c", p=128)[:, :, c0:c0 + 512],
                            (4, 512))

            def load_row(r, n=D):
                g = gbt()
                dma(g[0:n], rowv[r:r + 1, 0:n].partition_broadcast(128))
                return g

            import os as _os2
            _sub = int(_os2.environ.get("KSUB", "999"))

            class _Stop(Exception):
                pass

            def _ph2(k):
                if k >= _sub:
                    raise _Stop()

            def talloc(n):
                p = P.tmp_ptr
                if p % 128:
                    p += 128 - p % 128
                if p + n > 3 * D:
                    p = 0
                P.tmp_ptr = p + n
                return BTMP.v(p, ((1, n),))

            def palloc(n):
                p = P.ptb_ptr
                if p + n > 2 * D:
                    p = 0
                P.ptb_ptr = p + n
                return BPTB.v(p, ((1, n),))

            def rstd_from_ss(ssv, eps):
                act(ssv, ssv, AF.Sqrt, bias=COL[eps:eps + 1], scale=1.0 / D)
                so, si = ssv.ap(), ssv.ap()
                P.add("dve", lambda e: e.reciprocal(so, si), [ssv], [ssv])

            def subgroups(tl):
                n = len(tl)
                a = (n + 1) // 2
                res = [(tl[0] * 128, a * 128)]
                if n - a > 0:
                    res.append(((tl[0] + a) * 128, (n - a) * 128))
                return res

            RSTD = SK[68:76]

            def preload_sqrt():
                d_ = stat(1)
                act(d_, COL[C_EPS_LN:C_EPS_LN + 1], AF.Sqrt)

            def pre_stats(i):
                ss = RSTD[i:i + 1]
                junk = temp()
                act(junk, X[i], AF.Square, accum=ss)
                rstd_from_ss(ss, C_EPS_RMS)

            XNBUF = [BPTB.v(0, ((1, D),)), BPTB.v(D, ((1, D),)), BSTASH.v(0, ((1, D),))]

            def pre_emit(i, colbase):
                slot = P.defer_map.get(i)
                xn = temp() if slot is None else XNBUF[slot]
                act(xn, X[i], AF.Identity, scale=RSTD[i:i + 1])

                def part2():
                    ps = psum(2)
                    for c in range(8):
                        tr(ps[c * 128:(c + 1) * 128], xn[c * 128:(c + 1) * 128])
                    tt("dve", HT[:, i * 128:(i + 1) * 128], ps.reshape(8, 128),
                       COL[colbase:colbase + 8].bcast_last(128), ALU.mult)
                if slot is None:
                    part2()
                else:
                    P.deferred.append(part2)

            def set_defer(tl):
                n = len(tl)
                a_ = (n + 1) // 2
                P.defer_map = {tl[a_ + j]: j for j in range(n - a_)}

            def run_deferred():
                for f_ in P.deferred:
                    f_()
                P.deferred = []
                P.defer_map = {}

            def prenorm_T(tl, colbase, stats=True):
                P.tag = P.tag.split("/")[0] + "/prenorm"
                if stats:
                    for i in tl:
                        pre_stats(i)
                for i in tl:
                    pre_emit(i, colbase)

            class Pipe:
                def __init__(self, rev=True):
                    self.q = []
                    self.rev = rev

                def push(self, *stages):
                    self.q.append(list(stages))
                    self.step()

                def step(self):
                    n = len(self.q)
                    lags = list(range(len(self.q[-1]) if self.q else 0))
                    if self.rev:
                        lags.reverse()
                    for lag in lags:
                        j = n - 1 - lag
                        if j >= 0 and self.q[j][lag] is not None:
                            f = self.q[j][lag]
                            self.q[j][lag] = None
                            f()

                def flush(self):
                    ns = max((len(x) for x in self.q), default=0)
                    for _ in range(ns):
                        self.q.append([None] * ns)
                        self.step()

            def tail_stages(i, evac_fn, ob_fn, ssv_fn, gpost, next_col, final_out):
                def s1():
                    evac_fn()
                    r = ssv_fn()
                    rstd_from_ss(r, C_EPS_RMS)
                    obs = ob_fn()
                    if isinstance(obs, View):
                        obs = [(X[i], obs)]
                    for xs, ov in obs:
                        stt("dve", xs, ov, r, xs, ALU.mult, ALU.add)
                    if final_out is not None:
                        dma_out(final_out, X[i])

                def s2():
                    if next_col is not None:
                        pre_stats(i)

                def s3():
                    if next_col is not None:
                        pre_emit(i, next_col)
                return s1, s2, s3

            def lin1(wsrc, nblk, tl, src, evac):
                P.tag = P.tag.split("/")[0] + "/lin1"
                sgs = subgroups(tl)
                for blk in range(nblk):
                    slot = w_o1(wsrc, blk * 256)
                    for j in range(2):
                        for (s0, n) in sgs:
                            ps = psum(1)
                            for k in range(8):
                                mm(ps[0:n], slot[k, j * 128:(j + 1) * 128], src[k, s0:s0 + n], k == 0, k == 7)
                            evac(ps[0:n], blk * 2 + j, s0, n)

            def lin2(wsrc, c0, tl, src, evac):
                P.tag = P.tag.split("/")[0] + "/lin2"
                for half in range(2):
                    sA = w_o2(wsrc, 0, c0 + half * 512)
                    sB = w_o2(wsrc, 1, c0 + half * 512)
                    for i in tl:
                        ps = psum(1)
                        for k in range(8):
                            mm(ps, src[k, i * 128:(i + 1) * 128], (sA if k < 4 else sB)[k % 4], k == 0, k == 7)
                        evac(ps, i, half)

            SAMPW = BBIG.v(16 * T, ((1, 4096),))
            WSTV2 = SAMPW[0:2048].reshape(2, 8, 128)
            BSBV2 = SAMPW[2048:4096].reshape(2, 1024)

            def gmlp(L, tl, gtile, next_col):
                P.tag = "gmlp%d/ln" % L
                lng = load_row(R_LNG + L)
                lnb = load_row(R_LNB + L)
                kinds = sorted(set(1 if gtile[i] == 17 else 0 for i in tl))
                for kd in kinds:
                    dma(SAMPW[kd * 1024:(kd + 1) * 1024], wst[L, kd])
                    dma(BSBV2[kd], bsb[L, kd:kd + 1, :].partition_broadcast(128))
                    for g in range(8):
                        tt("pool", WSTV2[kd, g], WSTV2[kd, g], TRIL, ALU.mult)
                _ph2(0)
                pipe = Pipe()

                def ln_stages(i):
                    vbi = VB[i]
                    st = {}

                    def s1():
                        stats = stat(12)
                        a0, a1 = stats[0:6].ap(), stats[6:12].ap()
                        v0, v1 = vbi[0:512], vbi[512:1024]
                        x0, x1 = v0.ap(), v1.ap()
                        P.add("dve", lambda e: e.bn_stats(a0, x0), [v0], [stats[0:6]])
                        P.add("dve", lambda e: e.bn_stats(a1, x1), [v1], [stats[6:12]])
                        mv = stat(2)
                        mva, a2r = mv.ap(), stats.reshape(2, 6).ap()
                        P.add("dve", lambda e: e.bn_aggr(mva, a2r), [stats], [mv])
                        rs = stat(1)
                        act(rs, mv[1:2], AF.Sqrt, bias=COL[C_EPS_LN:C_EPS_LN + 1], scale=1.0)
                        st["mv"], st["rs"] = mv, rs

                    def s2():
                        rs, mv = st["rs"], st["mv"]
                        rso = rs.ap()
                        P.add("dve", lambda e: e.reciprocal(rso, rso), [rs], [rs])
                        ts("dve", vbi, vbi, mv[0:1], rs, ALU.subtract, ALU.mult)

                    def s3():
                        tt("dve", vbi, vbi, lng, ALU.mult)
                        tt("dve", vbi, vbi, lnb, ALU.add)
                        if gtile[i] == 16:
                            dma_out(sv_o[L, 0], vbi)
                        elif gtile[i] == 17:
                            dma_out(sv_o[L, 1], vbi)
                    return s1, s2, s3

                def evac_v(ps, i, half):
                    act(VB[i, half * 512:(half + 1) * 512], ps, AF.Gelu)
                    if half == 1:
                        pipe.push(*ln_stages(i))
                P.tag = "gmlp%d" % L
                lin2(w_in[L], D, tl, HT, evac_v)
                pipe.flush()
                _ph2(1)
                lin1(w_in[L], 4, tl, HT, lambda ps, c, s0, n: act(UT[c, s0:s0 + n], ps, AF.Gelu))
                P.tag = "gmlp%d/mix" % L
                preload_sqrt()
                _ph2(2)
                for i in tl:
                    kd = 1 if gtile[i] == 17 else 0
                    ps = psum(2)
                    for g in range(8):
                        mm(ps[g * 128:(g + 1) * 128], VB[i, g * 128:(g + 1) * 128], WSTV2[kd, g], True, True)
                    tmp = temp()
                    tt("dve", tmp, ps, BSBV2[kd], ALU.add)
                    yv = UT[:, i * 128:(i + 1) * 128]
                    tt("pool", yv, tmp.reshape(8, 128), yv, ALU.mult)
                P.tag = "gmlp%d" % L
                _ph2(3)
                gpost = load_row(R_SGPOST + L)
                sss = {i: stat(3) for i in tl}
                pipe2 = Pipe()
                set_defer(tl)

                def evac(ps, i, half):
                    ob = VB[i, half * 512:(half + 1) * 512]
                    junk = talloc(512)
                    xk = [("PSRD",) + tuple(ps.keys())]
                    act(junk, ps, AF.Square, accum=sss[i][half:half + 1], xw=xk)
                    tt("dve", ob, ps, gpost[half * 512:(half + 1) * 512], ALU.mult, xw=xk)
                    if half == 1:
                        def ssv():
                            tt("dve", sss[i][2:3], sss[i][0:1], sss[i][1:2], ALU.add)
                            return sss[i][2:3]
                        pipe2.push(*tail_stages(i, lambda: None, lambda: VB[i], ssv, gpost, next_col, None))
                lin2(w_out[L], 0, tl, UT, evac)
                pipe2.flush()

            def ffn(L, tl, gtile, final, next_col):
                P.tag = "ffn%d/gateup" % L
                sgs = subgroups(tl)

                def gu(sg_, su_, blk, j, s0, n):
                    f = blk * 2 + j
                    pa = psum(1)
                    pb = psum(1)
                    for k in range(8):
                        mm(pa[0:n], sg_[k, j * 128:(j + 1) * 128], HT[k, s0:s0 + n], k == 0, k == 7)
                    for k in range(8):
                        mm(pb[0:n], su_[k, j * 128:(j + 1) * 128], HT[k, s0:s0 + n], k == 0, k == 7)
                    tmp = talloc(n)
                    act(tmp, pa[0:n], AF.Silu)
                    tt("dve", BIG[f, s0:s0 + n], tmp, pb[0:n], ALU.mult)

                first = 0
                if P.deferred and len(sgs) == 2:
                    wsl = []
                    for blk in range(2):
                        g_ = wget(wg[L].rearrange("(k p) c -> p k c", p=128)[:, :, blk * 256:blk * 256 + 256], (8, 256), look=1)
                        u_ = wget(wu[L].rearrange("(k p) c -> p k c", p=128)[:, :, blk * 256:blk * 256 + 256], (8, 256), look=1)
                        wsl.append((g_, u_))
                    for blk in range(2):
                        for j in range(2):
                            gu(wsl[blk][0], wsl[blk][1], blk, j, *sgs[0])
                    run_deferred()
                    for blk in range(2):
                        for j in range(2):
                            gu(wsl[blk][0], wsl[blk][1], blk, j, *sgs[1])
                    first = 2
                else:
                    run_deferred()
                for blk in range(first, 11):
                    sg_ = w_o1(wg[L], blk * 256)
                    su_ = w_o1(wu[L], blk * 256)
                    for j in range(2):
                        for (s0, n) in sgs:
                            gu(sg_, su_, blk, j, s0, n)
                P.tag = "ffn%d/down" % L
                preload_sqrt()
                gpost = load_row(R_FPOST + L)
                wdl = wd[L].rearrange("(f p) c -> p f c", p=128)
                pipe = Pipe(rev=False)
                sss = {i: stat(3) for i in tl}
                stash = {}
                for idx, i in enumerate(tl):
                    stash[i] = BPTB.v(idx * 512, ((1, 512),)) if idx < 4 else BSTASH.v((idx - 4) * 512, ((1, 512),))
                for half in range(2):
                    accs = {}
                    for i in tl:
                        accs[i] = psum(1)
                        P.live.add(bank_of(accs[i]))

                    def getslot(fb, look=None):
                        f0 = fb * 4
                        nf = min(4, NF - f0)
                        return wget(wdl[:, f0:f0 + nf, half * 512:(half + 1) * 512], (nf, 512), look=look), f0, nf

                    def evac1(i):
                        acc = accs[i]
                        if half == 0:
                            xk = [("PSRD",) + tuple(acc.keys())]
                            junk = talloc(512)
                            act(junk, acc, AF.Square, accum=sss[i][0:1], xw=xk)
                            tt("dve", stash[i], acc, gpost[0:512], ALU.mult, xw=xk)
                            P.live.discard(bank_of(acc))
                        else:
                            def mk(i=i, acc=acc):
                                ob1 = talloc(512)

                                def ev():
                                    xk = [("PSRD",) + tuple(acc.keys())]
                                    junk = talloc(512)
                                    act(junk, acc, AF.Square, accum=sss[i][1:2], xw=xk)
                                    tt("dve", ob1, acc, gpost[512:1024], ALU.mult, xw=xk)
                                    P.live.discard(bank_of(acc))

                                def ssv():
                                    tt("dve", sss[i][2:3], sss[i][0:1], sss[i][1:2], ALU.add)
                                    return sss[i][2:3]
                                s1, s2, s3 = tail_stages(i, ev, lambda: [(X[i, 0:512], stash[i]), (X[i, 512:1024], ob1)],
                                                         ssv, gpost, next_col, y_o[gtile[i] - 1] if final else None)
                                return s1, s2, None, None, s3
                            pipe.push(*mk())

                    nfb_major = 3
                    for fb in range(nfb_major):
                        slot, f0, nf = getslot(fb)
                        for ff in range(nf):
                            f = f0 + ff
                            for i in tl:
                                mm(accs[i], BIG[f, i * 128:(i + 1) * 128], slot[ff], f == 0, f == NF - 1)
                    if True:
                        sl3 = getslot(3, look=2)
                        sl4 = getslot(4, look=2)
                        sl5 = getslot(5, look=2)
                        for i in tl:
                            for (slot, f0, nf) in (sl3, sl4, sl5):
                                for ff in range(nf):
                                    f = f0 + ff
                                    mm(accs[i], BIG[f, i * 128:(i + 1) * 128], slot[ff], f == 0, f == NF - 1)
                            evac1(i)
                    else:
                        for i in tl:
                            evac1(i)
                pipe.flush()

            def kvslot(gi, i):
                return i if gi == 0 else i + 1

            def kvproj(gi, tl, gtile):
                P.tag = "kv"
                slot = wget(w_kv.rearrange("(k p) c -> p k c", p=128), (8, 256))
                bkv = load_row(R_BKV, 256)
                for i in tl:
                    ps = psum(1)
                    for k in range(8):
                        mm(ps[0:256], HT[k, i * 128:(i + 1) * 128], slot[k], k == 0, k == 7)
                    kvt = temp()
                    tt("dve", kvt[0:256], ps[0:256], bkv[0:256], ALU.add)
                    s = kvslot(gi, i)
                    cp("pool", VVd(s), kvt[128:256].reshape(2, 64))
                    if gtile[i] == 16:
                        dma_out(kv_o[0], kvt[0:256])
                    elif gtile[i] == 17:
                        dma_out(kv_o[1], kvt[0:256])
                    pt = psum(1)
                    tr(pt[0:128], kvt[0:128])
                    act(KTZ[0, s].part(0, 64), pt[0:128].part(0, 64), AF.Copy)
                    act(KTZ[1, s].part(64, 64), pt[0:128].part(64, 64), AF.Copy)

            def attn_prompt_all(L2, gi, tiles, gtile):
                P.tag = "attn%d/core" % L2
                items = [(i, g) for i in tiles for g in range(8)]
                st = {}

                def stA(n):
                    i, g = items[n]
                    s = kvslot(gi, i)
                    mask = MASKH if gtile[i] == 1 else MASKP
                    ps = bank(n % 3, 1)
                    for k in range(2):
                        o = ps[k * 256:(k + 1) * 256]
                        keys = BKT.v(k * 896 + (s - 1) * 128, ((1, 256),))
                        mm(o, ATT_Q[g, i * 128:(i + 1) * 128], keys, True, False)
                        mm(o, IDENT, mask, False, True)
                    rmax = stat(1)
                    reduce_max(rmax, ps[0:512])
                    negm = stat(1)
                    sk0 = L2 * 16 + 2 * g
                    pidx = 76 + L2 * 8 + g
                    stt("dve", negm, rmax, -0.125, SK[pidx:pidx + 1], ALU.mult, ALU.min)
                    pb = talloc(512)
                    act(pb, ps[0:512], AF.Exp, bias=negm, scale=0.125)
                    tq = stat(2)
                    tt("dve", tq, SK[sk0:sk0 + 2], negm.bcast_flat(2), ALU.add)
                    act(tq, tq, AF.Exp)
                    st[n] = {"pb": pb, "tq": tq}

                def stB(n):
                    pb = st[n]["pb"]
                    pt = bank(3 + n % 2, 1)
                    for k in range(2):
                        for kt in range(2):
                            c = k * 2 + kt
                            tr(pt[c * 128:(c + 1) * 128], pb[k * 256 + kt * 128:k * 256 + (kt + 1) * 128])
                    ptb = palloc(512)
                    cp("act_copy", ptb, pt)
                    st[n]["ptb"] = ptb

                def stC(n):
                    i, g = items[n]
                    s = kvslot(gi, i)
                    ptb = st[n]["ptb"]
                    tq = st[n]["tq"]
                    po = BPS.v(5 * 512, ((1, 132),))
                    for k in range(2):
                        for kt in range(2):
                            c = k * 2 + kt
                            mm(po[k * 66:(k + 1) * 66], ptb[c * 128:(c + 1) * 128],
                               VV2[s - 1 + kt, k * 66:(k + 1) * 66], kt == 0, kt == 1)
                    tt("dve", tq, tq, View(BPS, 0, 128, 5 * 512 + 64, ((66, 2),)), ALU.add)
                    rden = stat(2)
                    rdo, rdi = rden.ap(), tq.ap()
                    P.add("dve", lambda e: e.reciprocal(rdo, rdi), [tq], [rden])
                    att = talloc(128)
                    tt("dve", att.reshape(2, 64), View(BPS, 0, 128, 5 * 512, ((66, 2), (1, 64))),
                       rden.bcast_last(64), ALU.mult)
                    st[n]["att"] = att

                def stD(n):
                    i, g = items[n]
                    pt2 = BPS.v(6 * 512, ((1, 128),))
                    tr(pt2, st[n]["att"])
                    cp("act_copy", ATT_T[g, i * 128:(i + 1) * 128], pt2)
                    del st[n]

                N = len(items)
                for n in range(N + 3):
                    if n < N:
                        stA(n)
                    if 0 <= n - 1 < N:
                        stB(n - 1)
                    if 0 <= n - 2 < N:
                        stC(n - 2)
                    if 0 <= n - 3 < N:
                        stD(n - 3)

            def attn_sample(L2, gi, i):
                s = kvslot(gi, i)
                s0 = i * 128
                for hb in range(2):
                    stg = temp()
                    dma(stg.reshape(8, 128), ck[hb * 8:(hb + 1) * 8].rearrange("b r c -> r b c"))
                    pt = psum(2)
                    for b in range(8):
                        tr(pt[b * 128:(b + 1) * 128], stg[b * 128:(b + 1) * 128])
                    cp("act_copy", BBIG.v(16 * T + 2048 + hb * 1024, ((1, 1024),)), pt)
                dma(BHT.v(0, ((1, 2048),)).reshape(16, 128), cv.rearrange("b r c -> r b c"))
                qbd_all = BBIG.v(16 * T, ((1, 2048),))
                real_cp("pool", qbd_all.reshape(16, 128), View(BCON, 0, 128, K_ZERO, ((0, 16), (1, 128))))
                for k in range(2):
                    src = View(BBIG, k * 64, 64, 0 * T + s0, ((T, 8), (8, 16), (1, 8)))
                    dst = View(BBIG, k * 64, 64, 16 * T + k * 64, ((8, 8), (128, 16), (1, 8)))
                    cp("pool", dst, src)
                negsk = SK[64 + 2 + L2:64 + 2 + L2 + 1]
                sk = COL[C_SINKS + L2:C_SINKS + L2 + 1]
                st = {}

                def sA(b):
                    ps = bank(b % 3, 1)
                    mm(ps[0:128], QBD[b], KTC[b], True, False)
                    mm(ps[0:128], IDENT, MASKSC, False, True)
                    mm(ps[128:256], QBD[b], KTZ[0, s], True, False)
                    mm(ps[128:256], QBD[b], KTZ[1, s], False, False)
                    mm(ps[128:256], IDENT, MWIDE[(15 - b) * 8:(15 - b) * 8 + 128], False, True)
                    rmax = stat(1)
                    reduce_max(rmax, ps[0:256])
                    negm = stat(1)
                    stt("dve", negm, rmax, -0.125, negsk, ALU.mult, ALU.min)
                    rowsum = stat(1)
                    pb = talloc(256)
                    act(pb, ps[0:256], AF.Exp, bias=negm, scale=0.125, accum=rowsum)
                    tq = stat(1)
                    tt("dve", tq, negm, sk, ALU.add)
                    act(tq, tq, AF.Exp)
                    st[b] = {"pb": pb, "tq": tq, "rowsum": rowsum}

                def sB(b):
                    pb = st[b]["pb"]
                    pt = bank(3 + b % 2, 1)
                    tr(pt[0:128], pb[0:128])
                    tr(pt[128:256], pb[128:256])
                    ptb = palloc(256)
                    cp("dve", ptb, pt[0:256])
                    st[b]["ptb"] = ptb

                def sC(b):
                    tq, rowsum, ptb = st[b]["tq"], st[b]["rowsum"], st[b]["ptb"]
                    tt("dve", tq, tq, rowsum, ALU.add)
                    rden = stat(1)
                    rdo, rdi = rden.ap(), tq.ap()
                    P.add("dve", lambda e: e.reciprocal(rdo, rdi), [tq], [rden])
                    po = bank(5, 1)
                    mm(po[0:128], ptb[0:128], VC[b], True, False)
                    mm(po[0:128].reshape(2, 64), ptb[128:256], VVd(s), False, True)
                    ob_ = talloc(128)
                    act(ob_, po[0:128], AF.Identity, scale=rden)
                    st[b]["ob"] = ob_

                def sD(b):
                    pt3 = bank(6, 1)
                    tr(pt3[0:128], st[b]["ob"])
                    for k in range(2):
                        src = View(BPS, k * 64, 64, 6 * 512 + k * 64, ((8, 8), (1, 8)))
                        dst = View(BBIG, k * 64, 64, 8 * T + s0 + b * 8, ((T, 8), (1, 8)))
                        cp("act_copy", dst, src)
                    del st[b]

                for n in range(16 + 3):
                    if n < 16:
                        sA(n)
                    if 0 <= n - 1 < 16:
                        sB(n - 1)
                    if 0 <= n - 2 < 16:
                        sC(n - 2)
                    if 0 <= n - 3 < 16:
                        sD(n - 3)

            def attn_layer(L2, gi, tl, gtile, next_col):
                P.tag = "attn%d" % L2
                lin1(w_q[L2], 4, tl, HT,
                     lambda ps, c, s0, n: act(ATT_Q[c, s0:s0 + n], ps, AF.Identity,
                                              bias=COL[C_BQ + 8 * L2 + c:C_BQ + 8 * L2 + c + 1]))
                attn_prompt_all(L2, gi, [i for i in tl if gtile[i] != 17], gtile)
                for i in tl:
                    if gtile[i] == 17:
                        P.tag = "attn%d/sample" % L2
                        attn_sample(L2, gi, i)
                P.tag = "attn%d" % L2
                bo = load_row(R_BO + L2)
                gpost = load_row(R_SWPOST + L2)
                sss = {i: stat(3) for i in tl}
                OBQ = BBIG.v(0, ((1, GT * D),)).reshape(GT, D)
                pipe = Pipe()
                set_defer(tl)

                def evac(ps, i, half):
                    ob = OBQ[i, half * 512:(half + 1) * 512]
                    tt("dve", ob, ps, bo[half * 512:(half + 1) * 512], ALU.add)
                    junk = talloc(512)
                    act(junk, ob, AF.Square, accum=sss[i][half:half + 1])
                    tt("dve", ob, ob, gpost[half * 512:(half + 1) * 512], ALU.mult)
                    if half == 1:
                        def ssv():
                            tt("dve", sss[i][2:3], sss[i][0:1], sss[i][1:2], ALU.add)
                            return sss[i][2:3]
                        pipe.push(*tail_stages(i, lambda: None, lambda: OBQ[i], ssv, gpost, next_col, None))
                lin2(w_o[L2], 0, tl, ATT_T, evac)
                pipe.flush()

            ATT_Q = UT

            real_cp = cp

            def cp(eng, out, in_):
                if eng == "act_copy":
                    act(out, in_, AF.Copy)
                else:
                    real_cp(eng, out, in_)

            dma(BCON.v(), consts)
            real_cp("pool", BKT.v().reshape(14, 128), View(BCON, 0, 128, K_ZERO, ((0, 14), (1, 128))))
            real_cp("pool", VV2, View(BCON, 0, 128, K_VPAT, ((0, 7), (1, 132))))
            dma(MASKH, maskh)
            dma(COL, colv)
            dma(SK[0:32], rowv[R_SINK:R_SINK + 1, 0:32].partition_broadcast(128))
            ts("dve", SK[32:64], SK[0:32], -1.0, None, ALU.mult)
            ts("dve", SK[66:68], COL[C_SINKS:C_SINKS + 2], -1.0, None, ALU.mult)
            tt("dve", SK[76:92], View(BSK, 0, 128, 32, ((2, 16),)), View(BSK, 0, 128, 33, ((2, 16),)), ALU.min)

            import os as _os
            _stop = int(_os.environ.get("KSTOP", "999"))

            def _ph(k):
                if k >= _stop:
                    raise _Stop()
            try:
              for gi in range(3):
                gtile = [gi * GT + j for j in range(GT)]
                tl = list(range(GT))
                tl2 = [i for i in tl if gtile[i] != 0]
                for i in tl:
                    dma(X[i], xin[gtile[i]], q="pool")
                P.tag = "gmlp0"
                prenorm_T(tl, C_SGPRE)
                _ph(gi * 100 + 0)
                gmlp(0, tl, gtile, C_FPRE)
                _ph(gi * 100 + 1)
                ffn(0, tl, gtile, False, C_SGPRE + 8)
                _ph(gi * 100 + 2)
                gmlp(1, tl, gtile, C_FPRE + 8)
                _ph(gi * 100 + 3)
                ffn(1, tl, gtile, False, C_KVN)
                _ph(gi * 100 + 4)
                kvproj(gi, tl, gtile)
                P.tag = "attn0"
                prenorm_T(tl2, C_SWPRE, stats=False)
                _ph(gi * 100 + 5)
                attn_layer(0, gi, tl2, gtile, C_FPRE + 16)
                _ph(gi * 100 + 6)
                ffn(2, tl2, gtile, False, C_SWPRE + 8)
                _ph(gi * 100 + 7)
                attn_layer(1, gi, tl2, gtile, C_FPRE + 24)
                _ph(gi * 100 + 8)
                ffn(3, tl2, gtile, True, None)
                if gi < 2:
                    last = kvslot(gi, GT - 1)
                    real_cp("pool", KTZ[0, 0], KTZ[0, last])
                    real_cp("pool", KTZ[1, 0], KTZ[1, last])
                    real_cp("pool", VV2[0], VV2[last])
            except _Stop:
                pass
            P.add("sp", lambda e: e.nop(), list(P.outkeys), [])

        P0 = Prog(plan=None)
        run_body(P0)
        P = Prog(plan=P0.wdesc)
        run_body(P)
        assert P.wn == len(P.plan) and P.wissued == len(P.plan), (P.wn, P.wissued, len(P.plan))
        for m, pos in P.blk_dmapos.items():
            if m - NSLOT >= 0:
                assert P.blk_lastread.get(m - NSLOT, -1) < pos, ("slot reuse hazard", m)

        ins = P.ins
        last_w = {}
        readers = {}
        lane_last = {}
        lane_rr = 0
        lane_rr_sp = 0
        for it in ins:
            deps = set()
            for k in it.rk:
                w = last_w.get(k)
                if w is not None:
                    deps.add(w)
            for k in it.wk:
                w = last_w.get(k)
                if w is not None and (ins[w].eng != it.eng or it.dma or ins[w].dma):
                    deps.add(w)
                for r in readers.get(k, ()):
                    if ins[r].eng != it.eng or it.dma or ins[r].dma:
                        deps.add(r)
            if it.dma:
                if it.eng == "sp":
                    it.lane = lane_rr_sp % 4
                    lane_rr_sp += 1
                else:
                    it.lane = 4 + lane_rr % (NLANE - 4)
                    lane_rr += 1
                pl = lane_last.get(it.lane)
                if pl is not None:
                    deps.add(pl)
                lane_last[it.lane] = it.idx
            deps.discard(it.idx)
            if it.eng == "pe":
                deps = {d for d in deps if ins[d].eng != "pe"}
            it.deps = deps
            for k in it.wk:
                last_w[k] = it.idx
                readers[k] = []
            for k in it.rk:
                readers.setdefault(k, []).append(it.idx)
        for it in ins:
            it.ms = False
        for it in ins:
            for d in it.deps:
                ins[d].ms = True
        cnt = {e: 0 for e in ENGS}
        lane_cnt = [0] * NLANE
        for it in ins:
            if it.dma:
                lane_cnt[it.lane] += 1
                it.ev = (lanes[it.lane], 16 * lane_cnt[it.lane])
            elif it.ms:
                cnt[it.eng] += 1
                it.ev = (sems[it.eng], cnt[it.eng])
            else:
                it.ev = None

        per_eng = {e: [] for e in ENGS}
        for it in ins:
            per_eng[it.eng].append(it)

        def emit(ename, e):
            seen = {}
            for it in per_eng[ename]:
                need = {}
                for d in it.deps:
                    sem, val = ins[d].ev
                    key = id(sem)
                    if seen.get(key, 0) >= val:
                        continue
                    if key not in need or need[key][1] < val:
                        need[key] = (sem, val)
                for key, (sem, val) in need.items():
                    e.wait_ge(sem, val)
                    seen[key] = val
                bi = it.fn(e)
                if it.ev is not None:
                    bi.then_inc(it.ev[0], 16 if it.dma else 1)

        with nc.Block() as block:
            @block.sync
            def _(e):
                emit("sp", e)

            @block.tensor
            def _(e):
                emit("pe", e)

            @block.scalar
            def _(e):
                emit("act", e)

            @block.vector
            def _(e):
                emit("dve", e)

            @block.gpsimd
            def _(e):
                emit("pool", e)

    build_program.stats = {e: len(per_eng[e]) for e in ENGS}
    build_program.semmax = (dict(cnt), list(lane_cnt))
    build_program.tags = {e: [it.tag for it in per_eng[e]] for e in ENGS}
    return nc


def _prep_inputs(inp):
    f = lambda a: np.ascontiguousarray(np.asarray(a, dtype=np.float32))
    xp = f(inp["x_prompt"])
    xs = f(inp["x_sample"])
    ckf = f(inp["cache_k"]).reshape(128, 128, 128)
    cvf = f(inp["cache_v"]).reshape(128, 128, 128)
    L = 2
    w_q = f(inp["sw_w_q"]).reshape(L, D, 2, 8, 64).transpose(0, 1, 3, 2, 4).reshape(L, D, D)
    w_o = f(inp["sw_w_o"]).reshape(L, 2, 8, 64, D).transpose(0, 2, 1, 3, 4).reshape(L, D, D)
    b_q = f(inp["sw_b_q"]).reshape(L, 2, 8, 64).transpose(0, 2, 1, 3).reshape(L, D)
    sinks = f(inp["sw_sinks"])
    sinks_p = sinks.reshape(L, 2, 8).transpose(0, 2, 1).reshape(L, 16)
    w_s = f(inp["sg_w_s"])
    b_s = f(inp["sg_b_s"])
    wst = np.zeros((2, 2, 128, 8, 128), np.float32)
    wst[:, 0] = w_s.transpose(0, 3, 1, 2)
    ws8 = w_s[:, :, :8, :8].transpose(0, 3, 1, 2)
    for b in range(16):
        wst[:, 1, b * 8:(b + 1) * 8, :, b * 8:(b + 1) * 8] = ws8
    wst = wst.reshape(2, 2, 128, 1024)
    bsb = np.zeros((2, 2, 8, 128), np.float32)
    bsb[:, 0] = b_s
    bsb[:, 1] = np.tile(b_s[:, :, :8], (1, 1, 16))
    bsb = bsb.reshape(2, 2, 1024)

    def colize(v):
        return v.reshape(8, 128).T

    colv = np.zeros((128, NCOL), np.float32)
    for l in range(2):
        colv[:, C_SGPRE + 8 * l:C_SGPRE + 8 * l + 8] = colize(f(inp["sg_norm_pre"])[l])
        colv[:, C_SWPRE + 8 * l:C_SWPRE + 8 * l + 8] = colize(f(inp["sw_norm_pre"])[l])
        colv[:, C_BQ + 8 * l:C_BQ + 8 * l + 8] = colize(b_q[l])
        colv[:, C_SINKS + l] = np.repeat(sinks[l], 8)
    colv[:, C_KVN:C_KVN + 8] = colize(f(inp["kv_norm"]))
    colv[:, C_EPS_RMS] = RMS_EPS
    colv[:, C_EPS_LN] = LN_EPS
    for l in range(4):
        colv[:, C_FPRE + 8 * l:C_FPRE + 8 * l + 8] = colize(f(inp["f_norm_pre"])[l])
    rowv = np.zeros((NROW, D), np.float32)
    rowv[R_SGPOST:R_SGPOST + 2] = f(inp["sg_norm_post"])
    rowv[R_SWPOST:R_SWPOST + 2] = f(inp["sw_norm_post"])
    rowv[R_FPOST:R_FPOST + 4] = f(inp["f_norm_post"])
    rowv[R_LNG:R_LNG + 2] = f(inp["sg_ln_g"])
    rowv[R_LNB:R_LNB + 2] = f(inp["sg_ln_b"])
    rowv[R_BO:R_BO + 2] = f(inp["sw_b_o"])
    rowv[R_BKV, :256] = f(inp["b_kv"])
    rowv[R_SINK, :32] = sinks_p.reshape(32)

    consts = np.zeros((128, NCONST), np.float32)
    consts[:, K_ID:K_ID + 128] = np.eye(128, dtype=np.float32)
    p = np.arange(128)[:, None]
    j = np.arange(128)[None, :]
    consts[:, K_TRIL:K_TRIL + 128] = (p <= j).astype(np.float32)
    consts[:, K_MASKP:K_MASKP + 128] = np.where(j > p, 0.0, NEG)
    consts[:, K_MASKP + 128:K_MASKP + 256] = np.where(j <= p, 0.0, NEG)
    t_row = (np.arange(128) % 8)[:, None]
    consts[:, K_MASKSC:K_MASKSC + 128] = np.where(j > t_row, 0.0, NEG)
    mw = np.full((128, 248), NEG, np.float32)
    tp = np.arange(8)[None, :]
    mw[:, 120:128] = np.where(tp <= t_row, 0.0, NEG)
    consts[:, K_MWIDE:K_MWIDE + 248] = mw
    consts[:, K_VPAT + 64] = 1.0
    consts[:, K_VPAT + 130] = 1.0
    maskh_norm = consts[:, K_MASKP:K_MASKP + 256].copy()
    maskh_first = maskh_norm.copy()
    maskh_first[:, :128] = NEG

    shared = {
        "w_in": f(inp["sg_w_in"]), "w_out": f(inp["sg_w_out"]), "wst": wst, "bsb": bsb,
        "w_kv": f(inp["w_kv"]), "w_q": np.ascontiguousarray(w_q), "w_o": np.ascontiguousarray(w_o),
        "wg": f(inp["f_w_gate"]), "wu": f(inp["f_w_up"]), "wd": f(inp["f_w_down"]),
        "colv": colv, "rowv": rowv, "consts": consts,
    }
    in_maps = []
    for c in range(8):
        b, q = c // 4, c % 4
        s0 = q * 2048
        xin = np.empty((NTILE, 128, D), np.float32)
        if q == 0:
            xin[0] = xp[b, 0:128]
        else:
            xin[0] = xp[b, s0 - 128:s0]
        xin[1:17] = xp[b, s0:s0 + 2048].reshape(16, 128, D)
        xin[17] = xs[c * 16:(c + 1) * 16].reshape(128, D)
        m = dict(shared)
        m["xin"] = xin
        m["ck"] = np.ascontiguousarray(ckf[c * 16:(c + 1) * 16])
        m["cv"] = np.ascontiguousarray(cvf[c * 16:(c + 1) * 16])
        m["maskh"] = maskh_first if q == 0 else maskh_norm
        in_maps.append(m)
    return in_maps


_NC_CACHE = {}


def kernel(**inputs):
    in_maps = _prep_inputs(inputs)
    if "nc" not in _NC_CACHE:
        _NC_CACHE["nc"] = build_program()
    nc = _NC_CACHE["nc"]
    res = run_bass_kernel_spmd(nc, in_maps, core_ids=list(range(8)))
    rs = res.results
    y_prompt = np.empty((2, 8192, D), np.float32)
    y_sample = np.empty((128, 8, D), np.float32)
    sv_p = np.empty((2, 2, 128, D), np.float32)
    sv_s = np.empty((2, 128, 8, D), np.float32)
    k_p = np.empty((2, 128, 2, 64), np.float32)
    v_p = np.empty((2, 128, 2, 64), np.float32)
    k_s = np.empty((128, 8, 2, 64), np.float32)
    v_s = np.empty((128, 8, 2, 64), np.float32)
    for c in range(8):
        b, q = c // 4, c % 4
        y = np.asarray(rs[c]["y"])
        sv = np.asarray(rs[c]["sv"])
        kvo = np.asarray(rs[c]["kvo"])
        y_prompt[b, q * 2048:(q + 1) * 2048] = y[0:16].reshape(2048, D)
        y_sample[c * 16:(c + 1) * 16] = y[16].reshape(16, 8, D)
        sv_s[:, c * 16:(c + 1) * 16] = sv[:, 1].reshape(2, 16, 8, D)
        k_s[c * 16:(c + 1) * 16] = kvo[1][:, 0:128].reshape(16, 8, 2, 64)
        v_s[c * 16:(c + 1) * 16] = kvo[1][:, 128:256].reshape(16, 8, 2, 64)
        if q == 3:
            sv_p[:, b] = sv[:, 0]
            k_p[b] = kvo[0][:, 0:128].reshape(128, 2, 64)
            v_p[b] = kvo[0][:, 128:256].reshape(128, 2, 64)
    return (y_prompt, y_sample, sv_p, sv_s, k_p, v_p, k_s, v_s)
```

```python
import numpy as np
import concourse.bass as bass
import concourse.mybir as mybir
from concourse.ap import AP
from concourse.bass_utils import run_bass_kernel_spmd

F32 = mybir.dt.float32
F32R = mybir.dt.float32r
AF = mybir.ActivationFunctionType
ALU = mybir.AluOpType
AX = mybir.AxisListType

D = 1024
DFF = 2816
NF = 22
NTILE = 18
GT = 6
T = GT * 128
NSLOT = 5
SLOTF = 2048
RMS_EPS = 1e-6
LN_EPS = 1e-5
NEG = -30000.0

C_SGPRE = 0
C_KVN = 16
C_SWPRE = 24
C_FPRE = 40
C_BQ = 72
C_SINKS = 88
C_EPS_RMS = 90
C_EPS_LN = 91
NCOL = 92
R_SGPOST = 0
R_SWPOST = 2
R_FPOST = 4
R_LNG = 8
R_LNB = 10
R_BO = 12
R_BKV = 14
R_SINK = 15
NROW = 16
K_ID = 0
K_TRIL = 128
K_MASKP = 256
K_MASKSC = 512
K_MWIDE = 640
K_ZERO = 888
K_VPAT = 1016
NCONST = 1016 + 132


class Buf:
    def __init__(self, name, handle, row, gran, r=False):
        self.name, self.h, self.row, self.gran, self.r = name, handle, row, gran, r

    def v(self, off=0, dims=None, p0=0, npart=128):
        if dims is None:
            dims = ((1, self.row - off),)
        return View(self, p0, npart, off, tuple(dims))


class View:
    def __init__(self, buf, p0, npart, off, dims):
        self.buf, self.p0, self.npart, self.off, self.dims = buf, p0, npart, off, dims
        self.blk = None
        self.r = buf.r

    def _cp(self, v):
        v.blk = self.blk
        v.r = self.r
        return v

    def asr(self, flag=True):
        v = View(self.buf, self.p0, self.npart, self.off, self.dims)
        v.blk = self.blk
        v.r = flag
        return v

    def wap(self):
        return self.ap(r=self.r)

    def part(self, p0, n):
        return self._cp(View(self.buf, self.p0 + p0, n, self.off, self.dims))

    def __getitem__(self, idx):
        if not isinstance(idx, tuple):
            idx = (idx,)
        off = self.off
        dims = []
        for i, (s, n) in enumerate(self.dims):
            if i < len(idx):
                ix = idx[i]
                if isinstance(ix, int):
                    assert 0 <= ix < n
                    off += ix * s
                else:
                    a = 0 if ix.start is None else ix.start
                    b = n if ix.stop is None else ix.stop
                    assert 0 <= a < b <= n, (a, b, n)
                    off += a * s
                    dims.append((s, b - a))
            else:
                dims.append((s, n))
        return self._cp(View(self.buf, self.p0, self.npart, off, tuple(dims)))

    def reshape(self, *shape):
        assert len(self.dims) == 1 and self.dims[0][0] == 1
        tot = 1
        for s in shape:
            tot *= s
        assert tot == self.dims[0][1]
        dims = []
        st = tot
        for s in shape:
            st //= s
            dims.append((st, s))
        return self._cp(View(self.buf, self.p0, self.npart, self.off, tuple(dims)))

    def bcast_flat(self, n):
        return self._cp(View(self.buf, self.p0, self.npart, self.off, ((0, n),)))

    def bcast_last(self, n):
        return self._cp(View(self.buf, self.p0, self.npart, self.off, self.dims + ((0, n),)))

    def ap(self, r=False):
        a = AP(self.buf.h, self.p0 * self.buf.row + self.off,
               [[self.buf.row, self.npart]] + [[s, n] for s, n in self.dims])
        return a.bitcast(F32R) if r else a

    def keys(self):
        offs = np.array([self.off])
        for s, n in self.dims:
            if s == 0:
                continue
            offs = (offs[:, None] + (np.arange(n) * s)[None, :]).reshape(-1)
        g = self.buf.gran
        blks = np.unique(offs // g)
        return [(self.buf.name, int(b)) for b in blks]


class Ins:
    __slots__ = ("eng", "fn", "rk", "wk", "deps", "dma", "lane", "ms", "ev", "idx", "tag")


ENGS = ("pe", "act", "dve", "pool", "sp")


class Prog:
    def __init__(self, plan=None):
        self.ins = []
        self.plan = plan
        self.wdesc = []
        self.wn = 0
        self.wissued = 0
        self.blk_lastread = {}
        self.blk_dmapos = {}
        self.psum_ptr = 0
        self.tmp_ptr = 0
        self.ptb_ptr = 0
        self.gb_ptr = 0
        self.st_ptr = 0
        self.outkeys = []
        self.nout = 0
        self.tag = ""
        self.live = set()
        self.defer_map = {}
        self.deferred = []

    def add(self, eng, fn, reads, writes, dma=False):
        i = Ins()
        i.eng, i.fn, i.dma = eng, fn, dma
        rk = []
        for v in reads:
            if isinstance(v, View):
                rk.extend(v.keys())
                if v.blk is not None:
                    self.blk_lastread[v.blk] = len(self.ins)
            else:
                rk.append(v)
        wk = []
        for v in writes:
            if isinstance(v, View):
                wk.extend(v.keys())
            else:
                wk.append(v)
        i.rk, i.wk = rk, wk
        i.tag = self.tag
        i.idx = len(self.ins)
        self.ins.append(i)
        return i


def build_program():
    nc = bass.Bass("TRN2", target_bir_lowering=False)

    def din(name, shape, dt=F32):
        return nc.dram_tensor(name, list(shape), dt, kind="ExternalInput").ap()

    def dout(name, shape):
        return nc.dram_tensor(name, list(shape), F32, kind="ExternalOutput").ap()

    xin = din("xin", [NTILE, 128, D])
    ck = din("ck", [16, 128, 128])
    cv = din("cv", [16, 128, 128])
    w_in = din("w_in", [2, D, 2 * D])
    w_out = din("w_out", [2, D, D])
    wst = din("wst", [2, 2, 128, 1024])
    bsb = din("bsb", [2, 2, 1024])
    w_kv = din("w_kv", [D, 256])
    w_q = din("w_q", [2, D, D])
    w_o = din("w_o", [2, D, D])
    wg = din("wg", [4, D, DFF])
    wu = din("wu", [4, D, DFF])
    wd = din("wd", [4, DFF, D])
    colv = din("colv", [128, NCOL])
    rowv = din("rowv", [NROW, D])
    consts = din("consts", [128, NCONST])
    maskh = din("maskh", [128, 256])
    y_o = dout("y", [17, 128, D])
    sv_o = dout("sv", [2, 2, 128, D])
    kv_o = dout("kvo", [2, 128, 256])

    from contextlib import ExitStack
    es = ExitStack()

    def sb(name, n, gran=128, r=False):
        h = es.enter_context(nc.sbuf_tensor(name, [128, n], F32))
        return Buf(name, h, n, gran, r)

    with es:
        BX = sb("X", GT * D)
        BHT = sb("HT", 8 * T, r=True)
        BBIG = sb("BIG", NF * T, r=True)
        BWS = sb("WS", NSLOT * SLOTF, gran=SLOTF, r=True)
        BKT = sb("KT", 2 * 7 * 128, r=True)
        BVV = sb("VV", 7 * 132, r=True)
        BCON = sb("CON", NCONST, gran=8, r=True)
        BMH = sb("MH", 256, gran=256, r=True)
        BGB = sb("GB", 2 * D, gran=D)
        BTMP = sb("TMP", 3 * D, gran=128)
        BPTB = sb("PTB", 2 * D, gran=128, r=True)
        BSTASH = sb("STASH", D, gran=128)
        BCOL = sb("COL", NCOL, gran=1)
        BSK = sb("SK", 92, gran=1)
        BST = sb("ST", 512, gran=1)
        hps = es.enter_context(nc.psum_tensor("PS", [128, 4096], F32))
        BPS = Buf("PS", hps, 4096, 512)

        sems = {}
        for e in ("pe", "act", "dve", "pool"):
            sems[e] = es.enter_context(nc.semaphore("s_" + e))
        NLANE = 12
        lanes = [es.enter_context(nc.semaphore("l%d" % i)) for i in range(NLANE)]

        X = BX.v().reshape(GT, D)
        HT = BHT.v().reshape(8, T)
        OB2 = BHT.v().reshape(GT, D)
        BIG = BBIG.v().reshape(NF, T)
        UT = BBIG.v(0, ((1, 8 * T),)).reshape(8, T)
        VB = BBIG.v(8 * T, ((1, 8 * T),)).reshape(GT, D)
        ATT_T = BBIG.v(8 * T, ((1, 8 * T),)).reshape(8, T)
        SAMP = BBIG.v(16 * T, ((1, 4096),))
        QBD = SAMP[0:2048].reshape(16, 128)
        KTC = SAMP[2048:4096].reshape(16, 128)
        VC = BHT.v(0, ((1, 2048),)).reshape(16, 128)
        WSTV = BHT.v(0, ((1, 2048),)).reshape(2, 8, 128)
        BSBV = BHT.v(2048, ((1, 2048),)).reshape(2, 8, 128)
        KTZ = BKT.v().reshape(2, 7, 128)
        VV2 = BVV.v().reshape(7, 132)

        def VVd(slot):
            return View(BVV, 0, 128, slot * 132, ((66, 2), (1, 64)))
        IDENT = BCON.v(K_ID, ((1, 128),))
        TRIL = BCON.v(K_TRIL, ((1, 128),))
        MASKP = BCON.v(K_MASKP, ((1, 256),))
        MASKSC = BCON.v(K_MASKSC, ((1, 128),))
        MWIDE = BCON.v(K_MWIDE, ((1, 248),))
        MASKH = BMH.v()
        COL = BCOL.v()
        SK = BSK.v()

        def run_body(P):
            def psum(nb):
                p = P.psum_ptr
                for _ in range(16):
                    if p % nb:
                        p += nb - p % nb
                    if p + nb > 8:
                        p = 0
                    if not any((p + j) in P.live for j in range(nb)):
                        break
                    p += nb
                else:
                    raise RuntimeError("no free PSUM bank")
                P.psum_ptr = p + nb
                return BPS.v(p * 512, ((1, nb * 512),))

            def bank_of(v):
                return v.off // 512

            def bank(b, nb=1):
                return BPS.v(b * 512, ((1, nb * 512),))

            def temp():
                return talloc(D)

            def ptbuf():
                return palloc(D)

            def gbt():
                t = P.gb_ptr
                P.gb_ptr = (t + 1) % 2
                return BGB.v(t * D, ((1, D),))

            def stat(n):
                if P.st_ptr + n > 512:
                    P.st_ptr = 0
                v = BST.v(P.st_ptr, ((1, n),))
                P.st_ptr += n
                return v

            def dma(out, in_, reads=(), writes=(), q=None):
                o = out.wap() if isinstance(out, View) else out
                i = in_.ap() if isinstance(in_, View) else in_
                rd = list(reads) + ([in_] if isinstance(in_, View) else [])
                wr = list(writes) + ([out] if isinstance(out, View) else [])
                if q is None:
                    q = "pool" if (isinstance(out, View) and out.r and not isinstance(in_, View)) else "sp"
                return P.add(q, lambda e: e.dma_start(out=o, in_=i), rd, wr, dma=True)

            def dma_out(dram_ap, src):
                key = ("OUT", P.nout)
                P.nout += 1
                P.outkeys.append(key)
                dma(dram_ap, src, writes=[key])

            def mm(out, lhsT, rhs, start, stop):
                o, l, r = out.ap(), lhsT.ap(r=True), rhs.ap(r=True)
                P.add("pe", lambda e: e.matmul(o, l, r, start=start, stop=stop), [lhsT, rhs], [out])

            def tr(out, in_):
                o, i, idn = out.ap(), in_.ap(), IDENT.ap()

                def f(e):
                    try:
                        return e.transpose(o, i, idn)
                    except Exception:
                        print("TRANSPOSE FAIL", P.tag, o, i)
                        raise
                P.add("pe", f, [in_, IDENT], [out])

            def act(out, in_, func, bias=None, scale=None, accum=None, eng="act", xw=()):
                o, i = out.wap(), in_.ap()
                kw = {}
                rd = [in_]
                wr = [out] + list(xw)
                if bias is not None:
                    if isinstance(bias, View):
                        kw["bias"] = bias.ap()
                        rd.append(bias)
                    else:
                        kw["bias"] = bias
                if scale is not None:
                    if isinstance(scale, View):
                        kw["scale"] = scale.ap()
                        rd.append(scale)
                    else:
                        kw["scale"] = scale
                if accum is not None:
                    kw["accum_out"] = accum.ap()
                    wr.append(accum)
                P.add("act", lambda e: e.activation(o, i, func, **kw), rd, wr)

            def tt(eng, out, a, b, op, xw=()):
                o, x, y_ = out.wap(), a.ap(), b.ap()
                P.add(eng, lambda e: e.tensor_tensor(o, x, y_, op), [a, b], [out] + list(xw))

            def ts(eng, out, a, s1, s2, op0, op1=None):
                o, x = out.wap(), a.ap()
                rd = [a]
                if isinstance(s1, View):
                    rd.append(s1)
                    s1 = s1.ap()
                if isinstance(s2, View):
                    rd.append(s2)
                    s2 = s2.ap()
                if op1 is None:
                    P.add(eng, lambda e: e.tensor_scalar(o, x, s1, None, op0), rd, [out])
                else:
                    P.add(eng, lambda e: e.tensor_scalar(o, x, s1, s2, op0, op1), rd, [out])

            def stt(eng, out, a, sc, b, op0, op1):
                o, x, y_ = out.wap(), a.ap(), b.ap()
                rd = [a, b]
                if isinstance(sc, View):
                    rd.append(sc)
                    sc = sc.ap()
                P.add(eng, lambda e: e.scalar_tensor_tensor(o, x, sc, y_, op0, op1), rd, [out])

            def cp(eng, out, in_):
                o, i = out.wap(), in_.ap()
                P.add(eng, lambda e: e.tensor_copy(o, i), [in_], [out])

            def reduce_max(out, in_):
                o, i = out.ap(), in_.ap()
                P.add("dve", lambda e: e.tensor_reduce(o, i, AX.X, ALU.max), [in_], [out])

            def wget(dram_ap, shape, look=None):
                n = P.wn
                P.wn += 1
                nfl = 1
                for s in shape:
                    nfl *= s
                assert nfl <= SLOTF
                if P.plan is None:
                    P.wdesc.append((dram_ap, tuple(shape)))
                else:
                    if look is None:
                        look = NSLOT - 2
                    while P.wissued < len(P.plan) and P.wissued <= n + look:
                        m = P.wissued
                        dap, shp = P.plan[m]
                        tot = 1
                        for s in shp:
                            tot *= s
                        sv = BWS.v((m % NSLOT) * SLOTF, ((1, tot),)).reshape(*shp)
                        P.blk_dmapos[m] = len(P.ins)
                        dma(sv, dap)
                        P.wissued += 1
                v = BWS.v((n % NSLOT) * SLOTF, ((1, nfl),)).reshape(*shape)
                v.blk = n
                return v

            def w_o1(w2d, c0):
                return wget(w2d.rearrange("(k p) c -> p k c", p=128)[:, :, c0:c0 + 256], (8, 256))

            def w_o2(w2d, kh, c0):
                return wget(w2d[kh * 512:(kh + 1) * 512, :].rearrange("(k p) c -> p k c", p=128)[:, :, c0:c0 + 512],
                            (4, 512))

            def load_row(r, n=D):
                g = gbt()
                dma(g[0:n], rowv[r:r + 1, 0:n].partition_broadcast(128))
                return g

            import os as _os2
            _sub = int(_os2.environ.get("KSUB", "999"))

            class _Stop(Exception):
                pass

            def _ph2(k):
                if k >= _sub:
                    raise _Stop()

            def talloc(n):
                p = P.tmp_ptr
                if p % 128:
                    p += 128 - p % 128
                if p + n > 3 * D:
                    p = 0
                P.tmp_ptr = p + n
                return BTMP.v(p, ((1, n),))

            def palloc(n):
                p = P.ptb_ptr
                if p + n > 2 * D:
                    p = 0
                P.ptb_ptr = p + n
                return BPTB.v(p, ((1, n),))

            def rstd_from_ss(ssv, eps):
                act(ssv, ssv, AF.Sqrt, bias=COL[eps:eps + 1], scale=1.0 / D)
                so, si = ssv.ap(), ssv.ap()
                P.add("dve", lambda e: e.reciprocal(so, si), [ssv], [ssv])

            def subgroups(tl):
                n = len(tl)
                a = (n + 1) // 2
                res = [(tl[0] * 128, a * 128)]
                if n - a > 0:
                    res.append(((tl[0] + a) * 128, (n - a) * 128))
                return res

            RSTD = SK[68:76]

            def preload_sqrt():
                d_ = stat(1)
                act(d_, COL[C_EPS_LN:C_EPS_LN + 1], AF.Sqrt)

            def pre_stats(i):
                ss = RSTD[i:i + 1]
                junk = temp()
                act(junk, X[i], AF.Square, accum=ss)
                rstd_from_ss(ss, C_EPS_RMS)

            XNBUF = [BPTB.v(0, ((1, D),)), BPTB.v(D, ((1, D),)), BSTASH.v(0, ((1, D),))]

            def pre_emit(i, colbase):
                slot = P.defer_map.get(i)
                xn = temp() if slot is None else XNBUF[slot]
                act(xn, X[i], AF.Identity, scale=RSTD[i:i + 1])

                def part2():
                    ps = psum(2)
                    for c in range(8):
                        tr(ps[c * 128:(c + 1) * 128], xn[c * 128:(c + 1) * 128])
                    tt("dve", HT[:, i * 128:(i + 1) * 128], ps.reshape(8, 128),
                       COL[colbase:colbase + 8].bcast_last(128), ALU.mult)
                if slot is None:
                    part2()
                else:
                    P.deferred.append(part2)

            def set_defer(tl):
                n = len(tl)
                a_ = (n + 1) // 2
                P.defer_map = {tl[a_ + j]: j for j in range(n - a_)}

            def run_deferred():
                for f_ in P.deferred:
                    f_()
                P.deferred = []
                P.defer_map = {}

            def prenorm_T(tl, colbase, stats=True):
                P.tag = P.tag.split("/")[0] + "/prenorm"
                if stats:
                    for i in tl:
                        pre_stats(i)
                for i in tl:
                    pre_emit(i, colbase)

            class Pipe:
                def __init__(self, rev=True):
                    self.q = []
                    self.rev = rev

                def push(self, *stages):
                    self.q.append(list(stages))
                    self.step()

                def step(self):
                    n = len(self.q)
                    lags = list(range(len(self.q[-1]) if self.q else 0))
                    if self.rev:
                        lags.reverse()
                    for lag in lags:
                        j = n - 1 - lag
                        if j >= 0 and self.q[j][lag] is not None:
                            f = self.q[j][lag]
                            self.q[j][lag] = None
                            f()

                def flush(self):
                    ns = max((len(x) for x in self.q), default=0)
                    for _ in range(ns):
                        self.q.append([None] * ns)
                        self.step()

            def tail_stages(i, evac_fn, ob_fn, ssv_fn, gpost, next_col, final_out):
                def s1():
                    evac_fn()
                    r = ssv_fn()
                    rstd_from_ss(r, C_EPS_RMS)
                    obs = ob_fn()
                    if isinstance(obs, View):
                        obs = [(X[i], obs)]
                    for xs, ov in obs:
                        stt("dve", xs, ov, r, xs, ALU.mult, ALU.add)
                    if final_out is not None:
                        dma_out(final_out, X[i])

                def s2():
                    if next_col is not None:
                        pre_stats(i)

                def s3():
                    if next_col is not None:
                        pre_emit(i, next_col)
                return s1, s2, s3

            def lin1(wsrc, nblk, tl, src, evac):
                P.tag = P.tag.split("/")[0] + "/lin1"
                sgs = subgroups(tl)
                for blk in range(nblk):
                    slot = w_o1(wsrc, blk * 256)
                    for j in range(2):
                        for (s0, n) in sgs:
                            ps = psum(1)
                            for k in range(8):
                                mm(ps[0:n], slot[k, j * 128:(j + 1) * 128], src[k, s0:s0 + n], k == 0, k == 7)
                            evac(ps[0:n], blk * 2 + j, s0, n)

            def lin2(wsrc, c0, tl, src, evac):
                P.tag = P.tag.split("/")[0] + "/lin2"
                for half in range(2):
                    sA = w_o2(wsrc, 0, c0 + half * 512)
                    sB = w_o2(wsrc, 1, c0 + half * 512)
                    for i in tl:
                        ps = psum(1)
                        for k in range(8):
                            mm(ps, src[k, i * 128:(i + 1) * 128], (sA if k < 4 else sB)[k % 4], k == 0, k == 7)
                        evac(ps, i, half)

            SAMPW = BBIG.v(16 * T, ((1, 4096),))
            WSTV2 = SAMPW[0:2048].reshape(2, 8, 128)
            BSBV2 = SAMPW[2048:4096].reshape(2, 1024)

            def gmlp(L, tl, gtile, next_col):
                P.tag = "gmlp%d/ln" % L
                lng = load_row(R_LNG + L)
                lnb = load_row(R_LNB + L)
                kinds = sorted(set(1 if gtile[i] == 17 else 0 for i in tl))
                for kd in kinds:
                    dma(SAMPW[kd * 1024:(kd + 1) * 1024], wst[L, kd])
                    dma(BSBV2[kd], bsb[L, kd:kd + 1, :].partition_broadcast(128))
                    for g in range(8):
                        tt("pool", WSTV2[kd, g], WSTV2[kd, g], TRIL, ALU.mult)
                _ph2(0)
                pipe = Pipe()

                def ln_stages(i):
                    vbi = VB[i]
                    st = {}

                    def s1():
                        stats = stat(12)
                        a0, a1 = stats[0:6].ap(), stats[6:12].ap()
                        v0, v1 = vbi[0:512], vbi[512:1024]
                        x0, x1 = v0.ap(), v1.ap()
                        P.add("dve", lambda e: e.bn_stats(a0, x0), [v0], [stats[0:6]])
                        P.add("dve", lambda e: e.bn_stats(a1, x1), [v1], [stats[6:12]])
                        mv = stat(2)
                        mva, a2r = mv.ap(), stats.reshape(2, 6).ap()
                        P.add("dve", lambda e: e.bn_aggr(mva, a2r), [stats], [mv])
                        rs = stat(1)
                        act(rs, mv[1:2], AF.Sqrt, bias=COL[C_EPS_LN:C_EPS_LN + 1], scale=1.0)
                        st["mv"], st["rs"] = mv, rs

                    def s2():
                        rs, mv = st["rs"], st["mv"]
                        rso = rs.ap()
                        P.add("dve", lambda e: e.reciprocal(rso, rso), [rs], [rs])
                        ts("dve", vbi, vbi, mv[0:1], rs, ALU.subtract, ALU.mult)

                    def s3():
                        tt("dve", vbi, vbi, lng, ALU.mult)
                        tt("dve", vbi, vbi, lnb, ALU.add)
                        if gtile[i] == 16:
                            dma_out(sv_o[L, 0], vbi)
                        elif gtile[i] == 17:
                            dma_out(sv_o[L, 1], vbi)
                    return s1, s2, s3

                def evac_v(ps, i, half):
                    act(VB[i, half * 512:(half + 1) * 512], ps, AF.Gelu)
                    if half == 1:
                        pipe.push(*ln_stages(i))
                P.tag = "gmlp%d" % L
                lin2(w_in[L], D, tl, HT, evac_v)
                pipe.flush()
                _ph2(1)
                lin1(w_in[L], 4, tl, HT, lambda ps, c, s0, n: act(UT[c, s0:s0 + n], ps, AF.Gelu))
                P.tag = "gmlp%d/mix" % L
                preload_sqrt()
                _ph2(2)
                for i in tl:
                    kd = 1 if gtile[i] == 17 else 0
                    ps = psum(2)
                    for g in range(8):
                        mm(ps[g * 128:(g + 1) * 128], VB[i, g * 128:(g + 1) * 128], WSTV2[kd, g], True, True)
                    tmp = temp()
                    tt("dve", tmp, ps, BSBV2[kd], ALU.add)
                    yv = UT[:, i * 128:(i + 1) * 128]
                    tt("pool", yv, tmp.reshape(8, 128), yv, ALU.mult)
                P.tag = "gmlp%d" % L
                _ph2(3)
                gpost = load_row(R_SGPOST + L)
                sss = {i: stat(3) for i in tl}
                pipe2 = Pipe()
                set_defer(tl)

                def evac(ps, i, half):
                    ob = VB[i, half * 512:(half + 1) * 512]
                    junk = talloc(512)
                    xk = [("PSRD",) + tuple(ps.keys())]
                    act(junk, ps, AF.Square, accum=sss[i][half:half + 1], xw=xk)
                    tt("dve", ob, ps, gpost[half * 512:(half + 1) * 512], ALU.mult, xw=xk)
                    if half == 1:
                        def ssv():
                            tt("dve", sss[i][2:3], sss[i][0:1], sss[i][1:2], ALU.add)
                            return sss[i][2:3]
                        s1_, s2_, s3_ = tail_stages(i, lambda: None, lambda: VB[i], ssv, gpost, next_col, None)
                        pipe2.push(s1_, s2_, None, None, s3_)
                lin2(w_out[L], 0, tl, UT, evac)
                pipe2.flush()

            def ffn(L, tl, gtile, final, next_col):
                P.tag = "ffn%d/gateup" % L
                sgs = subgroups(tl)

                def gu(sg_, su_, blk, j, s0, n):
                    f = blk * 2 + j
                    pa = psum(1)
                    pb = psum(1)
                    for k in range(8):
                        mm(pa[0:n], sg_[k, j * 128:(j + 1) * 128], HT[k, s0:s0 + n], k == 0, k == 7)
                    for k in range(8):
                        mm(pb[0:n], su_[k, j * 128:(j + 1) * 128], HT[k, s0:s0 + n], k == 0, k == 7)
                    tmp = talloc(n)
                    act(tmp, pa[0:n], AF.Silu)
                    tt("dve", BIG[f, s0:s0 + n], tmp, pb[0:n], ALU.mult)

                first = 0
                if P.deferred and len(sgs) == 2:
                    wsl = []
                    for blk in range(2):
                        g_ = wget(wg[L].rearrange("(k p) c -> p k c", p=128)[:, :, blk * 256:blk * 256 + 256], (8, 256), look=1)
                        u_ = wget(wu[L].rearrange("(k p) c -> p k c", p=128)[:, :, blk * 256:blk * 256 + 256], (8, 256), look=1)
                        wsl.append((g_, u_))
                    for blk in range(2):
                        for j in range(2):
                            gu(wsl[blk][0], wsl[blk][1], blk, j, *sgs[0])
                    run_deferred()
                    for blk in range(2):
                        for j in range(2):
                            gu(wsl[blk][0], wsl[blk][1], blk, j, *sgs[1])
                    first = 2
                else:
                    run_deferred()
                for blk in range(first, 11):
                    sg_ = w_o1(wg[L], blk * 256)
                    su_ = w_o1(wu[L], blk * 256)
                    for j in range(2):
                        for (s0, n) in sgs:
                            gu(sg_, su_, blk, j, s0, n)
                P.tag = "ffn%d/down" % L
                preload_sqrt()
                gpost = load_row(R_FPOST + L)
                wdl = wd[L].rearrange("(f p) c -> p f c", p=128)
                pipe = Pipe(rev=False)
                sss = {i: stat(3) for i in tl}
                stash = {}
                for idx, i in enumerate(tl):
                    stash[i] = BPTB.v(idx * 512, ((1, 512),)) if idx < 4 else BSTASH.v((idx - 4) * 512, ((1, 512),))
                for half in range(2):
                    accs = {}
                    for i in tl:
                        accs[i] = psum(1)
                        P.live.add(bank_of(accs[i]))

                    def getslot(fb, look=None):
                        f0 = fb * 4
                        nf = min(4, NF - f0)
                        return wget(wdl[:, f0:f0 + nf, half * 512:(half + 1) * 512], (nf, 512), look=look), f0, nf

                    def evac1(i):
                        acc = accs[i]
                        if half == 0:
                            xk = [("PSRD",) + tuple(acc.keys())]
                            junk = talloc(512)
                            act(junk, acc, AF.Square, accum=sss[i][0:1], xw=xk)
                            tt("dve", stash[i], acc, gpost[0:512], ALU.mult, xw=xk)
                            P.live.discard(bank_of(acc))
                        else:
                            def mk(i=i, acc=acc):
                                ob1 = talloc(512)

                                def ev():
                                    xk = [("PSRD",) + tuple(acc.keys())]
                                    junk = talloc(512)
                                    act(junk, acc, AF.Square, accum=sss[i][1:2], xw=xk)
                                    tt("dve", ob1, acc, gpost[512:1024], ALU.mult, xw=xk)
                                    P.live.discard(bank_of(acc))

                                def ssv():
                                    tt("dve", sss[i][2:3], sss[i][0:1], sss[i][1:2], ALU.add)
                                    return sss[i][2:3]
                                s1, s2, s3 = tail_stages(i, ev, lambda: [(X[i, 0:512], stash[i]), (X[i, 512:1024], ob1)],
                                                         ssv, gpost, next_col, y_o[gtile[i] - 1] if final else None)
                                return s1, s2, None, None, s3
                            pipe.push(*mk())

                    nfb_major = 3
                    for fb in range(nfb_major):
                        slot, f0, nf = getslot(fb)
                        for ff in range(nf):
                            f = f0 + ff
                            for i in tl:
                                mm(accs[i], BIG[f, i * 128:(i + 1) * 128], slot[ff], f == 0, f == NF - 1)
                    if True:
                        sl3 = getslot(3, look=2)
                        sl4 = getslot(4, look=2)
                        sl5 = getslot(5, look=2)
                        for i in tl:
                            for (slot, f0, nf) in (sl3, sl4, sl5):
                                for ff in range(nf):
                                    f = f0 + ff
                                    mm(accs[i], BIG[f, i * 128:(i + 1) * 128], slot[ff], f == 0, f == NF - 1)
                            evac1(i)
                    else:
                        for i in tl:
                            evac1(i)
                pipe.flush()

            def kvslot(gi, i):
                return i if gi == 0 else i + 1

            def kvproj(gi, tl, gtile):
                P.tag = "kv"
                slot = wget(w_kv.rearrange("(k p) c -> p k c", p=128), (8, 256))
                bkv = load_row(R_BKV, 256)
                for i in tl:
                    ps = psum(1)
                    for k in range(8):
                        mm(ps[0:256], HT[k, i * 128:(i + 1) * 128], slot[k], k == 0, k == 7)
                    kvt = temp()
                    tt("dve", kvt[0:256], ps[0:256], bkv[0:256], ALU.add)
                    s = kvslot(gi, i)
                    cp("pool", VVd(s), kvt[128:256].reshape(2, 64))
                    if gtile[i] == 16:
                        dma_out(kv_o[0], kvt[0:256])
                    elif gtile[i] == 17:
                        dma_out(kv_o[1], kvt[0:256])
                    pt = psum(1)
                    tr(pt[0:128], kvt[0:128])
                    act(KTZ[0, s].part(0, 64), pt[0:128].part(0, 64), AF.Copy)
                    act(KTZ[1, s].part(64, 64), pt[0:128].part(64, 64), AF.Copy)

            def attn_prompt_all(L2, gi, tiles, gtile):
                P.tag = "attn%d/core" % L2
                items = [(i, g) for i in tiles for g in range(8)]
                st = {}

                def stA(n):
                    i, g = items[n]
                    s = kvslot(gi, i)
                    mask = MASKH if gtile[i] == 1 else MASKP
                    ps = bank(n % 3, 1)
                    for k in range(2):
                        o = ps[k * 256:(k + 1) * 256]
                        keys = BKT.v(k * 896 + (s - 1) * 128, ((1, 256),))
                        mm(o, ATT_Q[g, i * 128:(i + 1) * 128], keys, True, False)
                        mm(o, IDENT, mask, False, True)
                    rmax = stat(1)
                    reduce_max(rmax, ps[0:512])
                    negm = stat(1)
                    sk0 = L2 * 16 + 2 * g
                    pidx = 76 + L2 * 8 + g
                    stt("dve", negm, rmax, -0.125, SK[pidx:pidx + 1], ALU.mult, ALU.min)
                    pb = talloc(512)
                    act(pb, ps[0:512], AF.Exp, bias=negm, scale=0.125)
                    tq = stat(2)
                    tt("dve", tq, SK[sk0:sk0 + 2], negm.bcast_flat(2), ALU.add)
                    act(tq, tq, AF.Exp)
                    st[n] = {"pb": pb, "tq": tq}

                def stB(n):
                    pb = st[n]["pb"]
                    pt = bank(3 + n % 2, 1)
                    for k in range(2):
                        for kt in range(2):
                            c = k * 2 + kt
                            tr(pt[c * 128:(c + 1) * 128], pb[k * 256 + kt * 128:k * 256 + (kt + 1) * 128])
                    ptb = palloc(512)
                    cp("act_copy", ptb, pt)
                    st[n]["ptb"] = ptb

                def stC(n):
                    i, g = items[n]
                    s = kvslot(gi, i)
                    ptb = st[n]["ptb"]
                    tq = st[n]["tq"]
                    po = BPS.v(5 * 512, ((1, 132),))
                    for k in range(2):
                        for kt in range(2):
                            c = k * 2 + kt
                            mm(po[k * 66:(k + 1) * 66], ptb[c * 128:(c + 1) * 128],
                               VV2[s - 1 + kt, k * 66:(k + 1) * 66], kt == 0, kt == 1)
                    tt("dve", tq, tq, View(BPS, 0, 128, 5 * 512 + 64, ((66, 2),)), ALU.add)
                    rden = stat(2)
                    rdo, rdi = rden.ap(), tq.ap()
                    P.add("dve", lambda e: e.reciprocal(rdo, rdi), [tq], [rden])
                    att = talloc(128)
                    tt("dve", att.reshape(2, 64), View(BPS, 0, 128, 5 * 512, ((66, 2), (1, 64))),
                       rden.bcast_last(64), ALU.mult)
                    st[n]["att"] = att

                def stD(n):
                    i, g = items[n]
                    pt2 = BPS.v(6 * 512, ((1, 128),))
                    tr(pt2, st[n]["att"])
                    cp("act_copy", ATT_T[g, i * 128:(i + 1) * 128], pt2)
                    del st[n]

                N = len(items)
                for n in range(N + 3):
                    if n < N:
                        stA(n)
                    if 0 <= n - 1 < N:
                        stB(n - 1)
                    if 0 <= n - 2 < N:
                        stC(n - 2)
                    if 0 <= n - 3 < N:
                        stD(n - 3)

            def attn_sample(L2, gi, i):
                s = kvslot(gi, i)
                s0 = i * 128
                for hb in range(2):
                    stg = temp()
                    dma(stg.reshape(8, 128), ck[hb * 8:(hb + 1) * 8].rearrange("b r c -> r b c"))
                    pt = psum(2)
                    for b in range(8):
                        tr(pt[b * 128:(b + 1) * 128], stg[b * 128:(b + 1) * 128])
                    cp("act_copy", BBIG.v(16 * T + 2048 + hb * 1024, ((1, 1024),)), pt)
                dma(BHT.v(0, ((1, 2048),)).reshape(16, 128), cv.rearrange("b r c -> r b c"))
                qbd_all = BBIG.v(16 * T, ((1, 2048),))
                real_cp("pool", qbd_all.reshape(16, 128), View(BCON, 0, 128, K_ZERO, ((0, 16), (1, 128))))
                for k in range(2):
                    src = View(BBIG, k * 64, 64, 0 * T + s0, ((T, 8), (8, 16), (1, 8)))
                    dst = View(BBIG, k * 64, 64, 16 * T + k * 64, ((8, 8), (128, 16), (1, 8)))
                    cp("pool", dst, src)
                negsk = SK[64 + 2 + L2:64 + 2 + L2 + 1]
                sk = COL[C_SINKS + L2:C_SINKS + L2 + 1]
                st = {}

                def sA(b):
                    ps = bank(b % 3, 1)
                    mm(ps[0:128], QBD[b], KTC[b], True, False)
                    mm(ps[0:128], IDENT, MASKSC, False, True)
                    mm(ps[128:256], QBD[b], KTZ[0, s], True, False)
                    mm(ps[128:256], QBD[b], KTZ[1, s], False, False)
                    mm(ps[128:256], IDENT, MWIDE[(15 - b) * 8:(15 - b) * 8 + 128], False, True)
                    rmax = stat(1)
                    reduce_max(rmax, ps[0:256])
                    negm = stat(1)
                    stt("dve", negm, rmax, -0.125, negsk, ALU.mult, ALU.min)
                    rowsum = stat(1)
                    pb = talloc(256)
                    act(pb, ps[0:256], AF.Exp, bias=negm, scale=0.125, accum=rowsum)
                    tq = stat(1)
                    tt("dve", tq, negm, sk, ALU.add)
                    act(tq, tq, AF.Exp)
                    st[b] = {"pb": pb, "tq": tq, "rowsum": rowsum}

                def sB(b):
                    pb = st[b]["pb"]
                    pt = bank(3 + b % 2, 1)
                    tr(pt[0:128], pb[0:128])
                    tr(pt[128:256], pb[128:256])
                    ptb = palloc(256)
                    cp("dve", ptb, pt[0:256])
                    st[b]["ptb"] = ptb

                def sC(b):
                    tq, rowsum, ptb = st[b]["tq"], st[b]["rowsum"], st[b]["ptb"]
                    tt("dve", tq, tq, rowsum, ALU.add)
                    rden = stat(1)
                    rdo, rdi = rden.ap(), tq.ap()
                    P.add("dve", lambda e: e.reciprocal(rdo, rdi), [tq], [rden])
                    po = bank(5, 1)
                    mm(po[0:128], ptb[0:128], VC[b], True, False)
                    mm(po[0:128].reshape(2, 64), ptb[128:256], VVd(s), False, True)
                    ob_ = talloc(128)
                    act(ob_, po[0:128], AF.Identity, scale=rden)
                    st[b]["ob"] = ob_

                def sD(b):
                    pt3 = bank(6, 1)
                    tr(pt3[0:128], st[b]["ob"])
                    for k in range(2):
                        src = View(BPS, k * 64, 64, 6 * 512 + k * 64, ((8, 8), (1, 8)))
                        dst = View(BBIG, k * 64, 64, 8 * T + s0 + b * 8, ((T, 8), (1, 8)))
                        cp("act_copy", dst, src)
                    del st[b]

                for n in range(16 + 3):
                    if n < 16:
                        sA(n)
                    if 0 <= n - 1 < 16:
                        sB(n - 1)
                    if 0 <= n - 2 < 16:
                        sC(n - 2)
                    if 0 <= n - 3 < 16:
                        sD(n - 3)

            def attn_layer(L2, gi, tl, gtile, next_col):
                P.tag = "attn%d" % L2
                lin1(w_q[L2], 4, tl, HT,
                     lambda ps, c, s0, n: act(ATT_Q[c, s0:s0 + n], ps, AF.Identity,
                                              bias=COL[C_BQ + 8 * L2 + c:C_BQ + 8 * L2 + c + 1]))
                attn_prompt_all(L2, gi, [i for i in tl if gtile[i] != 17], gtile)
                for i in tl:
                    if gtile[i] == 17:
                        P.tag = "attn%d/sample" % L2
                        attn_sample(L2, gi, i)
                P.tag = "attn%d" % L2
                bo = load_row(R_BO + L2)
                gpost = load_row(R_SWPOST + L2)
                sss = {i: stat(3) for i in tl}
                OBQ = BBIG.v(0, ((1, GT * D),)).reshape(GT, D)
                pipe = Pipe()
                set_defer(tl)

                def evac(ps, i, half):
                    ob = OBQ[i, half * 512:(half + 1) * 512]
                    tt("dve", ob, ps, bo[half * 512:(half + 1) * 512], ALU.add)
                    junk = talloc(512)
                    act(junk, ob, AF.Square, accum=sss[i][half:half + 1])
                    tt("dve", ob, ob, gpost[half * 512:(half + 1) * 512], ALU.mult)
                    if half == 1:
                        def ssv():
                            tt("dve", sss[i][2:3], sss[i][0:1], sss[i][1:2], ALU.add)
                            return sss[i][2:3]
                        s1_, s2_, s3_ = tail_stages(i, lambda: None, lambda: OBQ[i], ssv, gpost, next_col, None)
                        pipe.push(s1_, s2_, None, None, s3_)
                lin2(w_o[L2], 0, tl, ATT_T, evac)
                pipe.flush()

            ATT_Q = UT

            real_cp = cp

            def cp(eng, out, in_):
                if eng == "act_copy":
                    act(out, in_, AF.Copy)
                else:
                    real_cp(eng, out, in_)

            dma(BCON.v(), consts)
            real_cp("pool", BKT.v().reshape(14, 128), View(BCON, 0, 128, K_ZERO, ((0, 14), (1, 128))))
            real_cp("pool", VV2, View(BCON, 0, 128, K_VPAT, ((0, 7), (1, 132))))
            dma(MASKH, maskh)
            dma(COL, colv)
            dma(SK[0:32], rowv[R_SINK:R_SINK + 1, 0:32].partition_broadcast(128))
            ts("dve", SK[32:64], SK[0:32], -1.0, None, ALU.mult)
            ts("dve", SK[66:68], COL[C_SINKS:C_SINKS + 2], -1.0, None, ALU.mult)
            tt("dve", SK[76:92], View(BSK, 0, 128, 32, ((2, 16),)), View(BSK, 0, 128, 33, ((2, 16),)), ALU.min)

            import os as _os
            _stop = int(_os.environ.get("KSTOP", "999"))

            def _ph(k):
                if k >= _stop:
                    raise _Stop()
            try:
              for gi in range(3):
                gtile = [gi * GT + j for j in range(GT)]
                tl = list(range(GT))
                tl2 = [i for i in tl if gtile[i] != 0]
                for i in tl:
                    dma(X[i], xin[gtile[i]], q="pool")
                P.tag = "gmlp0"
                prenorm_T(tl, C_SGPRE)
                _ph(gi * 100 + 0)
                gmlp(0, tl, gtile, C_FPRE)
                _ph(gi * 100 + 1)
                ffn(0, tl, gtile, False, C_SGPRE + 8)
                _ph(gi * 100 + 2)
                gmlp(1, tl, gtile, C_FPRE + 8)
                _ph(gi * 100 + 3)
                ffn(1, tl, gtile, False, C_KVN)
                _ph(gi * 100 + 4)
                kvproj(gi, tl, gtile)
                P.tag = "attn0"
                prenorm_T(tl2, C_SWPRE, stats=False)
                _ph(gi * 100 + 5)
                attn_layer(0, gi, tl2, gtile, C_FPRE + 16)
                _ph(gi * 100 + 6)
                ffn(2, tl2, gtile, False, C_SWPRE + 8)
                _ph(gi * 100 + 7)
                attn_layer(1, gi, tl2, gtile, C_FPRE + 24)
                _ph(gi * 100 + 8)
                ffn(3, tl2, gtile, True, None)
                if gi < 2:
                    last = kvslot(gi, GT - 1)
                    real_cp("pool", KTZ[0, 0], KTZ[0, last])
                    real_cp("pool", KTZ[1, 0], KTZ[1, last])
                    real_cp("pool", VV2[0], VV2[last])
            except _Stop:
                pass
            P.add("sp", lambda e: e.nop(), list(P.outkeys), [])

        P0 = Prog(plan=None)
        run_body(P0)
        P = Prog(plan=P0.wdesc)
        run_body(P)
        assert P.wn == len(P.plan) and P.wissued == len(P.plan), (P.wn, P.wissued, len(P.plan))
        for m, pos in P.blk_dmapos.items():
            if m - NSLOT >= 0:
                assert P.blk_lastread.get(m - NSLOT, -1) < pos, ("slot reuse hazard", m)

        ins = P.ins
        last_w = {}
        readers = {}
        lane_last = {}
        lane_rr = 0
        lane_rr_sp = 0
        for it in ins:
            deps = set()
            for k in it.rk:
                w = last_w.get(k)
                if w is not None:
                    deps.add(w)
            for k in it.wk:
                w = last_w.get(k)
                if w is not None and (ins[w].eng != it.eng or it.dma or ins[w].dma):
                    deps.add(w)
                for r in readers.get(k, ()):
                    if ins[r].eng != it.eng or it.dma or ins[r].dma:
                        deps.add(r)
            if it.dma:
                if it.eng == "sp":
                    it.lane = lane_rr_sp % 4
                    lane_rr_sp += 1
                else:
                    it.lane = 4 + lane_rr % (NLANE - 4)
                    lane_rr += 1
                pl = lane_last.get(it.lane)
                if pl is not None:
                    deps.add(pl)
                lane_last[it.lane] = it.idx
            deps.discard(it.idx)
            if it.eng == "pe":
                deps = {d for d in deps if ins[d].eng != "pe"}
            it.deps = deps
            for k in it.wk:
                last_w[k] = it.idx
                readers[k] = []
            for k in it.rk:
                readers.setdefault(k, []).append(it.idx)
        for it in ins:
            it.ms = False
        for it in ins:
            for d in it.deps:
                ins[d].ms = True
        cnt = {e: 0 for e in ENGS}
        lane_cnt = [0] * NLANE
        for it in ins:
            if it.dma:
                lane_cnt[it.lane] += 1
                it.ev = (lanes[it.lane], 16 * lane_cnt[it.lane])
            elif it.ms:
                cnt[it.eng] += 1
                it.ev = (sems[it.eng], cnt[it.eng])
            else:
                it.ev = None

        per_eng = {e: [] for e in ENGS}
        for it in ins:
            per_eng[it.eng].append(it)

        def emit(ename, e):
            seen = {}
            for it in per_eng[ename]:
                need = {}
                for d in it.deps:
                    sem, val = ins[d].ev
                    key = id(sem)
                    if seen.get(key, 0) >= val:
                        continue
                    if key not in need or need[key][1] < val:
                        need[key] = (sem, val)
                for key, (sem, val) in need.items():
                    e.wait_ge(sem, val)
                    seen[key] = val
                bi = it.fn(e)
                if it.ev is not None:
                    bi.then_inc(it.ev[0], 16 if it.dma else 1)

        with nc.Block() as block:
            @block.sync
            def _(e):
                emit("sp", e)

            @block.tensor
            def _(e):
                emit("pe", e)

            @block.scalar
            def _(e):
                emit("act", e)

            @block.vector
            def _(e):
                emit("dve", e)

            @block.gpsimd
            def _(e):
                emit("pool", e)

    build_program.stats = {e: len(per_eng[e]) for e in ENGS}
    build_program.semmax = (dict(cnt), list(lane_cnt))
    build_program.tags = {e: [it.tag for it in per_eng[e]] for e in ENGS}
    return nc


def _prep_inputs(inp):
    f = lambda a: np.ascontiguousarray(np.asarray(a, dtype=np.float32))
    xp = f(inp["x_prompt"])
    xs = f(inp["x_sample"])
    ckf = f(inp["cache_k"]).reshape(128, 128, 128)
    cvf = f(inp["cache_v"]).reshape(128, 128, 128)
    L = 2
    w_q = f(inp["sw_w_q"]).reshape(L, D, 2, 8, 64).transpose(0, 1, 3, 2, 4).reshape(L, D, D)
    w_o = f(inp["sw_w_o"]).reshape(L, 2, 8, 64, D).transpose(0, 2, 1, 3, 4).reshape(L, D, D)
    b_q = f(inp["sw_b_q"]).reshape(L, 2, 8, 64).transpose(0, 2, 1, 3).reshape(L, D)
    sinks = f(inp["sw_sinks"])
    sinks_p = sinks.reshape(L, 2, 8).transpose(0, 2, 1).reshape(L, 16)
    w_s = f(inp["sg_w_s"])
    b_s = f(inp["sg_b_s"])
    wst = np.zeros((2, 2, 128, 8, 128), np.float32)
    wst[:, 0] = w_s.transpose(0, 3, 1, 2)
    ws8 = w_s[:, :, :8, :8].transpose(0, 3, 1, 2)
    for b in range(16):
        wst[:, 1, b * 8:(b + 1) * 8, :, b * 8:(b + 1) * 8] = ws8
    wst = wst.reshape(2, 2, 128, 1024)
    bsb = np.zeros((2, 2, 8, 128), np.float32)
    bsb[:, 0] = b_s
    bsb[:, 1] = np.tile(b_s[:, :, :8], (1, 1, 16))
    bsb = bsb.reshape(2, 2, 1024)

    def colize(v):
        return v.reshape(8, 128).T

    colv = np.zeros((128, NCOL), np.float32)
    for l in range(2):
        colv[:, C_SGPRE + 8 * l:C_SGPRE + 8 * l + 8] = colize(f(inp["sg_norm_pre"])[l])
        colv[:, C_SWPRE + 8 * l:C_SWPRE + 8 * l + 8] = colize(f(inp["sw_norm_pre"])[l])
        colv[:, C_BQ + 8 * l:C_BQ + 8 * l + 8] = colize(b_q[l])
        colv[:, C_SINKS + l] = np.repeat(sinks[l], 8)
    colv[:, C_KVN:C_KVN + 8] = colize(f(inp["kv_norm"]))
    colv[:, C_EPS_RMS] = RMS_EPS
    colv[:, C_EPS_LN] = LN_EPS
    for l in range(4):
        colv[:, C_FPRE + 8 * l:C_FPRE + 8 * l + 8] = colize(f(inp["f_norm_pre"])[l])
    rowv = np.zeros((NROW, D), np.float32)
    rowv[R_SGPOST:R_SGPOST + 2] = f(inp["sg_norm_post"])
    rowv[R_SWPOST:R_SWPOST + 2] = f(inp["sw_norm_post"])
    rowv[R_FPOST:R_FPOST + 4] = f(inp["f_norm_post"])
    rowv[R_LNG:R_LNG + 2] = f(inp["sg_ln_g"])
    rowv[R_LNB:R_LNB + 2] = f(inp["sg_ln_b"])
    rowv[R_BO:R_BO + 2] = f(inp["sw_b_o"])
    rowv[R_BKV, :256] = f(inp["b_kv"])
    rowv[R_SINK, :32] = sinks_p.reshape(32)

    consts = np.zeros((128, NCONST), np.float32)
    consts[:, K_ID:K_ID + 128] = np.eye(128, dtype=np.float32)
    p = np.arange(128)[:, None]
    j = np.arange(128)[None, :]
    consts[:, K_TRIL:K_TRIL + 128] = (p <= j).astype(np.float32)
    consts[:, K_MASKP:K_MASKP + 128] = np.where(j > p, 0.0, NEG)
    consts[:, K_MASKP + 128:K_MASKP + 256] = np.where(j <= p, 0.0, NEG)
    t_row = (np.arange(128) % 8)[:, None]
    consts[:, K_MASKSC:K_MASKSC + 128] = np.where(j > t_row, 0.0, NEG)
    mw = np.full((128, 248), NEG, np.float32)
    tp = np.arange(8)[None, :]
    mw[:, 120:128] = np.where(tp <= t_row, 0.0, NEG)
    consts[:, K_MWIDE:K_MWIDE + 248] = mw
    consts[:, K_VPAT + 64] = 1.0
    consts[:, K_VPAT + 130] = 1.0
    maskh_norm = consts[:, K_MASKP:K_MASKP + 256].copy()
    maskh_first = maskh_norm.copy()
    maskh_first[:, :128] = NEG

    shared = {
        "w_in": f(inp["sg_w_in"]), "w_out": f(inp["sg_w_out"]), "wst": wst, "bsb": bsb,
        "w_kv": f(inp["w_kv"]), "w_q": np.ascontiguousarray(w_q), "w_o": np.ascontiguousarray(w_o),
        "wg": f(inp["f_w_gate"]), "wu": f(inp["f_w_up"]), "wd": f(inp["f_w_down"]),
        "colv": colv, "rowv": rowv, "consts": consts,
    }
    in_maps = []
    for c in range(8):
        b, q = c // 4, c % 4
        s0 = q * 2048
        xin = np.empty((NTILE, 128, D), np.float32)
        if q == 0:
            xin[0] = xp[b, 0:128]
        else:
            xin[0] = xp[b, s0 - 128:s0]
        xin[1:17] = xp[b, s0:s0 + 2048].reshape(16, 128, D)
        xin[17] = xs[c * 16:(c + 1) * 16].reshape(128, D)
        m = dict(shared)
        m["xin"] = xin
        m["ck"] = np.ascontiguousarray(ckf[c * 16:(c + 1) * 16])
        m["cv"] = np.ascontiguousarray(cvf[c * 16:(c + 1) * 16])
        m["maskh"] = maskh_first if q == 0 else maskh_norm
        in_maps.append(m)
    return in_maps


_NC_CACHE = {}


def kernel(**inputs):
    in_maps = _prep_inputs(inputs)
    if "nc" not in _NC_CACHE:
        _NC_CACHE["nc"] = build_program()
    nc = _NC_CACHE["nc"]
    res = run_bass_kernel_spmd(nc, in_maps, core_ids=list(range(8)))
    rs = res.results
    y_prompt = np.empty((2, 8192, D), np.float32)
    y_sample = np.empty((128, 8, D), np.float32)
    sv_p = np.empty((2, 2, 128, D), np.float32)
    sv_s = np.empty((2, 128, 8, D), np.float32)
    k_p = np.empty((2, 128, 2, 64), np.float32)
    v_p = np.empty((2, 128, 2, 64), np.float32)
    k_s = np.empty((128, 8, 2, 64), np.float32)
    v_s = np.empty((128, 8, 2, 64), np.float32)
    for c in range(8):
        b, q = c // 4, c % 4
        y = np.asarray(rs[c]["y"])
        sv = np.asarray(rs[c]["sv"])
        kvo = np.asarray(rs[c]["kvo"])
        y_prompt[b, q * 2048:(q + 1) * 2048] = y[0:16].reshape(2048, D)
        y_sample[c * 16:(c + 1) * 16] = y[16].reshape(16, 8, D)
        sv_s[:, c * 16:(c + 1) * 16] = sv[:, 1].reshape(2, 16, 8, D)
        k_s[c * 16:(c + 1) * 16] = kvo[1][:, 0:128].reshape(16, 8, 2, 64)
        v_s[c * 16:(c + 1) * 16] = kvo[1][:, 128:256].reshape(16, 8, 2, 64)
        if q == 3:
            sv_p[:, b] = sv[:, 0]
            k_p[b] = kvo[0][:, 0:128].reshape(128, 2, 64)
            v_p[b] = kvo[0][:, 128:256].reshape(128, 2, 64)
    return (y_prompt, y_sample, sv_p, sv_s, k_p, v_p, k_s, v_s)
```

```python
import numpy as np
import concourse.bass as bass
import concourse.mybir as mybir
from concourse.ap import AP
from concourse.bass_utils import run_bass_kernel_spmd

F32 = mybir.dt.float32
F32R = mybir.dt.float32r
AF = mybir.ActivationFunctionType
ALU = mybir.AluOpType
AX = mybir.AxisListType

D = 1024
DFF = 2816
NF = 22
NTILE = 18
GT = 6
T = GT * 128
NSLOT = 5
SLOTF = 2048
RMS_EPS = 1e-6
LN_EPS = 1e-5
NEG = -30000.0

C_SGPRE = 0
C_KVN = 16
C_SWPRE = 24
C_FPRE = 40
C_BQ = 72
C_SINKS = 88
C_EPS_RMS = 90
C_EPS_LN = 91
NCOL = 92
R_SGPOST = 0
R_SWPOST = 2
R_FPOST = 4
R_LNG = 8
R_LNB = 10
R_BO = 12
R_BKV = 14
R_SINK = 15
NROW = 16
K_ID = 0
K_TRIL = 128
K_MASKP = 256
K_MASKSC = 512
K_MWIDE = 640
K_ZERO = 888
K_VPAT = 1016
NCONST = 1016 + 132


class Buf:
    def __init__(self, name, handle, row, gran, r=False):
        self.name, self.h, self.row, self.gran, self.r = name, handle, row, gran, r

    def v(self, off=0, dims=None, p0=0, npart=128):
        if dims is None:
            dims = ((1, self.row - off),)
        return View(self, p0, npart, off, tuple(dims))


class View:
    def __init__(self, buf, p0, npart, off, dims):
        self.buf, self.p0, self.npart, self.off, self.dims = buf, p0, npart, off, dims
        self.blk = None
        self.r = buf.r

    def _cp(self, v):
        v.blk = self.blk
        v.r = self.r
        return v

    def asr(self, flag=True):
        v = View(self.buf, self.p0, self.npart, self.off, self.dims)
        v.blk = self.blk
        v.r = flag
        return v

    def wap(self):
        return self.ap(r=self.r)

    def part(self, p0, n):
        return self._cp(View(self.buf, self.p0 + p0, n, self.off, self.dims))

    def __getitem__(self, idx):
        if not isinstance(idx, tuple):
            idx = (idx,)
        off = self.off
        dims = []
        for i, (s, n) in enumerate(self.dims):
            if i < len(idx):
                ix = idx[i]
                if isinstance(ix, int):
                    assert 0 <= ix < n
                    off += ix * s
                else:
                    a = 0 if ix.start is None else ix.start
                    b = n if ix.stop is None else ix.stop
                    assert 0 <= a < b <= n, (a, b, n)
                    off += a * s
                    dims.append((s, b - a))
            else:
                dims.append((s, n))
        return self._cp(View(self.buf, self.p0, self.npart, off, tuple(dims)))

    def reshape(self, *shape):
        assert len(self.dims) == 1 and self.dims[0][0] == 1
        tot = 1
        for s in shape:
            tot *= s
        assert tot == self.dims[0][1]
        dims = []
        st = tot
        for s in shape:
            st //= s
            dims.append((st, s))
        return self._cp(View(self.buf, self.p0, self.npart, self.off, tuple(dims)))

    def bcast_flat(self, n):
        return self._cp(View(self.buf, self.p0, self.npart, self.off, ((0, n),)))

    def bcast_last(self, n):
        return self._cp(View(self.buf, self.p0, self.npart, self.off, self.dims + ((0, n),)))

    def ap(self, r=False):
        a = AP(self.buf.h, self.p0 * self.buf.row + self.off,
               [[self.buf.row, self.npart]] + [[s, n] for s, n in self.dims])
        return a.bitcast(F32R) if r else a

    def keys(self):
        offs = np.array([self.off])
        for s, n in self.dims:
            if s == 0:
                continue
            offs = (offs[:, None] + (np.arange(n) * s)[None, :]).reshape(-1)
        g = self.buf.gran
        blks = np.unique(offs // g)
        return [(self.buf.name, int(b)) for b in blks]


class Ins:
    __slots__ = ("eng", "fn", "rk", "wk", "deps", "dma", "lane", "ms", "ev", "idx", "tag")


ENGS = ("pe", "act", "dve", "pool", "sp")


class Prog:
    def __init__(self, plan=None):
        self.ins = []
        self.plan = plan
        self.wdesc = []
        self.wn = 0
        self.wissued = 0
        self.blk_lastread = {}
        self.blk_dmapos = {}
        self.psum_ptr = 0
        self.tmp_ptr = 0
        self.ptb_ptr = 0
        self.gb_ptr = 0
        self.st_ptr = 0
        self.outkeys = []
        self.nout = 0
        self.tag = ""
        self.live = set()
        self.defer_map = {}
        self.deferred = []

    def add(self, eng, fn, reads, writes, dma=False):
        i = Ins()
        i.eng, i.fn, i.dma = eng, fn, dma
        rk = []
        for v in reads:
            if isinstance(v, View):
                rk.extend(v.keys())
                if v.blk is not None:
                    self.blk_lastread[v.blk] = len(self.ins)
            else:
                rk.append(v)
        wk = []
        for v in writes:
            if isinstance(v, View):
                wk.extend(v.keys())
            else:
                wk.append(v)
        i.rk, i.wk = rk, wk
        i.tag = self.tag
        i.idx = len(self.ins)
        self.ins.append(i)
        return i


def build_program():
    nc = bass.Bass("TRN2", target_bir_lowering=False)

    def din(name, shape, dt=F32):
        return nc.dram_tensor(name, list(shape), dt, kind="ExternalInput").ap()

    def dout(name, shape):
        return nc.dram_tensor(name, list(shape), F32, kind="ExternalOutput").ap()

    xin = din("xin", [NTILE, 128, D])
    ck = din("ck", [16, 128, 128])
    cv = din("cv", [16, 128, 128])
    w_in = din("w_in", [2, D, 2 * D])
    w_out = din("w_out", [2, D, D])
    wst = din("wst", [2, 2, 128, 1024])
    bsb = din("bsb", [2, 2, 1024])
    w_kv = din("w_kv", [D, 256])
    w_q = din("w_q", [2, D, D])
    w_o = din("w_o", [2, D, D])
    wg = din("wg", [4, D, DFF])
    wu = din("wu", [4, D, DFF])
    wd = din("wd", [4, DFF, D])
    colv = din("colv", [128, NCOL])
    rowv = din("rowv", [NROW, D])
    consts = din("consts", [128, NCONST])
    maskh = din("maskh", [128, 256])
    y_o = dout("y", [17, 128, D])
    sv_o = dout("sv", [2, 2, 128, D])
    kv_o = dout("kvo", [2, 128, 256])

    from contextlib import ExitStack
    es = ExitStack()

    def sb(name, n, gran=128, r=False):
        h = es.enter_context(nc.sbuf_tensor(name, [128, n], F32))
        return Buf(name, h, n, gran, r)

    with es:
        BX = sb("X", GT * D)
        BHT = sb("HT", 8 * T, r=True)
        BBIG = sb("BIG", NF * T, r=True)
        BWS = sb("WS", NSLOT * SLOTF, gran=SLOTF, r=True)
        BKT = sb("KT", 2 * 7 * 128, r=True)
        BVV = sb("VV", 7 * 132, r=True)
        BCON = sb("CON", NCONST, gran=8, r=True)
        BMH = sb("MH", 256, gran=256, r=True)
        BGB = sb("GB", 2 * D, gran=D)
        BTMP = sb("TMP", 3 * D, gran=128)
        BPTB = sb("PTB", 2 * D, gran=128, r=True)
        BSTASH = sb("STASH", D, gran=128)
        BCOL = sb("COL", NCOL, gran=1)
        BSK = sb("SK", 92, gran=1)
        BST = sb("ST", 512, gran=1)
        hps = es.enter_context(nc.psum_tensor("PS", [128, 4096], F32))
        BPS = Buf("PS", hps, 4096, 512)

        sems = {}
        for e in ("pe", "act", "dve", "pool"):
            sems[e] = es.enter_context(nc.semaphore("s_" + e))
        NLANE = 12
        lanes = [es.enter_context(nc.semaphore("l%d" % i)) for i in range(NLANE)]

        X = BX.v().reshape(GT, D)
        HT = BHT.v().reshape(8, T)
        OB2 = BHT.v().reshape(GT, D)
        BIG = BBIG.v().reshape(NF, T)
        UT = BBIG.v(0, ((1, 8 * T),)).reshape(8, T)
        VB = BBIG.v(8 * T, ((1, 8 * T),)).reshape(GT, D)
        ATT_T = BBIG.v(8 * T, ((1, 8 * T),)).reshape(8, T)
        SAMP = BBIG.v(16 * T, ((1, 4096),))
        QBD = SAMP[0:2048].reshape(16, 128)
        KTC = SAMP[2048:4096].reshape(16, 128)
        VC = BHT.v(0, ((1, 2048),)).reshape(16, 128)
        WSTV = BHT.v(0, ((1, 2048),)).reshape(2, 8, 128)
        BSBV = BHT.v(2048, ((1, 2048),)).reshape(2, 8, 128)
        KTZ = BKT.v().reshape(2, 7, 128)
        VV2 = BVV.v().reshape(7, 132)

        def VVd(slot):
            return View(BVV, 0, 128, slot * 132, ((66, 2), (1, 64)))
        IDENT = BCON.v(K_ID, ((1, 128),))
        TRIL = BCON.v(K_TRIL, ((1, 128),))
        MASKP = BCON.v(K_MASKP, ((1, 256),))
        MASKSC = BCON.v(K_MASKSC, ((1, 128),))
        MWIDE = BCON.v(K_MWIDE, ((1, 248),))
        MASKH = BMH.v()
        COL = BCOL.v()
        SK = BSK.v()

        def run_body(P):
            def psum(nb):
                p = P.psum_ptr
                for _ in range(16):
                    if p % nb:
                        p += nb - p % nb
                    if p + nb > 8:
                        p = 0
                    if not any((p + j) in P.live for j in range(nb)):
                        break
                    p += nb
                else:
                    raise RuntimeError("no free PSUM bank")
                P.psum_ptr = p + nb
                return BPS.v(p * 512, ((1, nb * 512),))

            def bank_of(v):
                return v.off // 512

            def bank(b, nb=1):
                return BPS.v(b * 512, ((1, nb * 512),))

            def temp():
                return talloc(D)

            def ptbuf():
                return palloc(D)

            def gbt():
                t = P.gb_ptr
                P.gb_ptr = (t + 1) % 2
                return BGB.v(t * D, ((1, D),))

            def stat(n):
                if P.st_ptr + n > 512:
                    P.st_ptr = 0
                v = BST.v(P.st_ptr, ((1, n),))
                P.st_ptr += n
                return v

            def dma(out, in_, reads=(), writes=(), q=None):
                o = out.wap() if isinstance(out, View) else out
                i = in_.ap() if isinstance(in_, View) else in_
                rd = list(reads) + ([in_] if isinstance(in_, View) else [])
                wr = list(writes) + ([out] if isinstance(out, View) else [])
                if q is None:
                    q = "pool" if (isinstance(out, View) and out.r and not isinstance(in_, View)) else "sp"
                return P.add(q, lambda e: e.dma_start(out=o, in_=i), rd, wr, dma=True)

            def dma_out(dram_ap, src):
                key = ("OUT", P.nout)
                P.nout += 1
                P.outkeys.append(key)
                dma(dram_ap, src, writes=[key])

            def mm(out, lhsT, rhs, start, stop):
                o, l, r = out.ap(), lhsT.ap(r=True), rhs.ap(r=True)
                P.add("pe", lambda e: e.matmul(o, l, r, start=start, stop=stop), [lhsT, rhs], [out])

            def tr(out, in_):
                o, i, idn = out.ap(), in_.ap(), IDENT.ap()

                def f(e):
                    try:
                        return e.transpose(o, i, idn)
                    except Exception:
                        print("TRANSPOSE FAIL", P.tag, o, i)
                        raise
                P.add("pe", f, [in_, IDENT], [out])

            def act(out, in_, func, bias=None, scale=None, accum=None, eng="act", xw=()):
                o, i = out.wap(), in_.ap()
                kw = {}
                rd = [in_]
                wr = [out] + list(xw)
                if bias is not None:
                    if isinstance(bias, View):
                        kw["bias"] = bias.ap()
                        rd.append(bias)
                    else:
                        kw["bias"] = bias
                if scale is not None:
                    if isinstance(scale, View):
                        kw["scale"] = scale.ap()
                        rd.append(scale)
                    else:
                        kw["scale"] = scale
                if accum is not None:
                    kw["accum_out"] = accum.ap()
                    wr.append(accum)
                P.add("act", lambda e: e.activation(o, i, func, **kw), rd, wr)

            def tt(eng, out, a, b, op, xw=()):
                o, x, y_ = out.wap(), a.ap(), b.ap()
                P.add(eng, lambda e: e.tensor_tensor(o, x, y_, op), [a, b], [out] + list(xw))

            def ts(eng, out, a, s1, s2, op0, op1=None):
                o, x = out.wap(), a.ap()
                rd = [a]
                if isinstance(s1, View):
                    rd.append(s1)
                    s1 = s1.ap()
                if isinstance(s2, View):
                    rd.append(s2)
                    s2 = s2.ap()
                if op1 is None:
                    P.add(eng, lambda e: e.tensor_scalar(o, x, s1, None, op0), rd, [out])
                else:
                    P.add(eng, lambda e: e.tensor_scalar(o, x, s1, s2, op0, op1), rd, [out])

            def stt(eng, out, a, sc, b, op0, op1):
                o, x, y_ = out.wap(), a.ap(), b.ap()
                rd = [a, b]
                if isinstance(sc, View):
                    rd.append(sc)
                    sc = sc.ap()
                P.add(eng, lambda e: e.scalar_tensor_tensor(o, x, sc, y_, op0, op1), rd, [out])

            def cp(eng, out, in_):
                o, i = out.wap(), in_.ap()
                P.add(eng, lambda e: e.tensor_copy(o, i), [in_], [out])

            def reduce_max(out, in_):
                o, i = out.ap(), in_.ap()
                P.add("dve", lambda e: e.tensor_reduce(o, i, AX.X, ALU.max), [in_], [out])

            def wget(dram_ap, shape, look=None):
                n = P.wn
                P.wn += 1
                nfl = 1
                for s in shape:
                    nfl *= s
                assert nfl <= SLOTF
                if P.plan is None:
                    P.wdesc.append((dram_ap, tuple(shape)))
                else:
                    if look is None:
                        look = NSLOT - 2
                    while P.wissued < len(P.plan) and P.wissued <= n + look:
                        m = P.wissued
                        dap, shp = P.plan[m]
                        tot = 1
                        for s in shp:
                            tot *= s
                        sv = BWS.v((m % NSLOT) * SLOTF, ((1, tot),)).reshape(*shp)
                        P.blk_dmapos[m] = len(P.ins)
                        dma(sv, dap)
                        P.wissued += 1
                v = BWS.v((n % NSLOT) * SLOTF, ((1, nfl),)).reshape(*shape)
                v.blk = n
                return v

            def w_o1(w2d, c0):
                return wget(w2d.rearrange("(k p) c -> p k c", p=128)[:, :, c0:c0 + 256], (8, 256))

            def w_o2(w2d, kh, c0):
                return wget(w2d[kh * 512:(kh + 1) * 512, :].rearrange("(k p) c -> p k c", p=128)[:, :, c0:c0 + 512],
                            (4, 512))

            def load_row(r, n=D):
                g = gbt()
                dma(g[0:n], rowv[r:r + 1, 0:n].partition_broadcast(128))
                return g

            import os as _os2
            _sub = int(_os2.environ.get("KSUB", "999"))

            class _Stop(Exception):
                pass

            def _ph2(k):
                if k >= _sub:
                    raise _Stop()

            def talloc(n):
                p = P.tmp_ptr
                if p % 128:
                    p += 128 - p % 128
                if p + n > 3 * D:
                    p = 0
                P.tmp_ptr = p + n
                return BTMP.v(p, ((1, n),))

            def palloc(n):
                p = P.ptb_ptr
                if p + n > 2 * D:
                    p = 0
                P.ptb_ptr = p + n
                return BPTB.v(p, ((1, n),))

            def rstd_from_ss(ssv, eps):
                act(ssv, ssv, AF.Sqrt, bias=COL[eps:eps + 1], scale=1.0 / D)
                so, si = ssv.ap(), ssv.ap()
                P.add("dve", lambda e: e.reciprocal(so, si), [ssv], [ssv])

            def subgroups(tl):
                n = len(tl)
                a = (n + 1) // 2
                res = [(tl[0] * 128, a * 128)]
                if n - a > 0:
                    res.append(((tl[0] + a) * 128, (n - a) * 128))
                return res

            RSTD = SK[68:76]

            def preload_sqrt():
                d_ = stat(1)
                act(d_, COL[C_EPS_LN:C_EPS_LN + 1], AF.Sqrt)

            def pre_stats(i):
                ss = RSTD[i:i + 1]
                junk = temp()
                act(junk, X[i], AF.Square, accum=ss)
                rstd_from_ss(ss, C_EPS_RMS)

            XNBUF = [BPTB.v(0, ((1, D),)), BPTB.v(D, ((1, D),)), BSTASH.v(0, ((1, D),))]

            def pre_emit(i, colbase):
                slot = P.defer_map.get(i)
                xn = temp() if slot is None else XNBUF[slot]
                act(xn, X[i], AF.Identity, scale=RSTD[i:i + 1])

                def part2():
                    ps = psum(2)
                    for c in range(8):
                        tr(ps[c * 128:(c + 1) * 128], xn[c * 128:(c + 1) * 128])
                    tt("dve", HT[:, i * 128:(i + 1) * 128], ps.reshape(8, 128),
                       COL[colbase:colbase + 8].bcast_last(128), ALU.mult)
                if slot is None:
                    part2()
                else:
                    P.deferred.append(part2)

            def set_defer(tl):
                n = len(tl)
                a_ = (n + 1) // 2
                P.defer_map = {tl[a_ + j]: j for j in range(n - a_)}

            def run_deferred():
                for f_ in P.deferred:
                    f_()
                P.deferred = []
                P.defer_map = {}

            def prenorm_T(tl, colbase, stats=True):
                P.tag = P.tag.split("/")[0] + "/prenorm"
                if stats:
                    for i in tl:
                        pre_stats(i)
                for i in tl:
                    pre_emit(i, colbase)

            class Pipe:
                def __init__(self, rev=True):
                    self.q = []
                    self.rev = rev

                def push(self, *stages):
                    self.q.append(list(stages))
                    self.step()

                def step(self):
                    n = len(self.q)
                    lags = list(range(len(self.q[-1]) if self.q else 0))
                    if self.rev:
                        lags.reverse()
                    for lag in lags:
                        j = n - 1 - lag
                        if j >= 0 and self.q[j][lag] is not None:
                            f = self.q[j][lag]
                            self.q[j][lag] = None
                            f()

                def flush(self):
                    ns = max((len(x) for x in self.q), default=0)
                    for _ in range(ns):
                        self.q.append([None] * ns)
                        self.step()

            def tail_stages(i, evac_fn, ob_fn, ssv_fn, gpost, next_col, final_out):
                def s1():
                    evac_fn()
                    r = ssv_fn()
                    rstd_from_ss(r, C_EPS_RMS)
                    obs = ob_fn()
                    if isinstance(obs, View):
                        obs = [(X[i], obs)]
                    for xs, ov in obs:
                        stt("dve", xs, ov, r, xs, ALU.mult, ALU.add)
                    if final_out is not None:
                        dma_out(final_out, X[i])

                def s2():
                    if next_col is not None:
                        pre_stats(i)

                def s3():
                    if next_col is not None:
                        pre_emit(i, next_col)
                return s1, s2, s3

            def lin1(wsrc, nblk, tl, src, evac):
                P.tag = P.tag.split("/")[0] + "/lin1"
                sgs = subgroups(tl)
                for blk in range(nblk):
                    slot = w_o1(wsrc, blk * 256)
                    for j in range(2):
                        for (s0, n) in sgs:
                            ps = psum(1)
                            for k in range(8):
                                mm(ps[0:n], slot[k, j * 128:(j + 1) * 128], src[k, s0:s0 + n], k == 0, k == 7)
                            evac(ps[0:n], blk * 2 + j, s0, n)

            def lin2(wsrc, c0, tl, src, evac):
                P.tag = P.tag.split("/")[0] + "/lin2"
                for half in range(2):
                    sA = w_o2(wsrc, 0, c0 + half * 512)
                    sB = w_o2(wsrc, 1, c0 + half * 512)
                    for i in tl:
                        ps = psum(1)
                        for k in range(8):
                            mm(ps, src[k, i * 128:(i + 1) * 128], (sA if k < 4 else sB)[k % 4], k == 0, k == 7)
                        evac(ps, i, half)

            SAMPW = BBIG.v(16 * T, ((1, 4096),))
            WSTV2 = SAMPW[0:2048].reshape(2, 8, 128)
            BSBV2 = SAMPW[2048:4096].reshape(2, 1024)

            def gmlp(L, tl, gtile, next_col):
                P.tag = "gmlp%d/ln" % L
                lng = load_row(R_LNG + L)
                lnb = load_row(R_LNB + L)
                kinds = sorted(set(1 if gtile[i] == 17 else 0 for i in tl))
                for kd in kinds:
                    dma(SAMPW[kd * 1024:(kd + 1) * 1024], wst[L, kd])
                    dma(BSBV2[kd], bsb[L, kd:kd + 1, :].partition_broadcast(128))
                    for g in range(8):
                        tt("pool", WSTV2[kd, g], WSTV2[kd, g], TRIL, ALU.mult)
                _ph2(0)
                pipe = Pipe()

                def ln_stages(i):
                    vbi = VB[i]
                    st = {}

                    def s1():
                        stats = stat(12)
                        a0, a1 = stats[0:6].ap(), stats[6:12].ap()
                        v0, v1 = vbi[0:512], vbi[512:1024]
                        x0, x1 = v0.ap(), v1.ap()
                        P.add("dve", lambda e: e.bn_stats(a0, x0), [v0], [stats[0:6]])
                        P.add("dve", lambda e: e.bn_stats(a1, x1), [v1], [stats[6:12]])
                        mv = stat(2)
                        mva, a2r = mv.ap(), stats.reshape(2, 6).ap()
                        P.add("dve", lambda e: e.bn_aggr(mva, a2r), [stats], [mv])
                        rs = stat(1)
                        act(rs, mv[1:2], AF.Sqrt, bias=COL[C_EPS_LN:C_EPS_LN + 1], scale=1.0)
                        st["mv"], st["rs"] = mv, rs

                    def s2():
                        rs, mv = st["rs"], st["mv"]
                        rso = rs.ap()
                        P.add("dve", lambda e: e.reciprocal(rso, rso), [rs], [rs])
                        ts("dve", vbi, vbi, mv[0:1], rs, ALU.subtract, ALU.mult)

                    def s3():
                        tt("dve", vbi, vbi, lng, ALU.mult)
                        tt("dve", vbi, vbi, lnb, ALU.add)
                        if gtile[i] == 16:
                            dma_out(sv_o[L, 0], vbi)
                        elif gtile[i] == 17:
                            dma_out(sv_o[L, 1], vbi)
                    return s1, s2, s3

                def evac_v(ps, i, half):
                    act(VB[i, half * 512:(half + 1) * 512], ps, AF.Gelu)
                    if half == 1:
                        pipe.push(*ln_stages(i))
                P.tag = "gmlp%d" % L
                lin2(w_in[L], D, tl, HT, evac_v)
                pipe.flush()
                _ph2(1)
                lin1(w_in[L], 4, tl, HT, lambda ps, c, s0, n: act(UT[c, s0:s0 + n], ps, AF.Gelu))
                P.tag = "gmlp%d/mix" % L
                preload_sqrt()
                _ph2(2)
                for i in tl:
                    kd = 1 if gtile[i] == 17 else 0
                    ps = psum(2)
                    for g in range(8):
                        mm(ps[g * 128:(g + 1) * 128], VB[i, g * 128:(g + 1) * 128], WSTV2[kd, g], True, True)
                    tmp = temp()
                    tt("dve", tmp, ps, BSBV2[kd], ALU.add)
                    yv = UT[:, i * 128:(i + 1) * 128]
                    tt("pool", yv, tmp.reshape(8, 128), yv, ALU.mult)
                P.tag = "gmlp%d" % L
                _ph2(3)
                gpost = load_row(R_SGPOST + L)
                sss = {i: stat(3) for i in tl}
                pipe2 = Pipe()
                set_defer(tl)

                def evac(ps, i, half):
                    ob = VB[i, half * 512:(half + 1) * 512]
                    junk = talloc(512)
                    xk = [("PSRD",) + tuple(ps.keys())]
                    act(junk, ps, AF.Square, accum=sss[i][half:half + 1], xw=xk)
                    tt("dve", ob, ps, gpost[half * 512:(half + 1) * 512], ALU.mult, xw=xk)
                    if half == 1:
                        def ssv():
                            tt("dve", sss[i][2:3], sss[i][0:1], sss[i][1:2], ALU.add)
                            return sss[i][2:3]
                        s1_, s2_, s3_ = tail_stages(i, lambda: None, lambda: VB[i], ssv, gpost, next_col, None)
                        pipe2.push(s1_, s2_, None, s3_)
                lin2(w_out[L], 0, tl, UT, evac)
                pipe2.flush()

            def ffn(L, tl, gtile, final, next_col):
                P.tag = "ffn%d/gateup" % L
                sgs = subgroups(tl)

                def gu(sg_, su_, blk, j, s0, n):
                    f = blk * 2 + j
                    pa = psum(1)
                    pb = psum(1)
                    for k in range(8):
                        mm(pa[0:n], sg_[k, j * 128:(j + 1) * 128], HT[k, s0:s0 + n], k == 0, k == 7)
                    for k in range(8):
                        mm(pb[0:n], su_[k, j * 128:(j + 1) * 128], HT[k, s0:s0 + n], k == 0, k == 7)
                    tmp = talloc(n)
                    act(tmp, pa[0:n], AF.Silu)
                    tt("dve", BIG[f, s0:s0 + n], tmp, pb[0:n], ALU.mult)

                first = 0
                if P.deferred and len(sgs) == 2:
                    wsl = []
                    for blk in range(2):
                        g_ = wget(wg[L].rearrange("(k p) c -> p k c", p=128)[:, :, blk * 256:blk * 256 + 256], (8, 256), look=1)
                        u_ = wget(wu[L].rearrange("(k p) c -> p k c", p=128)[:, :, blk * 256:blk * 256 + 256], (8, 256), look=1)
                        wsl.append((g_, u_))
                    for blk in range(2):
                        for j in range(2):
                            gu(wsl[blk][0], wsl[blk][1], blk, j, *sgs[0])
                    run_deferred()
                    for blk in range(2):
                        for j in range(2):
                            gu(wsl[blk][0], wsl[blk][1], blk, j, *sgs[1])
                    first = 2
                else:
                    run_deferred()
                for blk in range(first, 11):
                    sg_ = w_o1(wg[L], blk * 256)
                    su_ = w_o1(wu[L], blk * 256)
                    for j in range(2):
                        for (s0, n) in sgs:
                            gu(sg_, su_, blk, j, s0, n)
                P.tag = "ffn%d/down" % L
                preload_sqrt()
                gpost = load_row(R_FPOST + L)
                wdl = wd[L].rearrange("(f p) c -> p f c", p=128)
                pipe = Pipe(rev=False)
                sss = {i: stat(3) for i in tl}
                stash = {}
                for idx, i in enumerate(tl):
                    stash[i] = BPTB.v(idx * 512, ((1, 512),)) if idx < 4 else BSTASH.v((idx - 4) * 512, ((1, 512),))
                for half in range(2):
                    accs = {}
                    for i in tl:
                        accs[i] = psum(1)
                        P.live.add(bank_of(accs[i]))

                    def getslot(fb, look=None):
                        f0 = fb * 4
                        nf = min(4, NF - f0)
                        return wget(wdl[:, f0:f0 + nf, half * 512:(half + 1) * 512], (nf, 512), look=look), f0, nf

                    def evac1(i):
                        acc = accs[i]
                        if half == 0:
                            xk = [("PSRD",) + tuple(acc.keys())]
                            junk = talloc(512)
                            act(junk, acc, AF.Square, accum=sss[i][0:1], xw=xk)
                            tt("dve", stash[i], acc, gpost[0:512], ALU.mult, xw=xk)
                            P.live.discard(bank_of(acc))
                        else:
                            def mk(i=i, acc=acc):
                                ob1 = talloc(512)

                                def ev():
                                    xk = [("PSRD",) + tuple(acc.keys())]
                                    junk = talloc(512)
                                    act(junk, acc, AF.Square, accum=sss[i][1:2], xw=xk)
                                    tt("dve", ob1, acc, gpost[512:1024], ALU.mult, xw=xk)
                                    P.live.discard(bank_of(acc))

                                def ssv():
                                    tt("dve", sss[i][2:3], sss[i][0:1], sss[i][1:2], ALU.add)
                                    return sss[i][2:3]
                                s1, s2, s3 = tail_stages(i, ev, lambda: [(X[i, 0:512], stash[i]), (X[i, 512:1024], ob1)],
                                                         ssv, gpost, next_col, y_o[gtile[i] - 1] if final else None)
                                return s1, s2, None, None, s3
                            pipe.push(*mk())

                    nfb_major = 3
                    for fb in range(nfb_major):
                        slot, f0, nf = getslot(fb)
                        for ff in range(nf):
                            f = f0 + ff
                            for i in tl:
                                mm(accs[i], BIG[f, i * 128:(i + 1) * 128], slot[ff], f == 0, f == NF - 1)
                    if True:
                        sl3 = getslot(3, look=2)
                        sl4 = getslot(4, look=2)
                        sl5 = getslot(5, look=2)
                        for i in tl:
                            for (slot, f0, nf) in (sl3, sl4, sl5):
                                for ff in range(nf):
                                    f = f0 + ff
                                    mm(accs[i], BIG[f, i * 128:(i + 1) * 128], slot[ff], f == 0, f == NF - 1)
                            evac1(i)
                    else:
                        for i in tl:
                            evac1(i)
                pipe.flush()

            def kvslot(gi, i):
                return i if gi == 0 else i + 1

            def kvproj(gi, tl, gtile):
                P.tag = "kv"
                slot = wget(w_kv.rearrange("(k p) c -> p k c", p=128), (8, 256))
                bkv = load_row(R_BKV, 256)
                for i in tl:
                    ps = psum(1)
                    for k in range(8):
                        mm(ps[0:256], HT[k, i * 128:(i + 1) * 128], slot[k], k == 0, k == 7)
                    kvt = temp()
                    tt("dve", kvt[0:256], ps[0:256], bkv[0:256], ALU.add)
                    s = kvslot(gi, i)
                    cp("pool", VVd(s), kvt[128:256].reshape(2, 64))
                    if gtile[i] == 16:
                        dma_out(kv_o[0], kvt[0:256])
                    elif gtile[i] == 17:
                        dma_out(kv_o[1], kvt[0:256])
                    pt = psum(1)
                    tr(pt[0:128], kvt[0:128])
                    act(KTZ[0, s].part(0, 64), pt[0:128].part(0, 64), AF.Copy)
                    act(KTZ[1, s].part(64, 64), pt[0:128].part(64, 64), AF.Copy)

            def attn_prompt_all(L2, gi, tiles, gtile):
                P.tag = "attn%d/core" % L2
                items = [(i, g) for i in tiles for g in range(8)]
                st = {}

                def stA(n):
                    i, g = items[n]
                    s = kvslot(gi, i)
                    mask = MASKH if gtile[i] == 1 else MASKP
                    ps = bank(n % 3, 1)
                    for k in range(2):
                        o = ps[k * 256:(k + 1) * 256]
                        keys = BKT.v(k * 896 + (s - 1) * 128, ((1, 256),))
                        mm(o, ATT_Q[g, i * 128:(i + 1) * 128], keys, True, False)
                        mm(o, IDENT, mask, False, True)
                    rmax = stat(1)
                    reduce_max(rmax, ps[0:512])
                    negm = stat(1)
                    sk0 = L2 * 16 + 2 * g
                    pidx = 76 + L2 * 8 + g
                    stt("dve", negm, rmax, -0.125, SK[pidx:pidx + 1], ALU.mult, ALU.min)
                    pb = talloc(512)
                    act(pb, ps[0:512], AF.Exp, bias=negm, scale=0.125)
                    tq = stat(2)
                    tt("dve", tq, SK[sk0:sk0 + 2], negm.bcast_flat(2), ALU.add)
                    act(tq, tq, AF.Exp)
                    st[n] = {"pb": pb, "tq": tq}

                def stB(n):
                    pb = st[n]["pb"]
                    pt = bank(3 + n % 2, 1)
                    for k in range(2):
                        for kt in range(2):
                            c = k * 2 + kt
                            tr(pt[c * 128:(c + 1) * 128], pb[k * 256 + kt * 128:k * 256 + (kt + 1) * 128])
                    ptb = palloc(512)
                    cp("act_copy", ptb, pt)
                    st[n]["ptb"] = ptb

                def stC(n):
                    i, g = items[n]
                    s = kvslot(gi, i)
                    ptb = st[n]["ptb"]
                    tq = st[n]["tq"]
                    po = BPS.v(5 * 512, ((1, 132),))
                    for k in range(2):
                        for kt in range(2):
                            c = k * 2 + kt
                            mm(po[k * 66:(k + 1) * 66], ptb[c * 128:(c + 1) * 128],
                               VV2[s - 1 + kt, k * 66:(k + 1) * 66], kt == 0, kt == 1)
                    tt("dve", tq, tq, View(BPS, 0, 128, 5 * 512 + 64, ((66, 2),)), ALU.add)
                    rden = stat(2)
                    rdo, rdi = rden.ap(), tq.ap()
                    P.add("dve", lambda e: e.reciprocal(rdo, rdi), [tq], [rden])
                    att = talloc(128)
                    tt("dve", att.reshape(2, 64), View(BPS, 0, 128, 5 * 512, ((66, 2), (1, 64))),
                       rden.bcast_last(64), ALU.mult)
                    st[n]["att"] = att

                def stD(n):
                    i, g = items[n]
                    pt2 = BPS.v(6 * 512, ((1, 128),))
                    tr(pt2, st[n]["att"])
                    cp("act_copy", ATT_T[g, i * 128:(i + 1) * 128], pt2)
                    del st[n]

                N = len(items)
                for n in range(N + 3):
                    if n < N:
                        stA(n)
                    if 0 <= n - 1 < N:
                        stB(n - 1)
                    if 0 <= n - 2 < N:
                        stC(n - 2)
                    if 0 <= n - 3 < N:
                        stD(n - 3)

            def attn_sample(L2, gi, i):
                s = kvslot(gi, i)
                s0 = i * 128
                for hb in range(2):
                    stg = temp()
                    dma(stg.reshape(8, 128), ck[hb * 8:(hb + 1) * 8].rearrange("b r c -> r b c"))
                    pt = psum(2)
                    for b in range(8):
                        tr(pt[b * 128:(b + 1) * 128], stg[b * 128:(b + 1) * 128])
                    cp("act_copy", BBIG.v(16 * T + 2048 + hb * 1024, ((1, 1024),)), pt)
                dma(BHT.v(0, ((1, 2048),)).reshape(16, 128), cv.rearrange("b r c -> r b c"))
                qbd_all = BBIG.v(16 * T, ((1, 2048),))
                real_cp("pool", qbd_all.reshape(16, 128), View(BCON, 0, 128, K_ZERO, ((0, 16), (1, 128))))
                for k in range(2):
                    src = View(BBIG, k * 64, 64, 0 * T + s0, ((T, 8), (8, 16), (1, 8)))
                    dst = View(BBIG, k * 64, 64, 16 * T + k * 64, ((8, 8), (128, 16), (1, 8)))
                    cp("pool", dst, src)
                negsk = SK[64 + 2 + L2:64 + 2 + L2 + 1]
                sk = COL[C_SINKS + L2:C_SINKS + L2 + 1]
                st = {}

                def sA(b):
                    ps = bank(b % 3, 1)
                    mm(ps[0:128], QBD[b], KTC[b], True, False)
                    mm(ps[0:128], IDENT, MASKSC, False, True)
                    mm(ps[128:256], QBD[b], KTZ[0, s], True, False)
                    mm(ps[128:256], QBD[b], KTZ[1, s], False, False)
                    mm(ps[128:256], IDENT, MWIDE[(15 - b) * 8:(15 - b) * 8 + 128], False, True)
                    rmax = stat(1)
                    reduce_max(rmax, ps[0:256])
                    negm = stat(1)
                    stt("dve", negm, rmax, -0.125, negsk, ALU.mult, ALU.min)
                    rowsum = stat(1)
                    pb = talloc(256)
                    act(pb, ps[0:256], AF.Exp, bias=negm, scale=0.125, accum=rowsum)
                    tq = stat(1)
                    tt("dve", tq, negm, sk, ALU.add)
                    act(tq, tq, AF.Exp)
                    st[b] = {"pb": pb, "tq": tq, "rowsum": rowsum}

                def sB(b):
                    pb = st[b]["pb"]
                    pt = bank(3 + b % 2, 1)
                    tr(pt[0:128], pb[0:128])
                    tr(pt[128:256], pb[128:256])
                    ptb = palloc(256)
                    cp("dve", ptb, pt[0:256])
                    st[b]["ptb"] = ptb

                def sC(b):
                    tq, rowsum, ptb = st[b]["tq"], st[b]["rowsum"], st[b]["ptb"]
                    tt("dve", tq, tq, rowsum, ALU.add)
                    rden = stat(1)
                    rdo, rdi = rden.ap(), tq.ap()
                    P.add("dve", lambda e: e.reciprocal(rdo, rdi), [tq], [rden])
                    po = bank(5, 1)
                    mm(po[0:128], ptb[0:128], VC[b], True, False)
                    mm(po[0:128].reshape(2, 64), ptb[128:256], VVd(s), False, True)
                    ob_ = talloc(128)
                    act(ob_, po[0:128], AF.Identity, scale=rden)
                    st[b]["ob"] = ob_

                def sD(b):
                    pt3 = bank(6, 1)
                    tr(pt3[0:128], st[b]["ob"])
                    for k in range(2):
                        src = View(BPS, k * 64, 64, 6 * 512 + k * 64, ((8, 8), (1, 8)))
                        dst = View(BBIG, k * 64, 64, 8 * T + s0 + b * 8, ((T, 8), (1, 8)))
                        cp("act_copy", dst, src)
                    del st[b]

                for n in range(16 + 3):
                    if n < 16:
                        sA(n)
                    if 0 <= n - 1 < 16:
                        sB(n - 1)
                    if 0 <= n - 2 < 16:
                        sC(n - 2)
                    if 0 <= n - 3 < 16:
                        sD(n - 3)

            def attn_layer(L2, gi, tl, gtile, next_col):
                P.tag = "attn%d" % L2
                lin1(w_q[L2], 4, tl, HT,
                     lambda ps, c, s0, n: act(ATT_Q[c, s0:s0 + n], ps, AF.Identity,
                                              bias=COL[C_BQ + 8 * L2 + c:C_BQ + 8 * L2 + c + 1]))
                attn_prompt_all(L2, gi, [i for i in tl if gtile[i] != 17], gtile)
                for i in tl:
                    if gtile[i] == 17:
                        P.tag = "attn%d/sample" % L2
                        attn_sample(L2, gi, i)
                P.tag = "attn%d" % L2
                bo = load_row(R_BO + L2)
                gpost = load_row(R_SWPOST + L2)
                sss = {i: stat(3) for i in tl}
                OBQ = BBIG.v(0, ((1, GT * D),)).reshape(GT, D)
                pipe = Pipe()
                set_defer(tl)

                def evac(ps, i, half):
                    ob = OBQ[i, half * 512:(half + 1) * 512]
                    tt("dve", ob, ps, bo[half * 512:(half + 1) * 512], ALU.add)
                    junk = talloc(512)
                    act(junk, ob, AF.Square, accum=sss[i][half:half + 1])
                    tt("dve", ob, ob, gpost[half * 512:(half + 1) * 512], ALU.mult)
                    if half == 1:
                        def ssv():
                            tt("dve", sss[i][2:3], sss[i][0:1], sss[i][1:2], ALU.add)
                            return sss[i][2:3]
                        s1_, s2_, s3_ = tail_stages(i, lambda: None, lambda: OBQ[i], ssv, gpost, next_col, None)
                        pipe.push(s1_, s2_, None, s3_)
                lin2(w_o[L2], 0, tl, ATT_T, evac)
                pipe.flush()

            ATT_Q = UT

            real_cp = cp

            def cp(eng, out, in_):
                if eng == "act_copy":
                    act(out, in_, AF.Copy)
                else:
                    real_cp(eng, out, in_)

            dma(BCON.v(), consts)
            real_cp("pool", BKT.v().reshape(14, 128), View(BCON, 0, 128, K_ZERO, ((0, 14), (1, 128))))
            real_cp("pool", VV2, View(BCON, 0, 128, K_VPAT, ((0, 7), (1, 132))))
            dma(MASKH, maskh)
            dma(COL, colv)
            dma(SK[0:32], rowv[R_SINK:R_SINK + 1, 0:32].partition_broadcast(128))
            ts("dve", SK[32:64], SK[0:32], -1.0, None, ALU.mult)
            ts("dve", SK[66:68], COL[C_SINKS:C_SINKS + 2], -1.0, None, ALU.mult)
            tt("dve", SK[76:92], View(BSK, 0, 128, 32, ((2, 16),)), View(BSK, 0, 128, 33, ((2, 16),)), ALU.min)

            import os as _os
            _stop = int(_os.environ.get("KSTOP", "999"))

            def _ph(k):
                if k >= _stop:
                    raise _Stop()
            try:
              for gi in range(3):
                gtile = [gi * GT + j for j in range(GT)]
                tl = list(range(GT))
                tl2 = [i for i in tl if gtile[i] != 0]
                for i in tl:
                    dma(X[i], xin[gtile[i]], q="pool")
                P.tag = "gmlp0"
                prenorm_T(tl, C_SGPRE)
                _ph(gi * 100 + 0)
                gmlp(0, tl, gtile, C_FPRE)
                _ph(gi * 100 + 1)
                ffn(0, tl, gtile, False, C_SGPRE + 8)
                _ph(gi * 100 + 2)
                gmlp(1, tl, gtile, C_FPRE + 8)
                _ph(gi * 100 + 3)
                ffn(1, tl, gtile, False, C_KVN)
                _ph(gi * 100 + 4)
                kvproj(gi, tl, gtile)
                P.tag = "attn0"
                prenorm_T(tl2, C_SWPRE, stats=False)
                _ph(gi * 100 + 5)
                attn_layer(0, gi, tl2, gtile, C_FPRE + 16)
                _ph(gi * 100 + 6)
                ffn(2, tl2, gtile, False, C_SWPRE + 8)
                _ph(gi * 100 + 7)
                attn_layer(1, gi, tl2, gtile, C_FPRE + 24)
                _ph(gi * 100 + 8)
                ffn(3, tl2, gtile, True, None)
                if gi < 2:
                    last = kvslot(gi, GT - 1)
                    real_cp("pool", KTZ[0, 0], KTZ[0, last])
                    real_cp("pool", KTZ[1, 0], KTZ[1, last])
                    real_cp("pool", VV2[0], VV2[last])
            except _Stop:
                pass
            P.add("sp", lambda e: e.nop(), list(P.outkeys), [])

        P0 = Prog(plan=None)
        run_body(P0)
        P = Prog(plan=P0.wdesc)
        run_body(P)
        assert P.wn == len(P.plan) and P.wissued == len(P.plan), (P.wn, P.wissued, len(P.plan))
        for m, pos in P.blk_dmapos.items():
            if m - NSLOT >= 0:
                assert P.blk_lastread.get(m - NSLOT, -1) < pos, ("slot reuse hazard", m)

        ins = P.ins
        last_w = {}
        readers = {}
        lane_last = {}
        lane_rr = 0
        lane_rr_sp = 0
        for it in ins:
            deps = set()
            for k in it.rk:
                w = last_w.get(k)
                if w is not None:
                    deps.add(w)
            for k in it.wk:
                w = last_w.get(k)
                if w is not None and (ins[w].eng != it.eng or it.dma or ins[w].dma):
                    deps.add(w)
                for r in readers.get(k, ()):
                    if ins[r].eng != it.eng or it.dma or ins[r].dma:
                        deps.add(r)
            if it.dma:
                if it.eng == "sp":
                    it.lane = lane_rr_sp % 4
                    lane_rr_sp += 1
                else:
                    it.lane = 4 + lane_rr % (NLANE - 4)
                    lane_rr += 1
                pl = lane_last.get(it.lane)
                if pl is not None:
                    deps.add(pl)
                lane_last[it.lane] = it.idx
            deps.discard(it.idx)
            if it.eng == "pe":
                deps = {d for d in deps if ins[d].eng != "pe"}
            it.deps = deps
            for k in it.wk:
                last_w[k] = it.idx
                readers[k] = []
            for k in it.rk:
                readers.setdefault(k, []).append(it.idx)
        for it in ins:
            it.ms = False
        for it in ins:
            for d in it.deps:
                ins[d].ms = True
        cnt = {e: 0 for e in ENGS}
        lane_cnt = [0] * NLANE
        for it in ins:
            if it.dma:
                lane_cnt[it.lane] += 1
                it.ev = (lanes[it.lane], 16 * lane_cnt[it.lane])
            elif it.ms:
                cnt[it.eng] += 1
                it.ev = (sems[it.eng], cnt[it.eng])
            else:
                it.ev = None

        per_eng = {e: [] for e in ENGS}
        for it in ins:
            per_eng[it.eng].append(it)

        def emit(ename, e):
            seen = {}
            for it in per_eng[ename]:
                need = {}
                for d in it.deps:
                    sem, val = ins[d].ev
                    key = id(sem)
                    if seen.get(key, 0) >= val:
                        continue
                    if key not in need or need[key][1] < val:
                        need[key] = (sem, val)
                for key, (sem, val) in need.items():
                    e.wait_ge(sem, val)
                    seen[key] = val
                bi = it.fn(e)
                if it.ev is not None:
                    bi.then_inc(it.ev[0], 16 if it.dma else 1)

        with nc.Block() as block:
            @block.sync
            def _(e):
                emit("sp", e)

            @block.tensor
            def _(e):
                emit("pe", e)

            @block.scalar
            def _(e):
                emit("act", e)

            @block.vector
            def _(e):
                emit("dve", e)

            @block.gpsimd
            def _(e):
                emit("pool", e)

    build_program.stats = {e: len(per_eng[e]) for e in ENGS}
    build_program.semmax = (dict(cnt), list(lane_cnt))
    build_program.tags = {e: [it.tag for it in per_eng[e]] for e in ENGS}
    return nc


def _prep_inputs(inp):
    f = lambda a: np.ascontiguousarray(np.asarray(a, dtype=np.float32))
    xp = f(inp["x_prompt"])
    xs = f(inp["x_sample"])
    ckf = f(inp["cache_k"]).reshape(128, 128, 128)
    cvf = f(inp["cache_v"]).reshape(128, 128, 128)
    L = 2
    w_q = f(inp["sw_w_q"]).reshape(L, D, 2, 8, 64).transpose(0, 1, 3, 2, 4).reshape(L, D, D)
    w_o = f(inp["sw_w_o"]).reshape(L, 2, 8, 64, D).transpose(0, 2, 1, 3, 4).reshape(L, D, D)
    b_q = f(inp["sw_b_q"]).reshape(L, 2, 8, 64).transpose(0, 2, 1, 3).reshape(L, D)
    sinks = f(inp["sw_sinks"])
    sinks_p = sinks.reshape(L, 2, 8).transpose(0, 2, 1).reshape(L, 16)
    w_s = f(inp["sg_w_s"])
    b_s = f(inp["sg_b_s"])
    wst = np.zeros((2, 2, 128, 8, 128), np.float32)
    wst[:, 0] = w_s.transpose(0, 3, 1, 2)
    ws8 = w_s[:, :, :8, :8].transpose(0, 3, 1, 2)
    for b in range(16):
        wst[:, 1, b * 8:(b + 1) * 8, :, b * 8:(b + 1) * 8] = ws8
    wst = wst.reshape(2, 2, 128, 1024)
    bsb = np.zeros((2, 2, 8, 128), np.float32)
    bsb[:, 0] = b_s
    bsb[:, 1] = np.tile(b_s[:, :, :8], (1, 1, 16))
    bsb = bsb.reshape(2, 2, 1024)

    def colize(v):
        return v.reshape(8, 128).T

    colv = np.zeros((128, NCOL), np.float32)
    for l in range(2):
        colv[:, C_SGPRE + 8 * l:C_SGPRE + 8 * l + 8] = colize(f(inp["sg_norm_pre"])[l])
        colv[:, C_SWPRE + 8 * l:C_SWPRE + 8 * l + 8] = colize(f(inp["sw_norm_pre"])[l])
        colv[:, C_BQ + 8 * l:C_BQ + 8 * l + 8] = colize(b_q[l])
        colv[:, C_SINKS + l] = np.repeat(sinks[l], 8)
    colv[:, C_KVN:C_KVN + 8] = colize(f(inp["kv_norm"]))
    colv[:, C_EPS_RMS] = RMS_EPS
    colv[:, C_EPS_LN] = LN_EPS
    for l in range(4):
        colv[:, C_FPRE + 8 * l:C_FPRE + 8 * l + 8] = colize(f(inp["f_norm_pre"])[l])
    rowv = np.zeros((NROW, D), np.float32)
    rowv[R_SGPOST:R_SGPOST + 2] = f(inp["sg_norm_post"])
    rowv[R_SWPOST:R_SWPOST + 2] = f(inp["sw_norm_post"])
    rowv[R_FPOST:R_FPOST + 4] = f(inp["f_norm_post"])
    rowv[R_LNG:R_LNG + 2] = f(inp["sg_ln_g"])
    rowv[R_LNB:R_LNB + 2] = f(inp["sg_ln_b"])
    rowv[R_BO:R_BO + 2] = f(inp["sw_b_o"])
    rowv[R_BKV, :256] = f(inp["b_kv"])
    rowv[R_SINK, :32] = sinks_p.reshape(32)

    consts = np.zeros((128, NCONST), np.float32)
    consts[:, K_ID:K_ID + 128] = np.eye(128, dtype=np.float32)
    p = np.arange(128)[:, None]
    j = np.arange(128)[None, :]
    consts[:, K_TRIL:K_TRIL + 128] = (p <= j).astype(np.float32)
    consts[:, K_MASKP:K_MASKP + 128] = np.where(j > p, 0.0, NEG)
    consts[:, K_MASKP + 128:K_MASKP + 256] = np.where(j <= p, 0.0, NEG)
    t_row = (np.arange(128) % 8)[:, None]
    consts[:, K_MASKSC:K_MASKSC + 128] = np.where(j > t_row, 0.0, NEG)
    mw = np.full((128, 248), NEG, np.float32)
    tp = np.arange(8)[None, :]
    mw[:, 120:128] = np.where(tp <= t_row, 0.0, NEG)
    consts[:, K_MWIDE:K_MWIDE + 248] = mw
    consts[:, K_VPAT + 64] = 1.0
    consts[:, K_VPAT + 130] = 1.0
    maskh_norm = consts[:, K_MASKP:K_MASKP + 256].copy()
    maskh_first = maskh_norm.copy()
    maskh_first[:, :128] = NEG

    shared = {
        "w_in": f(inp["sg_w_in"]), "w_out": f(inp["sg_w_out"]), "wst": wst, "bsb": bsb,
        "w_kv": f(inp["w_kv"]), "w_q": np.ascontiguousarray(w_q), "w_o": np.ascontiguousarray(w_o),
        "wg": f(inp["f_w_gate"]), "wu": f(inp["f_w_up"]), "wd": f(inp["f_w_down"]),
        "colv": colv, "rowv": rowv, "consts": consts,
    }
    in_maps = []
    for c in range(8):
        b, q = c // 4, c % 4
        s0 = q * 2048
        xin = np.empty((NTILE, 128, D), np.float32)
        if q == 0:
            xin[0] = xp[b, 0:128]
        else:
            xin[0] = xp[b, s0 - 128:s0]
        xin[1:17] = xp[b, s0:s0 + 2048].reshape(16, 128, D)
        xin[17] = xs[c * 16:(c + 1) * 16].reshape(128, D)
        m = dict(shared)
        m["xin"] = xin
        m["ck"] = np.ascontiguousarray(ckf[c * 16:(c + 1) * 16])
        m["cv"] = np.ascontiguousarray(cvf[c * 16:(c + 1) * 16])
        m["maskh"] = maskh_first if q == 0 else maskh_norm
        in_maps.append(m)
    return in_maps


_NC_CACHE = {}


def kernel(**inputs):
    in_maps = _prep_inputs(inputs)
    if "nc" not in _NC_CACHE:
        _NC_CACHE["nc"] = build_program()
    nc = _NC_CACHE["nc"]
    res = run_bass_kernel_spmd(nc, in_maps, core_ids=list(range(8)))
    rs = res.results
    y_prompt = np.empty((2, 8192, D), np.float32)
    y_sample = np.empty((128, 8, D), np.float32)
    sv_p = np.empty((2, 2, 128, D), np.float32)
    sv_s = np.empty((2, 128, 8, D), np.float32)
    k_p = np.empty((2, 128, 2, 64), np.float32)
    v_p = np.empty((2, 128, 2, 64), np.float32)
    k_s = np.empty((128, 8, 2, 64), np.float32)
    v_s = np.empty((128, 8, 2, 64), np.float32)
    for c in range(8):
        b, q = c // 4, c % 4
        y = np.asarray(rs[c]["y"])
        sv = np.asarray(rs[c]["sv"])
        kvo = np.asarray(rs[c]["kvo"])
        y_prompt[b, q * 2048:(q + 1) * 2048] = y[0:16].reshape(2048, D)
        y_sample[c * 16:(c + 1) * 16] = y[16].reshape(16, 8, D)
        sv_s[:, c * 16:(c + 1) * 16] = sv[:, 1].reshape(2, 16, 8, D)
        k_s[c * 16:(c + 1) * 16] = kvo[1][:, 0:128].reshape(16, 8, 2, 64)
        v_s[c * 16:(c + 1) * 16] = kvo[1][:, 128:256].reshape(16, 8, 2, 64)
        if q == 3:
            sv_p[:, b] = sv[:, 0]
            k_p[b] = kvo[0][:, 0:128].reshape(128, 2, 64)
            v_p[b] = kvo[0][:, 128:256].reshape(128, 2, 64)
    return (y_prompt, y_sample, sv_p, sv_s, k_p, v_p, k_s, v_s)
```

```python
import numpy as np
import concourse.bass as bass
import concourse.mybir as mybir
from concourse.ap import AP
from concourse.bass_utils import run_bass_kernel_spmd

F32 = mybir.dt.float32
F32R = mybir.dt.float32r
AF = mybir.ActivationFunctionType
ALU = mybir.AluOpType
AX = mybir.AxisListType

D = 1024
DFF = 2816
NF = 22
NTILE = 18
GT = 6
T = GT * 128
NSLOT = 5
SLOTF = 2048
RMS_EPS = 1e-6
LN_EPS = 1e-5
NEG = -30000.0

C_SGPRE = 0
C_KVN = 16
C_SWPRE = 24
C_FPRE = 40
C_BQ = 72
C_SINKS = 88
C_EPS_RMS = 90
C_EPS_LN = 91
NCOL = 92
R_SGPOST = 0
R_SWPOST = 2
R_FPOST = 4
R_LNG = 8
R_LNB = 10
R_BO = 12
R_BKV = 14
R_SINK = 15
NROW = 16
K_ID = 0
K_TRIL = 128
K_MASKP = 256
K_MASKSC = 512
K_MWIDE = 640
K_ZERO = 888
K_VPAT = 1016
NCONST = 1016 + 132


class Buf:
    def __init__(self, name, handle, row, gran, r=False):
        self.name, self.h, self.row, self.gran, self.r = name, handle, row, gran, r

    def v(self, off=0, dims=None, p0=0, npart=128):
        if dims is None:
            dims = ((1, self.row - off),)
        return View(self, p0, npart, off, tuple(dims))


class View:
    def __init__(self, buf, p0, npart, off, dims):
        self.buf, self.p0, self.npart, self.off, self.dims = buf, p0, npart, off, dims
        self.blk = None
        self.r = buf.r

    def _cp(self, v):
        v.blk = self.blk
        v.r = self.r
        return v

    def asr(self, flag=True):
        v = View(self.buf, self.p0, self.npart, self.off, self.dims)
        v.blk = self.blk
        v.r = flag
        return v

    def wap(self):
        return self.ap(r=self.r)

    def part(self, p0, n):
        return self._cp(View(self.buf, self.p0 + p0, n, self.off, self.dims))

    def __getitem__(self, idx):
        if not isinstance(idx, tuple):
            idx = (idx,)
        off = self.off
        dims = []
        for i, (s, n) in enumerate(self.dims):
            if i < len(idx):
                ix = idx[i]
                if isinstance(ix, int):
                    assert 0 <= ix < n
                    off += ix * s
                else:
                    a = 0 if ix.start is None else ix.start
                    b = n if ix.stop is None else ix.stop
                    assert 0 <= a < b <= n, (a, b, n)
                    off += a * s
                    dims.append((s, b - a))
            else:
                dims.append((s, n))
        return self._cp(View(self.buf, self.p0, self.npart, off, tuple(dims)))

    def reshape(self, *shape):
        assert len(self.dims) == 1 and self.dims[0][0] == 1
        tot = 1
        for s in shape:
            tot *= s
        assert tot == self.dims[0][1]
        dims = []
        st = tot
        for s in shape:
            st //= s
            dims.append((st, s))
        return self._cp(View(self.buf, self.p0, self.npart, self.off, tuple(dims)))

    def bcast_flat(self, n):
        return self._cp(View(self.buf, self.p0, self.npart, self.off, ((0, n),)))

    def bcast_last(self, n):
        return self._cp(View(self.buf, self.p0, self.npart, self.off, self.dims + ((0, n),)))

    def ap(self, r=False):
        a = AP(self.buf.h, self.p0 * self.buf.row + self.off,
               [[self.buf.row, self.npart]] + [[s, n] for s, n in self.dims])
        return a.bitcast(F32R) if r else a

    def keys(self):
        offs = np.array([self.off])
        for s, n in self.dims:
            if s == 0:
                continue
            offs = (offs[:, None] + (np.arange(n) * s)[None, :]).reshape(-1)
        g = self.buf.gran
        blks = np.unique(offs // g)
        return [(self.buf.name, int(b)) for b in blks]


class Ins:
    __slots__ = ("eng", "fn", "rk", "wk", "deps", "dma", "lane", "ms", "ev", "idx", "tag")


ENGS = ("pe", "act", "dve", "pool", "sp")


class Prog:
    def __init__(self, plan=None):
        self.ins = []
        self.plan = plan
        self.wdesc = []
        self.wn = 0
        self.wissued = 0
        self.blk_lastread = {}
        self.blk_dmapos = {}
        self.psum_ptr = 0
        self.tmp_ptr = 0
        self.ptb_ptr = 0
        self.gb_ptr = 0
        self.st_ptr = 0
        self.outkeys = []
        self.nout = 0
        self.tag = ""
        self.live = set()
        self.defer_map = {}
        self.deferred = []

    def add(self, eng, fn, reads, writes, dma=False):
        i = Ins()
        i.eng, i.fn, i.dma = eng, fn, dma
        rk = []
        for v in reads:
            if isinstance(v, View):
                rk.extend(v.keys())
                if v.blk is not None:
                    self.blk_lastread[v.blk] = len(self.ins)
            else:
                rk.append(v)
        wk = []
        for v in writes:
            if isinstance(v, View):
                wk.extend(v.keys())
            else:
                wk.append(v)
        i.rk, i.wk = rk, wk
        i.tag = self.tag
        i.idx = len(self.ins)
        self.ins.append(i)
        return i


def build_program():
    nc = bass.Bass("TRN2", target_bir_lowering=False)

    def din(name, shape, dt=F32):
        return nc.dram_tensor(name, list(shape), dt, kind="ExternalInput").ap()

    def dout(name, shape):
        return nc.dram_tensor(name, list(shape), F32, kind="ExternalOutput").ap()

    xin = din("xin", [NTILE, 128, D])
    ck = din("ck", [16, 128, 128])
    cv = din("cv", [16, 128, 128])
    w_in = din("w_in", [2, D, 2 * D])
    w_out = din("w_out", [2, D, D])
    wst = din("wst", [2, 2, 128, 1024])
    bsb = din("bsb", [2, 2, 1024])
    w_kv = din("w_kv", [D, 256])
    w_q = din("w_q", [2, D, D])
    w_o = din("w_o", [2, D, D])
    wg = din("wg", [4, D, DFF])
    wu = din("wu", [4, D, DFF])
    wd = din("wd", [4, DFF, D])
    colv = din("colv", [128, NCOL])
    rowv = din("rowv", [NROW, D])
    consts = din("consts", [128, NCONST])
    maskh = din("maskh", [128, 256])
    y_o = dout("y", [17, 128, D])
    sv_o = dout("sv", [2, 2, 128, D])
    kv_o = dout("kvo", [2, 128, 256])

    from contextlib import ExitStack
    es = ExitStack()

    def sb(name, n, gran=128, r=False):
        h = es.enter_context(nc.sbuf_tensor(name, [128, n], F32))
        return Buf(name, h, n, gran, r)

    with es:
        BX = sb("X", GT * D)
        BHT = sb("HT", 8 * T, r=True)
        BBIG = sb("BIG", NF * T, r=True)
        BWS = sb("WS", NSLOT * SLOTF, gran=SLOTF, r=True)
        BKT = sb("KT", 2 * 7 * 128, r=True)
        BVV = sb("VV", 7 * 132, r=True)
        BCON = sb("CON", NCONST, gran=8, r=True)
        BMH = sb("MH", 256, gran=256, r=True)
        BGB = sb("GB", 2 * D, gran=D)
        BTMP = sb("TMP", 3 * D, gran=128)
        BPTB = sb("PTB", 2 * D, gran=128, r=True)
        BSTASH = sb("STASH", D, gran=128)
        BCOL = sb("COL", NCOL, gran=1)
        BSK = sb("SK", 92, gran=1)
        BST = sb("ST", 512, gran=1)
        hps = es.enter_context(nc.psum_tensor("PS", [128, 4096], F32))
        BPS = Buf("PS", hps, 4096, 512)

        sems = {}
        for e in ("pe", "act", "dve", "pool"):
            sems[e] = es.enter_context(nc.semaphore("s_" + e))
        NLANE = 12
        lanes = [es.enter_context(nc.semaphore("l%d" % i)) for i in range(NLANE)]

        X = BX.v().reshape(GT, D)
        HT = BHT.v().reshape(8, T)
        OB2 = BHT.v().reshape(GT, D)
        BIG = BBIG.v().reshape(NF, T)
        UT = BBIG.v(0, ((1, 8 * T),)).reshape(8, T)
        VB = BBIG.v(8 * T, ((1, 8 * T),)).reshape(GT, D)
        ATT_T = BBIG.v(8 * T, ((1, 8 * T),)).reshape(8, T)
        SAMP = BBIG.v(16 * T, ((1, 4096),))
        QBD = SAMP[0:2048].reshape(16, 128)
        KTC = SAMP[2048:4096].reshape(16, 128)
        VC = BHT.v(0, ((1, 2048),)).reshape(16, 128)
        WSTV = BHT.v(0, ((1, 2048),)).reshape(2, 8, 128)
        BSBV = BHT.v(2048, ((1, 2048),)).reshape(2, 8, 128)
        KTZ = BKT.v().reshape(2, 7, 128)
        VV2 = BVV.v().reshape(7, 132)

        def VVd(slot):
            return View(BVV, 0, 128, slot * 132, ((66, 2), (1, 64)))
        IDENT = BCON.v(K_ID, ((1, 128),))
        TRIL = BCON.v(K_TRIL, ((1, 128),))
        MASKP = BCON.v(K_MASKP, ((1, 256),))
        MASKSC = BCON.v(K_MASKSC, ((1, 128),))
        MWIDE = BCON.v(K_MWIDE, ((1, 248),))
        MASKH = BMH.v()
        COL = BCOL.v()
        SK = BSK.v()

        def run_body(P):
            def psum(nb):
                p = P.psum_ptr
                for _ in range(16):
                    if p % nb:
                        p += nb - p % nb
                    if p + nb > 8:
                        p = 0
                    if not any((p + j) in P.live for j in range(nb)):
                        break
                    p += nb
                else:
                    raise RuntimeError("no free PSUM bank")
                P.psum_ptr = p + nb
                return BPS.v(p * 512, ((1, nb * 512),))

            def bank_of(v):
                return v.off // 512

            def bank(b, nb=1):
                return BPS.v(b * 512, ((1, nb * 512),))

            def temp():
                return talloc(D)

            def ptbuf():
                return palloc(D)

            def gbt():
                t = P.gb_ptr
                P.gb_ptr = (t + 1) % 2
                return BGB.v(t * D, ((1, D),))

            def stat(n):
                if P.st_ptr + n > 512:
                    P.st_ptr = 0
                v = BST.v(P.st_ptr, ((1, n),))
                P.st_ptr += n
                return v

            def dma(out, in_, reads=(), writes=(), q=None):
                o = out.wap() if isinstance(out, View) else out
                i = in_.ap() if isinstance(in_, View) else in_
                rd = list(reads) + ([in_] if isinstance(in_, View) else [])
                wr = list(writes) + ([out] if isinstance(out, View) else [])
                if q is None:
                    q = "pool" if (isinstance(out, View) and out.r and not isinstance(in_, View)) else "sp"
                return P.add(q, lambda e: e.dma_start(out=o, in_=i), rd, wr, dma=True)

            def dma_out(dram_ap, src):
                key = ("OUT", P.nout)
                P.nout += 1
                P.outkeys.append(key)
                dma(dram_ap, src, writes=[key])

            def mm(out, lhsT, rhs, start, stop):
                o, l, r = out.ap(), lhsT.ap(r=True), rhs.ap(r=True)
                P.add("pe", lambda e: e.matmul(o, l, r, start=start, stop=stop), [lhsT, rhs], [out])

            def tr(out, in_):
                o, i, idn = out.ap(), in_.ap(), IDENT.ap()

                def f(e):
                    try:
                        return e.transpose(o, i, idn)
                    except Exception:
                        print("TRANSPOSE FAIL", P.tag, o, i)
                        raise
                P.add("pe", f, [in_, IDENT], [out])

            def act(out, in_, func, bias=None, scale=None, accum=None, eng="act", xw=()):
                o, i = out.wap(), in_.ap()
                kw = {}
                rd = [in_]
                wr = [out] + list(xw)
                if bias is not None:
                    if isinstance(bias, View):
                        kw["bias"] = bias.ap()
                        rd.append(bias)
                    else:
                        kw["bias"] = bias
                if scale is not None:
                    if isinstance(scale, View):
                        kw["scale"] = scale.ap()
                        rd.append(scale)
                    else:
                        kw["scale"] = scale
                if accum is not None:
                    kw["accum_out"] = accum.ap()
                    wr.append(accum)
                P.add("act", lambda e: e.activation(o, i, func, **kw), rd, wr)

            def tt(eng, out, a, b, op, xw=()):
                o, x, y_ = out.wap(), a.ap(), b.ap()
                P.add(eng, lambda e: e.tensor_tensor(o, x, y_, op), [a, b], [out] + list(xw))

            def ts(eng, out, a, s1, s2, op0, op1=None):
                o, x = out.wap(), a.ap()
                rd = [a]
                if isinstance(s1, View):
                    rd.append(s1)
                    s1 = s1.ap()
                if isinstance(s2, View):
                    rd.append(s2)
                    s2 = s2.ap()
                if op1 is None:
                    P.add(eng, lambda e: e.tensor_scalar(o, x, s1, None, op0), rd, [out])
                else:
                    P.add(eng, lambda e: e.tensor_scalar(o, x, s1, s2, op0, op1), rd, [out])

            def stt(eng, out, a, sc, b, op0, op1):
                o, x, y_ = out.wap(), a.ap(), b.ap()
                rd = [a, b]
                if isinstance(sc, View):
                    rd.append(sc)
                    sc = sc.ap()
                P.add(eng, lambda e: e.scalar_tensor_tensor(o, x, sc, y_, op0, op1), rd, [out])

            def cp(eng, out, in_):
                o, i = out.wap(), in_.ap()
                P.add(eng, lambda e: e.tensor_copy(o, i), [in_], [out])

            def reduce_max(out, in_):
                o, i = out.ap(), in_.ap()
                P.add("dve", lambda e: e.tensor_reduce(o, i, AX.X, ALU.max), [in_], [out])

            def wget(dram_ap, shape, look=None):
                n = P.wn
                P.wn += 1
                nfl = 1
                for s in shape:
                    nfl *= s
                assert nfl <= SLOTF
                if P.plan is None:
                    P.wdesc.append((dram_ap, tuple(shape)))
                else:
                    if look is None:
                        look = NSLOT - 2
                    while P.wissued < len(P.plan) and P.wissued <= n + look:
                        m = P.wissued
                        dap, shp = P.plan[m]
                        tot = 1
                        for s in shp:
                            tot *= s
                        sv = BWS.v((m % NSLOT) * SLOTF, ((1, tot),)).reshape(*shp)
                        P.blk_dmapos[m] = len(P.ins)
                        dma(sv, dap)
                        P.wissued += 1
                v = BWS.v((n % NSLOT) * SLOTF, ((1, nfl),)).reshape(*shape)
                v.blk = n
                return v

            def w_o1(w2d, c0):
                return wget(w2d.rearrange("(k p) c -> p k c", p=128)[:, :, c0:c0 + 256], (8, 256))

            def w_o2(w2d, kh, c0):
                return wget(w2d[kh * 512:(kh + 1) * 512, :].rearrange("(k p) c -> p k c", p=128)[:, :, c0:c0 + 512],
                            (4, 512))

            def load_row(r, n=D):
                g = gbt()
                dma(g[0:n], rowv[r:r + 1, 0:n].partition_broadcast(128))
                return g

            import os as _os2
            _sub = int(_os2.environ.get("KSUB", "999"))

            class _Stop(Exception):
                pass

            def _ph2(k):
                if k >= _sub:
                    raise _Stop()

            def talloc(n):
                p = P.tmp_ptr
                if p % 128:
                    p += 128 - p % 128
                if p + n > 3 * D:
                    p = 0
                P.tmp_ptr = p + n
                return BTMP.v(p, ((1, n),))

            def palloc(n):
                p = P.ptb_ptr
                if p + n > 2 * D:
                    p = 0
                P.ptb_ptr = p + n
                return BPTB.v(p, ((1, n),))

            def rstd_from_ss(ssv, eps):
                act(ssv, ssv, AF.Sqrt, bias=COL[eps:eps + 1], scale=1.0 / D)
                so, si = ssv.ap(), ssv.ap()
                P.add("dve", lambda e: e.reciprocal(so, si), [ssv], [ssv])

            def subgroups(tl):
                n = len(tl)
                a = (n + 1) // 2
                res = [(tl[0] * 128, a * 128)]
                if n - a > 0:
                    res.append(((tl[0] + a) * 128, (n - a) * 128))
                return res

            RSTD = SK[68:76]

            def preload_sqrt():
                d_ = stat(1)
                act(d_, COL[C_EPS_LN:C_EPS_LN + 1], AF.Sqrt)

            def pre_stats(i):
                ss = RSTD[i:i + 1]
                junk = temp()
                act(junk, X[i], AF.Square, accum=ss)
                rstd_from_ss(ss, C_EPS_RMS)

            XNBUF = [BPTB.v(0, ((1, D),)), BPTB.v(D, ((1, D),)), BSTASH.v(0, ((1, D),))]

            def pre_emit(i, colbase):
                slot = P.defer_map.get(i)
                xn = temp() if slot is None else XNBUF[slot]
                act(xn, X[i], AF.Identity, scale=RSTD[i:i + 1])

                def part2():
                    ps = psum(2)
                    for c in range(8):
                        tr(ps[c * 128:(c + 1) * 128], xn[c * 128:(c + 1) * 128])
                    tt("dve", HT[:, i * 128:(i + 1) * 128], ps.reshape(8, 128),
                       COL[colbase:colbase + 8].bcast_last(128), ALU.mult)
                if slot is None:
                    part2()
                else:
                    P.deferred.append(part2)

            def set_defer(tl):
                n = len(tl)
                a_ = (n + 1) // 2
                P.defer_map = {tl[a_ + j]: j for j in range(n - a_)}

            def run_deferred():
                for f_ in P.deferred:
                    f_()
                P.deferred = []
                P.defer_map = {}

            def prenorm_T(tl, colbase, stats=True):
                P.tag = P.tag.split("/")[0] + "/prenorm"
                if stats:
                    for i in tl:
                        pre_stats(i)
                for i in tl:
                    pre_emit(i, colbase)

            class Pipe:
                def __init__(self, rev=True):
                    self.q = []
                    self.rev = rev

                def push(self, *stages):
                    self.q.append(list(stages))
                    self.step()

                def step(self):
                    n = len(self.q)
                    lags = list(range(len(self.q[-1]) if self.q else 0))
                    if self.rev:
                        lags.reverse()
                    for lag in lags:
                        j = n - 1 - lag
                        if j >= 0 and self.q[j][lag] is not None:
                            f = self.q[j][lag]
                            self.q[j][lag] = None
                            f()

                def flush(self):
                    ns = max((len(x) for x in self.q), default=0)
                    for _ in range(ns):
                        self.q.append([None] * ns)
                        self.step()

            def tail_stages(i, evac_fn, ob_fn, ssv_fn, gpost, next_col, final_out):
                def s1():
                    evac_fn()
                    r = ssv_fn()
                    rstd_from_ss(r, C_EPS_RMS)
                    obs = ob_fn()
                    if isinstance(obs, View):
                        obs = [(X[i], obs)]
                    for xs, ov in obs:
                        stt("dve", xs, ov, r, xs, ALU.mult, ALU.add)
                    if final_out is not None:
                        dma_out(final_out, X[i])

                def s2():
                    if next_col is not None:
                        pre_stats(i)

                def s3():
                    if next_col is not None:
                        pre_emit(i, next_col)
                return s1, s2, s3

            def lin1(wsrc, nblk, tl, src, evac):
                P.tag = P.tag.split("/")[0] + "/lin1"
                sgs = subgroups(tl)
                for blk in range(nblk):
                    slot = w_o1(wsrc, blk * 256)
                    for j in range(2):
                        for (s0, n) in sgs:
                            ps = psum(1)
                            for k in range(8):
                                mm(ps[0:n], slot[k, j * 128:(j + 1) * 128], src[k, s0:s0 + n], k == 0, k == 7)
                            evac(ps[0:n], blk * 2 + j, s0, n)

            def lin2(wsrc, c0, tl, src, evac):
                P.tag = P.tag.split("/")[0] + "/lin2"
                for half in range(2):
                    sA = w_o2(wsrc, 0, c0 + half * 512)
                    sB = w_o2(wsrc, 1, c0 + half * 512)
                    for i in tl:
                        ps = psum(1)
                        for k in range(8):
                            mm(ps, src[k, i * 128:(i + 1) * 128], (sA if k < 4 else sB)[k % 4], k == 0, k == 7)
                        evac(ps, i, half)

            SAMPW = BBIG.v(16 * T, ((1, 4096),))
            WSTV2 = SAMPW[0:2048].reshape(2, 8, 128)
            BSBV2 = SAMPW[2048:4096].reshape(2, 1024)

            def gmlp(L, tl, gtile, next_col):
                P.tag = "gmlp%d/ln" % L
                lng = load_row(R_LNG + L)
                lnb = load_row(R_LNB + L)
                kinds = sorted(set(1 if gtile[i] == 17 else 0 for i in tl))
                for kd in kinds:
                    dma(SAMPW[kd * 1024:(kd + 1) * 1024], wst[L, kd])
                    dma(BSBV2[kd], bsb[L, kd:kd + 1, :].partition_broadcast(128))
                    for g in range(8):
                        tt("pool", WSTV2[kd, g], WSTV2[kd, g], TRIL, ALU.mult)
                _ph2(0)
                pipe = Pipe()

                def ln_stages(i):
                    vbi = VB[i]
                    st = {}

                    def s1():
                        stats = stat(12)
                        a0, a1 = stats[0:6].ap(), stats[6:12].ap()
                        v0, v1 = vbi[0:512], vbi[512:1024]
                        x0, x1 = v0.ap(), v1.ap()
                        P.add("dve", lambda e: e.bn_stats(a0, x0), [v0], [stats[0:6]])
                        P.add("dve", lambda e: e.bn_stats(a1, x1), [v1], [stats[6:12]])
                        mv = stat(2)
                        mva, a2r = mv.ap(), stats.reshape(2, 6).ap()
                        P.add("dve", lambda e: e.bn_aggr(mva, a2r), [stats], [mv])
                        rs = stat(1)
                        act(rs, mv[1:2], AF.Sqrt, bias=COL[C_EPS_LN:C_EPS_LN + 1], scale=1.0)
                        st["mv"], st["rs"] = mv, rs

                    def s2():
                        rs, mv = st["rs"], st["mv"]
                        rso = rs.ap()
                        P.add("dve", lambda e: e.reciprocal(rso, rso), [rs], [rs])
                        ts("dve", vbi, vbi, mv[0:1], rs, ALU.subtract, ALU.mult)

                    def s3():
                        tt("dve", vbi, vbi, lng, ALU.mult)
                        tt("dve", vbi, vbi, lnb, ALU.add)
                        if gtile[i] == 16:
                            dma_out(sv_o[L, 0], vbi)
                        elif gtile[i] == 17:
                            dma_out(sv_o[L, 1], vbi)
                    return s1, s2, s3

                def evac_v(ps, i, half):
                    act(VB[i, half * 512:(half + 1) * 512], ps, AF.Gelu)
                    if half == 1:
                        pipe.push(*ln_stages(i))
                P.tag = "gmlp%d" % L
                lin2(w_in[L], D, tl, HT, evac_v)
                pipe.flush()
                _ph2(1)
                lin1(w_in[L], 4, tl, HT, lambda ps, c, s0, n: act(UT[c, s0:s0 + n], ps, AF.Gelu))
                P.tag = "gmlp%d/mix" % L
                preload_sqrt()
                _ph2(2)
                for i in tl:
                    kd = 1 if gtile[i] == 17 else 0
                    ps = psum(2)
                    for g in range(8):
                        mm(ps[g * 128:(g + 1) * 128], VB[i, g * 128:(g + 1) * 128], WSTV2[kd, g], True, True)
                    tmp = temp()
                    tt("dve", tmp, ps, BSBV2[kd], ALU.add)
                    yv = UT[:, i * 128:(i + 1) * 128]
                    tt("pool", yv, tmp.reshape(8, 128), yv, ALU.mult)
                P.tag = "gmlp%d" % L
                _ph2(3)
                gpost = load_row(R_SGPOST + L)
                sss = {i: stat(3) for i in tl}
                pipe2 = Pipe()
                set_defer(tl)

                def evac(ps, i, half):
                    ob = VB[i, half * 512:(half + 1) * 512]
                    junk = talloc(512)
                    xk = [("PSRD",) + tuple(ps.keys())]
                    act(junk, ps, AF.Square, accum=sss[i][half:half + 1], xw=xk)
                    tt("dve", ob, ps, gpost[half * 512:(half + 1) * 512], ALU.mult, xw=xk)
                    if half == 1:
                        def ssv():
                            tt("dve", sss[i][2:3], sss[i][0:1], sss[i][1:2], ALU.add)
                            return sss[i][2:3]
                        pipe2.push(*tail_stages(i, lambda: None, lambda: VB[i], ssv, gpost, next_col, None))
                lin2(w_out[L], 0, tl, UT, evac)
                pipe2.flush()

            def ffn(L, tl, gtile, final, next_col):
                P.tag = "ffn%d/gateup" % L
                sgs = subgroups(tl)

                def gu(sg_, su_, blk, j, s0, n):
                    f = blk * 2 + j
                    pa = psum(1)
                    pb = psum(1)
                    for k in range(8):
                        mm(pa[0:n], sg_[k, j * 128:(j + 1) * 128], HT[k, s0:s0 + n], k == 0, k == 7)
                    for k in range(8):
                        mm(pb[0:n], su_[k, j * 128:(j + 1) * 128], HT[k, s0:s0 + n], k == 0, k == 7)
                    tmp = talloc(n)
                    act(tmp, pa[0:n], AF.Silu)
                    tt("dve", BIG[f, s0:s0 + n], tmp, pb[0:n], ALU.mult)

                first = 0
                if P.deferred and len(sgs) == 2:
                    wsl = []
                    for blk in range(2):
                        g_ = wget(wg[L].rearrange("(k p) c -> p k c", p=128)[:, :, blk * 256:blk * 256 + 256], (8, 256), look=1)
                        u_ = wget(wu[L].rearrange("(k p) c -> p k c", p=128)[:, :, blk * 256:blk * 256 + 256], (8, 256), look=1)
                        wsl.append((g_, u_))
                    for blk in range(2):
                        for j in range(2):
                            gu(wsl[blk][0], wsl[blk][1], blk, j, *sgs[0])
                    run_deferred()
                    for blk in range(2):
                        for j in range(2):
                            gu(wsl[blk][0], wsl[blk][1], blk, j, *sgs[1])
                    first = 2
                else:
                    run_deferred()
                for blk in range(first, 11):
                    sg_ = w_o1(wg[L], blk * 256)
                    su_ = w_o1(wu[L], blk * 256)
                    for j in range(2):
                        for (s0, n) in sgs:
                            gu(sg_, su_, blk, j, s0, n)
                P.tag = "ffn%d/down" % L
                preload_sqrt()
                gpost = load_row(R_FPOST + L)
                wdl = wd[L].rearrange("(f p) c -> p f c", p=128)
                pipe = Pipe(rev=False)
                sss = {i: stat(3) for i in tl}
                stash = {}
                for idx, i in enumerate(tl):
                    stash[i] = BPTB.v(idx * 512, ((1, 512),)) if idx < 4 else BSTASH.v((idx - 4) * 512, ((1, 512),))
                for half in range(2):
                    accs = {}
                    for i in tl:
                        accs[i] = psum(1)
                        P.live.add(bank_of(accs[i]))

                    def getslot(fb, look=None):
                        f0 = fb * 4
                        nf = min(4, NF - f0)
                        return wget(wdl[:, f0:f0 + nf, half * 512:(half + 1) * 512], (nf, 512), look=look), f0, nf

                    def evac1(i):
                        acc = accs[i]
                        if half == 0:
                            xk = [("PSRD",) + tuple(acc.keys())]
                            junk = talloc(512)
                            act(junk, acc, AF.Square, accum=sss[i][0:1], xw=xk)
                            tt("dve", stash[i], acc, gpost[0:512], ALU.mult, xw=xk)
                            P.live.discard(bank_of(acc))
                        else:
                            def mk(i=i, acc=acc):
                                ob1 = talloc(512)

                                def ev():
                                    xk = [("PSRD",) + tuple(acc.keys())]
                                    junk = talloc(512)
                                    act(junk, acc, AF.Square, accum=sss[i][1:2], xw=xk)
                                    tt("dve", ob1, acc, gpost[512:1024], ALU.mult, xw=xk)
                                    P.live.discard(bank_of(acc))

                                def ssv():
                                    tt("dve", sss[i][2:3], sss[i][0:1], sss[i][1:2], ALU.add)
                                    return sss[i][2:3]
                                s1, s2, s3 = tail_stages(i, ev, lambda: [(X[i, 0:512], stash[i]), (X[i, 512:1024], ob1)],
                                                         ssv, gpost, next_col, y_o[gtile[i] - 1] if final else None)
                                return s1, s2, None, None, s3
                            pipe.push(*mk())

                    nfb_major = 3
                    for fb in range(nfb_major):
                        slot, f0, nf = getslot(fb)
                        for ff in range(nf):
                            f = f0 + ff
                            for i in tl:
                                mm(accs[i], BIG[f, i * 128:(i + 1) * 128], slot[ff], f == 0, f == NF - 1)
                    if True:
                        sl3 = getslot(3, look=2)
                        sl4 = getslot(4, look=2)
                        sl5 = getslot(5, look=2)
                        for i in tl:
                            for (slot, f0, nf) in (sl3, sl4, sl5):
                                for ff in range(nf):
                                    f = f0 + ff
                                    mm(accs[i], BIG[f, i * 128:(i + 1) * 128], slot[ff], f == 0, f == NF - 1)
                            evac1(i)
                    else:
                        for i in tl:
                            evac1(i)
                pipe.flush()

            def kvslot(gi, i):
                return i if gi == 0 else i + 1

            def kvproj(gi, tl, gtile):
                P.tag = "kv"
                slot = wget(w_kv.rearrange("(k p) c -> p k c", p=128), (8, 256))
                bkv = load_row(R_BKV, 256)
                for i in tl:
                    ps = psum(1)
                    for k in range(8):
                        mm(ps[0:256], HT[k, i * 128:(i + 1) * 128], slot[k], k == 0, k == 7)
                    kvt = temp()
                    tt("dve", kvt[0:256], ps[0:256], bkv[0:256], ALU.add)
                    s = kvslot(gi, i)
                    cp("pool", VVd(s), kvt[128:256].reshape(2, 64))
                    if gtile[i] == 16:
                        dma_out(kv_o[0], kvt[0:256])
                    elif gtile[i] == 17:
                        dma_out(kv_o[1], kvt[0:256])
                    pt = psum(1)
                    tr(pt[0:128], kvt[0:128])
                    act(KTZ[0, s].part(0, 64), pt[0:128].part(0, 64), AF.Copy)
                    act(KTZ[1, s].part(64, 64), pt[0:128].part(64, 64), AF.Copy)

            def attn_prompt_all(L2, gi, tiles, gtile):
                P.tag = "attn%d/core" % L2
                items = [(i, g) for i in tiles for g in range(8)]
                st = {}

                def stA(n):
                    i, g = items[n]
                    s = kvslot(gi, i)
                    mask = MASKH if gtile[i] == 1 else MASKP
                    ps = bank(n % 3, 1)
                    for k in range(2):
                        o = ps[k * 256:(k + 1) * 256]
                        keys = BKT.v(k * 896 + (s - 1) * 128, ((1, 256),))
                        mm(o, ATT_Q[g, i * 128:(i + 1) * 128], keys, True, False)
                        mm(o, IDENT, mask, False, True)
                    rmax = stat(1)
                    reduce_max(rmax, ps[0:512])
                    negm = stat(1)
                    sk0 = L2 * 16 + 2 * g
                    pidx = 76 + L2 * 8 + g
                    stt("dve", negm, rmax, -0.125, SK[pidx:pidx + 1], ALU.mult, ALU.min)
                    pb = talloc(512)
                    act(pb, ps[0:512], AF.Exp, bias=negm, scale=0.125)
                    tq = stat(2)
                    tt("dve", tq, SK[sk0:sk0 + 2], negm.bcast_flat(2), ALU.add)
                    act(tq, tq, AF.Exp)
                    st[n] = {"pb": pb, "tq": tq}

                def stB(n):
                    pb = st[n]["pb"]
                    pt = bank(3 + n % 2, 1)
                    for k in range(2):
                        for kt in range(2):
                            c = k * 2 + kt
                            tr(pt[c * 128:(c + 1) * 128], pb[k * 256 + kt * 128:k * 256 + (kt + 1) * 128])
                    ptb = palloc(512)
                    cp("act_copy", ptb, pt)
                    st[n]["ptb"] = ptb

                def stC(n):
                    i, g = items[n]
                    s = kvslot(gi, i)
                    ptb = st[n]["ptb"]
                    tq = st[n]["tq"]
                    po = BPS.v(5 * 512, ((1, 132),))
                    for k in range(2):
                        for kt in range(2):
                            c = k * 2 + kt
                            mm(po[k * 66:(k + 1) * 66], ptb[c * 128:(c + 1) * 128],
                               VV2[s - 1 + kt, k * 66:(k + 1) * 66], kt == 0, kt == 1)
                    tt("dve", tq, tq, View(BPS, 0, 128, 5 * 512 + 64, ((66, 2),)), ALU.add)
                    rden = stat(2)
                    rdo, rdi = rden.ap(), tq.ap()
                    P.add("dve", lambda e: e.reciprocal(rdo, rdi), [tq], [rden])
                    att = talloc(128)
                    tt("dve", att.reshape(2, 64), View(BPS, 0, 128, 5 * 512, ((66, 2), (1, 64))),
                       rden.bcast_last(64), ALU.mult)
                    st[n]["att"] = att

                def stD(n):
                    i, g = items[n]
                    pt2 = BPS.v(6 * 512, ((1, 128),))
                    tr(pt2, st[n]["att"])
                    cp("act_copy", ATT_T[g, i * 128:(i + 1) * 128], pt2)
                    del st[n]

                N = len(items)
                for n in range(N + 3):
                    if n < N:
                        stA(n)
                    if 0 <= n - 1 < N:
                        stB(n - 1)
                    if 0 <= n - 2 < N:
                        stC(n - 2)
                    if 0 <= n - 3 < N:
                        stD(n - 3)

            def attn_sample(L2, gi, i):
                s = kvslot(gi, i)
                s0 = i * 128
                for hb in range(2):
                    stg = temp()
                    dma(stg.reshape(8, 128), ck[hb * 8:(hb + 1) * 8].rearrange("b r c -> r b c"))
                    pt = psum(2)
                    for b in range(8):
                        tr(pt[b * 128:(b + 1) * 128], stg[b * 128:(b + 1) * 128])
                    cp("act_copy", BBIG.v(16 * T + 2048 + hb * 1024, ((1, 1024),)), pt)
                dma(BHT.v(0, ((1, 2048),)).reshape(16, 128), cv.rearrange("b r c -> r b c"))
                qbd_all = BBIG.v(16 * T, ((1, 2048),))
                real_cp("pool", qbd_all.reshape(16, 128), View(BCON, 0, 128, K_ZERO, ((0, 16), (1, 128))))
                for k in range(2):
                    src = View(BBIG, k * 64, 64, 0 * T + s0, ((T, 8), (8, 16), (1, 8)))
                    dst = View(BBIG, k * 64, 64, 16 * T + k * 64, ((8, 8), (128, 16), (1, 8)))
                    cp("pool", dst, src)
                negsk = SK[64 + 2 + L2:64 + 2 + L2 + 1]
                sk = COL[C_SINKS + L2:C_SINKS + L2 + 1]
                st = {}

                def sA(b):
                    ps = bank(b % 3, 1)
                    mm(ps[0:128], QBD[b], KTC[b], True, False)
                    mm(ps[0:128], IDENT, MASKSC, False, True)
                    mm(ps[128:256], QBD[b], KTZ[0, s], True, False)
                    mm(ps[128:256], QBD[b], KTZ[1, s], False, False)
                    mm(ps[128:256], IDENT, MWIDE[(15 - b) * 8:(15 - b) * 8 + 128], False, True)
                    rmax = stat(1)
                    reduce_max(rmax, ps[0:256])
                    negm = stat(1)
                    stt("dve", negm, rmax, -0.125, negsk, ALU.mult, ALU.min)
                    rowsum = stat(1)
                    pb = talloc(256)
                    act(pb, ps[0:256], AF.Exp, bias=negm, scale=0.125, accum=rowsum)
                    tq = stat(1)
                    tt("dve", tq, negm, sk, ALU.add)
                    act(tq, tq, AF.Exp)
                    st[b] = {"pb": pb, "tq": tq, "rowsum": rowsum}

                def sB(b):
                    pb = st[b]["pb"]
                    pt = bank(3 + b % 2, 1)
                    tr(pt[0:128], pb[0:128])
                    tr(pt[128:256], pb[128:256])
                    ptb = palloc(256)
                    cp("dve", ptb, pt[0:256])
                    st[b]["ptb"] = ptb

                def sC(b):
                    tq, rowsum, ptb = st[b]["tq"], st[b]["rowsum"], st[b]["ptb"]
                    tt("dve", tq, tq, rowsum, ALU.add)
                    rden = stat(1)
                    rdo, rdi = rden.ap(), tq.ap()
                    P.add("dve", lambda e: e.reciprocal(rdo, rdi), [tq], [rden])
                    po = bank(5, 1)
                    mm(po[0:128], ptb[0:128], VC[b], True, False)
                    mm(po[0:128].reshape(2, 64), ptb[128:256], VVd(s), False, True)
                    ob_ = talloc(128)
                    act(ob_, po[0:128], AF.Identity, scale=rden)
                    st[b]["ob"] = ob_

                def sD(b):
                    pt3 = bank(6, 1)
                    tr(pt3[0:128], st[b]["ob"])
                    for k in range(2):
                        src = View(BPS, k * 64, 64, 6 * 512 + k * 64, ((8, 8), (1, 8)))
                        dst = View(BBIG, k * 64, 64, 8 * T + s0 + b * 8, ((T, 8), (1, 8)))
                        cp("act_copy", dst, src)
                    del st[b]

                for n in range(16 + 3):
                    if n < 16:
                        sA(n)
                    if 0 <= n - 1 < 16:
                        sB(n - 1)
                    if 0 <= n - 2 < 16:
                        sC(n - 2)
                    if 0 <= n - 3 < 16:
                        sD(n - 3)

            def attn_layer(L2, gi, tl, gtile, next_col):
                P.tag = "attn%d" % L2
                lin1(w_q[L2], 4, tl, HT,
                     lambda ps, c, s0, n: act(ATT_Q[c, s0:s0 + n], ps, AF.Identity,
                                              bias=COL[C_BQ + 8 * L2 + c:C_BQ + 8 * L2 + c + 1]))
                attn_prompt_all(L2, gi, [i for i in tl if gtile[i] != 17], gtile)
                for i in tl:
                    if gtile[i] == 17:
                        P.tag = "attn%d/sample" % L2
                        attn_sample(L2, gi, i)
                P.tag = "attn%d" % L2
                bo = load_row(R_BO + L2)
                gpost = load_row(R_SWPOST + L2)
                sss = {i: stat(3) for i in tl}
                OBQ = BBIG.v(0, ((1, GT * D),)).reshape(GT, D)
                pipe = Pipe()
                set_defer(tl)

                def evac(ps, i, half):
                    ob = OBQ[i, half * 512:(half + 1) * 512]
                    tt("dve", ob, ps, bo[half * 512:(half + 1) * 512], ALU.add)
                    junk = talloc(512)
                    act(junk, ob, AF.Square, accum=sss[i][half:half + 1])
                    tt("dve", ob, ob, gpost[half * 512:(half + 1) * 512], ALU.mult)
                    if half == 1:
                        def ssv():
                            tt("dve", sss[i][2:3], sss[i][0:1], sss[i][1:2], ALU.add)
                            return sss[i][2:3]
                        pipe.push(*tail_stages(i, lambda: None, lambda: OBQ[i], ssv, gpost, next_col, None))
                lin2(w_o[L2], 0, tl, ATT_T, evac)
                pipe.flush()

            ATT_Q = UT

            real_cp = cp

            def cp(eng, out, in_):
                if eng == "act_copy":
                    act(out, in_, AF.Copy)
                else:
                    real_cp(eng, out, in_)

            dma(BCON.v(), consts)
            real_cp("pool", BKT.v().reshape(14, 128), View(BCON, 0, 128, K_ZERO, ((0, 14), (1, 128))))
            real_cp("pool", VV2, View(BCON, 0, 128, K_VPAT, ((0, 7), (1, 132))))
            dma(MASKH, maskh)
            dma(COL, colv)
            dma(SK[0:32], rowv[R_SINK:R_SINK + 1, 0:32].partition_broadcast(128))
            ts("dve", SK[32:64], SK[0:32], -1.0, None, ALU.mult)
            ts("dve", SK[66:68], COL[C_SINKS:C_SINKS + 2], -1.0, None, ALU.mult)
            tt("dve", SK[76:92], View(BSK, 0, 128, 32, ((2, 16),)), View(BSK, 0, 128, 33, ((2, 16),)), ALU.min)

            import os as _os
            _stop = int(_os.environ.get("KSTOP", "999"))

            def _ph(k):
                if k >= _stop:
                    raise _Stop()
            try:
              for gi in range(3):
                gtile = [gi * GT + j for j in range(GT)]
                tl = list(range(GT))
                tl2 = [i for i in tl if gtile[i] != 0]
                for i in tl:
                    dma(X[i], xin[gtile[i]], q="pool")
                P.tag = "gmlp0"
                prenorm_T(tl, C_SGPRE)
                _ph(gi * 100 + 0)
                gmlp(0, tl, gtile, C_FPRE)
                _ph(gi * 100 + 1)
                ffn(0, tl, gtile, False, C_SGPRE + 8)
                _ph(gi * 100 + 2)
                gmlp(1, tl, gtile, C_FPRE + 8)
                _ph(gi * 100 + 3)
                ffn(1, tl, gtile, False, C_KVN)
                _ph(gi * 100 + 4)
                kvproj(gi, tl, gtile)
                P.tag = "attn0"
                prenorm_T(tl2, C_SWPRE, stats=False)
                _ph(gi * 100 + 5)
                attn_layer(0, gi, tl2, gtile, C_FPRE + 16)
                _ph(gi * 100 + 6)
                ffn(2, tl2, gtile, False, C_SWPRE + 8)
                _ph(gi * 100 + 7)
                attn_layer(1, gi, tl2, gtile, C_FPRE + 24)
                _ph(gi * 100 + 8)
                ffn(3, tl2, gtile, True, None)
                if gi < 2:
                    last = kvslot(gi, GT - 1)
                    real_cp("pool", KTZ[0, 0], KTZ[0, last])
                    real_cp("pool", KTZ[1, 0], KTZ[1, last])
                    real_cp("pool", VV2[0], VV2[last])
            except _Stop:
                pass
            P.add("sp", lambda e: e.nop(), list(P.outkeys), [])

        P0 = Prog(plan=None)
        run_body(P0)
        P = Prog(plan=P0.wdesc)
        run_body(P)
        assert P.wn == len(P.plan) and P.wissued == len(P.plan), (P.wn, P.wissued, len(P.plan))
        for m, pos in P.blk_dmapos.items():
            if m - NSLOT >= 0:
                assert P.blk_lastread.get(m - NSLOT, -1) < pos, ("slot reuse hazard", m)

        ins = P.ins
        last_w = {}
        readers = {}
        lane_last = {}
        lane_rr = 0
        lane_rr_sp = 0
        for it in ins:
            deps = set()
            for k in it.rk:
                w = last_w.get(k)
                if w is not None:
                    deps.add(w)
            for k in it.wk:
                w = last_w.get(k)
                if w is not None:
                    deps.add(w)
                for r in readers.get(k, ()):
                    deps.add(r)
            if it.dma:
                if it.eng == "sp":
                    it.lane = lane_rr_sp % 4
                    lane_rr_sp += 1
                else:
                    it.lane = 4 + lane_rr % (NLANE - 4)
                    lane_rr += 1
                pl = lane_last.get(it.lane)
                if pl is not None:
                    deps.add(pl)
                lane_last[it.lane] = it.idx
            deps.discard(it.idx)
            if it.eng == "pe":
                deps = {d for d in deps if ins[d].eng != "pe"}
            it.deps = deps
            for k in it.wk:
                last_w[k] = it.idx
                readers[k] = []
            for k in it.rk:
                readers.setdefault(k, []).append(it.idx)
        for it in ins:
            it.ms = False
        for it in ins:
            for d in it.deps:
                ins[d].ms = True
        cnt = {e: 0 for e in ENGS}
        lane_cnt = [0] * NLANE
        for it in ins:
            if it.dma:
                lane_cnt[it.lane] += 1
                it.ev = (lanes[it.lane], 16 * lane_cnt[it.lane])
            elif it.ms:
                cnt[it.eng] += 1
                it.ev = (sems[it.eng], cnt[it.eng])
            else:
                it.ev = None

        per_eng = {e: [] for e in ENGS}
        for it in ins:
            per_eng[it.eng].append(it)

        def emit(ename, e):
            seen = {}
            for it in per_eng[ename]:
                need = {}
                for d in it.deps:
                    sem, val = ins[d].ev
                    key = id(sem)
                    if seen.get(key, 0) >= val:
                        continue
                    if key not in need or need[key][1] < val:
                        need[key] = (sem, val)
                for key, (sem, val) in need.items():
                    e.wait_ge(sem, val)
                    seen[key] = val
                bi = it.fn(e)
                if it.ev is not None:
                    bi.then_inc(it.ev[0], 16 if it.dma else 1)

        with nc.Block() as block:
            @block.sync
            def _(e):
                emit("sp", e)

            @block.tensor
            def _(e):
                emit("pe", e)

            @block.scalar
            def _(e):
                emit("act", e)

            @block.vector
            def _(e):
                emit("dve", e)

            @block.gpsimd
            def _(e):
                emit("pool", e)

    build_program.stats = {e: len(per_eng[e]) for e in ENGS}
    build_program.semmax = (dict(cnt), list(lane_cnt))
    build_program.tags = {e: [it.tag for it in per_eng[e]] for e in ENGS}
    return nc


def _prep_inputs(inp):
    f = lambda a: np.ascontiguousarray(np.asarray(a, dtype=np.float32))
    xp = f(inp["x_prompt"])
    xs = f(inp["x_sample"])
    ckf = f(inp["cache_k"]).reshape(128, 128, 128)
    cvf = f(inp["cache_v"]).reshape(128, 128, 128)
    L = 2
    w_q = f(inp["sw_w_q"]).reshape(L, D, 2, 8, 64).transpose(0, 1, 3, 2, 4).reshape(L, D, D)
    w_o = f(inp["sw_w_o"]).reshape(L, 2, 8, 64, D).transpose(0, 2, 1, 3, 4).reshape(L, D, D)
    b_q = f(inp["sw_b_q"]).reshape(L, 2, 8, 64).transpose(0, 2, 1, 3).reshape(L, D)
    sinks = f(inp["sw_sinks"])
    sinks_p = sinks.reshape(L, 2, 8).transpose(0, 2, 1).reshape(L, 16)
    w_s = f(inp["sg_w_s"])
    b_s = f(inp["sg_b_s"])
    wst = np.zeros((2, 2, 128, 8, 128), np.float32)
    wst[:, 0] = w_s.transpose(0, 3, 1, 2)
    ws8 = w_s[:, :, :8, :8].transpose(0, 3, 1, 2)
    for b in range(16):
        wst[:, 1, b * 8:(b + 1) * 8, :, b * 8:(b + 1) * 8] = ws8
    wst = wst.reshape(2, 2, 128, 1024)
    bsb = np.zeros((2, 2, 8, 128), np.float32)
    bsb[:, 0] = b_s
    bsb[:, 1] = np.tile(b_s[:, :, :8], (1, 1, 16))
    bsb = bsb.reshape(2, 2, 1024)

    def colize(v):
        return v.reshape(8, 128).T

    colv = np.zeros((128, NCOL), np.float32)
    for l in range(2):
        colv[:, C_SGPRE + 8 * l:C_SGPRE + 8 * l + 8] = colize(f(inp["sg_norm_pre"])[l])
        colv[:, C_SWPRE + 8 * l:C_SWPRE + 8 * l + 8] = colize(f(inp["sw_norm_pre"])[l])
        colv[:, C_BQ + 8 * l:C_BQ + 8 * l + 8] = colize(b_q[l])
        colv[:, C_SINKS + l] = np.repeat(sinks[l], 8)
    colv[:, C_KVN:C_KVN + 8] = colize(f(inp["kv_norm"]))
    colv[:, C_EPS_RMS] = RMS_EPS
    colv[:, C_EPS_LN] = LN_EPS
    for l in range(4):
        colv[:, C_FPRE + 8 * l:C_FPRE + 8 * l + 8] = colize(f(inp["f_norm_pre"])[l])
    rowv = np.zeros((NROW, D), np.float32)
    rowv[R_SGPOST:R_SGPOST + 2] = f(inp["sg_norm_post"])
    rowv[R_SWPOST:R_SWPOST + 2] = f(inp["sw_norm_post"])
    rowv[R_FPOST:R_FPOST + 4] = f(inp["f_norm_post"])
    rowv[R_LNG:R_LNG + 2] = f(inp["sg_ln_g"])
    rowv[R_LNB:R_LNB + 2] = f(inp["sg_ln_b"])
    rowv[R_BO:R_BO + 2] = f(inp["sw_b_o"])
    rowv[R_BKV, :256] = f(inp["b_kv"])
    rowv[R_SINK, :32] = sinks_p.reshape(32)

    consts = np.zeros((128, NCONST), np.float32)
    consts[:, K_ID:K_ID + 128] = np.eye(128, dtype=np.float32)
    p = np.arange(128)[:, None]
    j = np.arange(128)[None, :]
    consts[:, K_TRIL:K_TRIL + 128] = (p <= j).astype(np.float32)
    consts[:, K_MASKP:K_MASKP + 128] = np.where(j > p, 0.0, NEG)
    consts[:, K_MASKP + 128:K_MASKP + 256] = np.where(j <= p, 0.0, NEG)
    t_row = (np.arange(128) % 8)[:, None]
    consts[:, K_MASKSC:K_MASKSC + 128] = np.where(j > t_row, 0.0, NEG)
    mw = np.full((128, 248), NEG, np.float32)
    tp = np.arange(8)[None, :]
    mw[:, 120:128] = np.where(tp <= t_row, 0.0, NEG)
    consts[:, K_MWIDE:K_MWIDE + 248] = mw
    consts[:, K_VPAT + 64] = 1.0
    consts[:, K_VPAT + 130] = 1.0
    maskh_norm = consts[:, K_MASKP:K_MASKP + 256].copy()
    maskh_first = maskh_norm.copy()
    maskh_first[:, :128] = NEG

    shared = {
        "w_in": f(inp["sg_w_in"]), "w_out": f(inp["sg_w_out"]), "wst": wst, "bsb": bsb,
        "w_kv": f(inp["w_kv"]), "w_q": np.ascontiguousarray(w_q), "w_o": np.ascontiguousarray(w_o),
        "wg": f(inp["f_w_gate"]), "wu": f(inp["f_w_up"]), "wd": f(inp["f_w_down"]),
        "colv": colv, "rowv": rowv, "consts": consts,
    }
    in_maps = []
    for c in range(8):
        b, q = c // 4, c % 4
        s0 = q * 2048
        xin = np.empty((NTILE, 128, D), np.float32)
        if q == 0:
            xin[0] = xp[b, 0:128]
        else:
            xin[0] = xp[b, s0 - 128:s0]
        xin[1:17] = xp[b, s0:s0 + 2048].reshape(16, 128, D)
        xin[17] = xs[c * 16:(c + 1) * 16].reshape(128, D)
        m = dict(shared)
        m["xin"] = xin
        m["ck"] = np.ascontiguousarray(ckf[c * 16:(c + 1) * 16])
        m["cv"] = np.ascontiguousarray(cvf[c * 16:(c + 1) * 16])
        m["maskh"] = maskh_first if q == 0 else maskh_norm
        in_maps.append(m)
    return in_maps


_NC_CACHE = {}


def kernel(**inputs):
    in_maps = _prep_inputs(inputs)
    if "nc" not in _NC_CACHE:
        _NC_CACHE["nc"] = build_program()
    nc = _NC_CACHE["nc"]
    res = run_bass_kernel_spmd(nc, in_maps, core_ids=list(range(8)))
    rs = res.results
    y_prompt = np.empty((2, 8192, D), np.float32)
    y_sample = np.empty((128, 8, D), np.float32)
    sv_p = np.empty((2, 2, 128, D), np.float32)
    sv_s = np.empty((2, 128, 8, D), np.float32)
    k_p = np.empty((2, 128, 2, 64), np.float32)
    v_p = np.empty((2, 128, 2, 64), np.float32)
    k_s = np.empty((128, 8, 2, 64), np.float32)
    v_s = np.empty((128, 8, 2, 64), np.float32)
    for c in range(8):
        b, q = c // 4, c % 4
        y = np.asarray(rs[c]["y"])
        sv = np.asarray(rs[c]["sv"])
        kvo = np.asarray(rs[c]["kvo"])
        y_prompt[b, q * 2048:(q + 1) * 2048] = y[0:16].reshape(2048, D)
        y_sample[c * 16:(c + 1) * 16] = y[16].reshape(16, 8, D)
        sv_s[:, c * 16:(c + 1) * 16] = sv[:, 1].reshape(2, 16, 8, D)
        k_s[c * 16:(c + 1) * 16] = kvo[1][:, 0:128].reshape(16, 8, 2, 64)
        v_s[c * 16:(c + 1) * 16] = kvo[1][:, 128:256].reshape(16, 8, 2, 64)
        if q == 3:
            sv_p[:, b] = sv[:, 0]
            k_p[b] = kvo[0][:, 0:128].reshape(128, 2, 64)
            v_p[b] = kvo[0][:, 128:256].reshape(128, 2, 64)
    return (y_prompt, y_sample, sv_p, sv_s, k_p, v_p, k_s, v_s)
```

```python
import numpy as np
import concourse.bass as bass
import concourse.mybir as mybir
from concourse.ap import AP
from concourse.bass_utils import run_bass_kernel_spmd

F32 = mybir.dt.float32
F32R = mybir.dt.float32r
AF = mybir.ActivationFunctionType
ALU = mybir.AluOpType
AX = mybir.AxisListType

D = 1024
DFF = 2816
NF = 22
NTILE = 18
GT = 6
T = GT * 128
NSLOT = 5
SLOTF = 2048
RMS_EPS = 1e-6
LN_EPS = 1e-5
NEG = -30000.0

C_SGPRE = 0
C_KVN = 16
C_SWPRE = 24
C_FPRE = 40
C_BQ = 72
C_SINKS = 88
C_EPS_RMS = 90
C_EPS_LN = 91
NCOL = 92
R_SGPOST = 0
R_SWPOST = 2
R_FPOST = 4
R_LNG = 8
R_LNB = 10
R_BO = 12
R_BKV = 14
R_SINK = 15
NROW = 16
K_ID = 0
K_TRIL = 128
K_MASKP = 256
K_MASKSC = 512
K_MWIDE = 640
K_ZERO = 888
K_VPAT = 1016
NCONST = 1016 + 132


class Buf:
    def __init__(self, name, handle, row, gran, r=False):
        self.name, self.h, self.row, self.gran, self.r = name, handle, row, gran, r

    def v(self, off=0, dims=None, p0=0, npart=128):
        if dims is None:
            dims = ((1, self.row - off),)
        return View(self, p0, npart, off, tuple(dims))


class View:
    def __init__(self, buf, p0, npart, off, dims):
        self.buf, self.p0, self.npart, self.off, self.dims = buf, p0, npart, off, dims
        self.blk = None
        self.r = buf.r

    def _cp(self, v):
        v.blk = self.blk
        v.r = self.r
        return v

    def asr(self, flag=True):
        v = View(self.buf, self.p0, self.npart, self.off, self.dims)
        v.blk = self.blk
        v.r = flag
        return v

    def wap(self):
        return self.ap(r=self.r)

    def part(self, p0, n):
        return self._cp(View(self.buf, self.p0 + p0, n, self.off, self.dims))

    def __getitem__(self, idx):
        if not isinstance(idx, tuple):
            idx = (idx,)
        off = self.off
        dims = []
        for i, (s, n) in enumerate(self.dims):
            if i < len(idx):
                ix = idx[i]
                if isinstance(ix, int):
                    assert 0 <= ix < n
                    off += ix * s
                else:
                    a = 0 if ix.start is None else ix.start
                    b = n if ix.stop is None else ix.stop
                    assert 0 <= a < b <= n, (a, b, n)
                    off += a * s
                    dims.append((s, b - a))
            else:
                dims.append((s, n))
        return self._cp(View(self.buf, self.p0, self.npart, off, tuple(dims)))

    def reshape(self, *shape):
        assert len(self.dims) == 1 and self.dims[0][0] == 1
        tot = 1
        for s in shape:
            tot *= s
        assert tot == self.dims[0][1]
        dims = []
        st = tot
        for s in shape:
            st //= s
            dims.append((st, s))
        return self._cp(View(self.buf, self.p0, self.npart, self.off, tuple(dims)))

    def bcast_flat(self, n):
        return self._cp(View(self.buf, self.p0, self.npart, self.off, ((0, n),)))

    def bcast_last(self, n):
        return self._cp(View(self.buf, self.p0, self.npart, self.off, self.dims + ((0, n),)))

    def ap(self, r=False):
        a = AP(self.buf.h, self.p0 * self.buf.row + self.off,
               [[self.buf.row, self.npart]] + [[s, n] for s, n in self.dims])
        return a.bitcast(F32R) if r else a

    def keys(self):
        offs = np.array([self.off])
        for s, n in self.dims:
            if s == 0:
                continue
            offs = (offs[:, None] + (np.arange(n) * s)[None, :]).reshape(-1)
        g = self.buf.gran
        blks = np.unique(offs // g)
        return [(self.buf.name, int(b)) for b in blks]


class Ins:
    __slots__ = ("eng", "fn", "rk", "wk", "deps", "dma", "lane", "ms", "ev", "idx", "tag")


ENGS = ("pe", "act", "dve", "pool", "sp")


class Prog:
    def __init__(self, plan=None):
        self.ins = []
        self.plan = plan
        self.wdesc = []
        self.wn = 0
        self.wissued = 0
        self.blk_lastread = {}
        self.blk_dmapos = {}
        self.psum_ptr = 0
        self.tmp_ptr = 0
        self.ptb_ptr = 0
        self.gb_ptr = 0
        self.st_ptr = 0
        self.outkeys = []
        self.nout = 0
        self.tag = ""
        self.live = set()
        self.defer_map = {}
        self.deferred = []

    def add(self, eng, fn, reads, writes, dma=False):
        i = Ins()
        i.eng, i.fn, i.dma = eng, fn, dma
        rk = []
        for v in reads:
            if isinstance(v, View):
                rk.extend(v.keys())
                if v.blk is not None:
                    self.blk_lastread[v.blk] = len(self.ins)
            else:
                rk.append(v)
        wk = []
        for v in writes:
            if isinstance(v, View):
                wk.extend(v.keys())
            else:
                wk.append(v)
        i.rk, i.wk = rk, wk
        i.tag = self.tag
        i.idx = len(self.ins)
        self.ins.append(i)
        return i


def build_program():
    nc = bass.Bass("TRN2", target_bir_lowering=False)

    def din(name, shape, dt=F32):
        return nc.dram_tensor(name, list(shape), dt, kind="ExternalInput").ap()

    def dout(name, shape):
        return nc.dram_tensor(name, list(shape), F32, kind="ExternalOutput").ap()

    xin = din("xin", [NTILE, 128, D])
    ck = din("ck", [16, 128, 128])
    cv = din("cv", [16, 128, 128])
    w_in = din("w_in", [2, D, 2 * D])
    w_out = din("w_out", [2, D, D])
    wst = din("wst", [2, 2, 128, 1024])
    bsb = din("bsb", [2, 2, 1024])
    w_kv = din("w_kv", [D, 256])
    w_q = din("w_q", [2, D, D])
    w_o = din("w_o", [2, D, D])
    wg = din("wg", [4, D, DFF])
    wu = din("wu", [4, D, DFF])
    wd = din("wd", [4, DFF, D])
    colv = din("colv", [128, NCOL])
    rowv = din("rowv", [NROW, D])
    consts = din("consts", [128, NCONST])
    maskh = din("maskh", [128, 256])
    y_o = dout("y", [17, 128, D])
    sv_o = dout("sv", [2, 2, 128, D])
    kv_o = dout("kvo", [2, 128, 256])

    from contextlib import ExitStack
    es = ExitStack()

    def sb(name, n, gran=128, r=False):
        h = es.enter_context(nc.sbuf_tensor(name, [128, n], F32))
        return Buf(name, h, n, gran, r)

    with es:
        BX = sb("X", GT * D)
        BHT = sb("HT", 8 * T, r=True)
        BBIG = sb("BIG", NF * T, r=True)
        BWS = sb("WS", NSLOT * SLOTF, gran=SLOTF, r=True)
        BKT = sb("KT", 2 * 7 * 128, r=True)
        BVV = sb("VV", 7 * 132, r=True)
        BCON = sb("CON", NCONST, gran=8, r=True)
        BMH = sb("MH", 256, gran=256, r=True)
        BGB = sb("GB", 2 * D, gran=D)
        BTMP = sb("TMP", 3 * D, gran=128)
        BPTB = sb("PTB", 2 * D, gran=128, r=True)
        BSTASH = sb("STASH", D, gran=128)
        BCOL = sb("COL", NCOL, gran=1)
        BSK = sb("SK", 92, gran=1)
        BST = sb("ST", 512, gran=1)
        hps = es.enter_context(nc.psum_tensor("PS", [128, 4096], F32))
        BPS = Buf("PS", hps, 4096, 512)

        sems = {}
        for e in ("pe", "act", "dve", "pool"):
            sems[e] = es.enter_context(nc.semaphore("s_" + e))
        NLANE = 12
        lanes = [es.enter_context(nc.semaphore("l%d" % i)) for i in range(NLANE)]

        X = BX.v().reshape(GT, D)
        HT = BHT.v().reshape(8, T)
        OB2 = BHT.v().reshape(GT, D)
        BIG = BBIG.v().reshape(NF, T)
        UT = BBIG.v(0, ((1, 8 * T),)).reshape(8, T)
        VB = BBIG.v(8 * T, ((1, 8 * T),)).reshape(GT, D)
        ATT_T = BBIG.v(8 * T, ((1, 8 * T),)).reshape(8, T)
        SAMP = BBIG.v(16 * T, ((1, 4096),))
        QBD = SAMP[0:2048].reshape(16, 128)
        KTC = SAMP[2048:4096].reshape(16, 128)
        VC = BHT.v(0, ((1, 2048),)).reshape(16, 128)
        WSTV = BHT.v(0, ((1, 2048),)).reshape(2, 8, 128)
        BSBV = BHT.v(2048, ((1, 2048),)).reshape(2, 8, 128)
        KTZ = BKT.v().reshape(2, 7, 128)
        VV2 = BVV.v().reshape(7, 132)

        def VVd(slot):
            return View(BVV, 0, 128, slot * 132, ((66, 2), (1, 64)))
        IDENT = BCON.v(K_ID, ((1, 128),))
        TRIL = BCON.v(K_TRIL, ((1, 128),))
        MASKP = BCON.v(K_MASKP, ((1, 256),))
        MASKSC = BCON.v(K_MASKSC, ((1, 128),))
        MWIDE = BCON.v(K_MWIDE, ((1, 248),))
        MASKH = BMH.v()
        COL = BCOL.v()
        SK = BSK.v()

        def run_body(P):
            def psum(nb):
                p = P.psum_ptr
                for _ in range(16):
                    if p % nb:
                        p += nb - p % nb
                    if p + nb > 8:
                        p = 0
                    if not any((p + j) in P.live for j in range(nb)):
                        break
                    p += nb
                else:
                    raise RuntimeError("no free PSUM bank")
                P.psum_ptr = p + nb
                return BPS.v(p * 512, ((1, nb * 512),))

            def bank_of(v):
                return v.off // 512

            def bank(b, nb=1):
                return BPS.v(b * 512, ((1, nb * 512),))

            def temp():
                return talloc(D)

            def ptbuf():
                return palloc(D)

            def gbt():
                t = P.gb_ptr
                P.gb_ptr = (t + 1) % 2
                return BGB.v(t * D, ((1, D),))

            def stat(n):
                if P.st_ptr + n > 512:
                    P.st_ptr = 0
                v = BST.v(P.st_ptr, ((1, n),))
                P.st_ptr += n
                return v

            def dma(out, in_, reads=(), writes=(), q=None):
                o = out.wap() if isinstance(out, View) else out
                i = in_.ap() if isinstance(in_, View) else in_
                rd = list(reads) + ([in_] if isinstance(in_, View) else [])
                wr = list(writes) + ([out] if isinstance(out, View) else [])
                if q is None:
                    q = "pool" if (isinstance(out, View) and out.r and not isinstance(in_, View)) else "sp"
                return P.add(q, lambda e: e.dma_start(out=o, in_=i), rd, wr, dma=True)

            def dma_out(dram_ap, src):
                key = ("OUT", P.nout)
                P.nout += 1
                P.outkeys.append(key)
                dma(dram_ap, src, writes=[key])

            def mm(out, lhsT, rhs, start, stop):
                o, l, r = out.ap(), lhsT.ap(r=True), rhs.ap(r=True)
                P.add("pe", lambda e: e.matmul(o, l, r, start=start, stop=stop), [lhsT, rhs], [out])

            def tr(out, in_):
                o, i, idn = out.ap(), in_.ap(), IDENT.ap()

                def f(e):
                    try:
                        return e.transpose(o, i, idn)
                    except Exception:
                        print("TRANSPOSE FAIL", P.tag, o, i)
                        raise
                P.add("pe", f, [in_, IDENT], [out])

            def act(out, in_, func, bias=None, scale=None, accum=None, eng="act", xw=()):
                o, i = out.wap(), in_.ap()
                kw = {}
                rd = [in_]
                wr = [out] + list(xw)
                if bias is not None:
                    if isinstance(bias, View):
                        kw["bias"] = bias.ap()
                        rd.append(bias)
                    else:
                        kw["bias"] = bias
                if scale is not None:
                    if isinstance(scale, View):
                        kw["scale"] = scale.ap()
                        rd.append(scale)
                    else:
                        kw["scale"] = scale
                if accum is not None:
                    kw["accum_out"] = accum.ap()
                    wr.append(accum)
                P.add("act", lambda e: e.activation(o, i, func, **kw), rd, wr)

            def tt(eng, out, a, b, op, xw=()):
                o, x, y_ = out.wap(), a.ap(), b.ap()
                P.add(eng, lambda e: e.tensor_tensor(o, x, y_, op), [a, b], [out] + list(xw))

            def ts(eng, out, a, s1, s2, op0, op1=None):
                o, x = out.wap(), a.ap()
                rd = [a]
                if isinstance(s1, View):
                    rd.append(s1)
                    s1 = s1.ap()
                if isinstance(s2, View):
                    rd.append(s2)
                    s2 = s2.ap()
                if op1 is None:
                    P.add(eng, lambda e: e.tensor_scalar(o, x, s1, None, op0), rd, [out])
                else:
                    P.add(eng, lambda e: e.tensor_scalar(o, x, s1, s2, op0, op1), rd, [out])

            def stt(eng, out, a, sc, b, op0, op1):
                o, x, y_ = out.wap(), a.ap(), b.ap()
                rd = [a, b]
                if isinstance(sc, View):
                    rd.append(sc)
                    sc = sc.ap()
                P.add(eng, lambda e: e.scalar_tensor_tensor(o, x, sc, y_, op0, op1), rd, [out])

            def cp(eng, out, in_):
                o, i = out.wap(), in_.ap()
                P.add(eng, lambda e: e.tensor_copy(o, i), [in_], [out])

            def reduce_max(out, in_):
                o, i = out.ap(), in_.ap()
                P.add("dve", lambda e: e.tensor_reduce(o, i, AX.X, ALU.max), [in_], [out])

            def wget(dram_ap, shape, look=None):
                n = P.wn
                P.wn += 1
                nfl = 1
                for s in shape:
                    nfl *= s
                assert nfl <= SLOTF
                if P.plan is None:
                    P.wdesc.append((dram_ap, tuple(shape)))
                else:
                    if look is None:
                        look = NSLOT - 2
                    while P.wissued < len(P.plan) and P.wissued <= n + look:
                        m = P.wissued
                        dap, shp = P.plan[m]
                        tot = 1
                        for s in shp:
                            tot *= s
                        sv = BWS.v((m % NSLOT) * SLOTF, ((1, tot),)).reshape(*shp)
                        P.blk_dmapos[m] = len(P.ins)
                        dma(sv, dap)
                        P.wissued += 1
                v = BWS.v((n % NSLOT) * SLOTF, ((1, nfl),)).reshape(*shape)
                v.blk = n
                return v

            def w_o1(w2d, c0):
                return wget(w2d.rearrange("(k p) c -> p k c", p=128)[:, :, c0:c0 + 256], (8, 256))

            def w_o2(w2d, kh, c0):
                return wget(w2d[kh * 512:(kh + 1) * 512, :].rearrange("(k p) c -> p k c", p=128)[:, :, c0:c0 + 512],
                            (4, 512))

            def load_row(r, n=D):
                g = gbt()
                dma(g[0:n], rowv[r:r + 1, 0:n].partition_broadcast(128))
                return g

            import os as _os2
            _sub = int(_os2.environ.get("KSUB", "999"))

            class _Stop(Exception):
                pass

            def _ph2(k):
                if k >= _sub:
                    raise _Stop()

            def talloc(n):
                p = P.tmp_ptr
                if p % 128:
                    p += 128 - p % 128
                if p + n > 3 * D:
                    p = 0
                P.tmp_ptr = p + n
                return BTMP.v(p, ((1, n),))

            def palloc(n):
                p = P.ptb_ptr
                if p + n > 2 * D:
                    p = 0
                P.ptb_ptr = p + n
                return BPTB.v(p, ((1, n),))

            def rstd_from_ss(ssv, eps):
                act(ssv, ssv, AF.Sqrt, bias=COL[eps:eps + 1], scale=1.0 / D)
                so, si = ssv.ap(), ssv.ap()
                P.add("dve", lambda e: e.reciprocal(so, si), [ssv], [ssv])

            def subgroups(tl):
                n = len(tl)
                a = (n + 1) // 2
                res = [(tl[0] * 128, a * 128)]
                if n - a > 0:
                    res.append(((tl[0] + a) * 128, (n - a) * 128))
                return res

            RSTD = SK[68:76]

            def preload_sqrt():
                d_ = stat(1)
                act(d_, COL[C_EPS_LN:C_EPS_LN + 1], AF.Sqrt)

            def pre_stats(i):
                ss = RSTD[i:i + 1]
                junk = temp()
                act(junk, X[i], AF.Square, accum=ss)
                rstd_from_ss(ss, C_EPS_RMS)

            XNBUF = [BPTB.v(0, ((1, D),)), BPTB.v(D, ((1, D),)), BSTASH.v(0, ((1, D),))]

            def pre_emit(i, colbase):
                slot = P.defer_map.get(i)
                xn = temp() if slot is None else XNBUF[slot]
                act(xn, X[i], AF.Identity, scale=RSTD[i:i + 1])

                def part2():
                    ps = psum(2)
                    for c in range(8):
                        tr(ps[c * 128:(c + 1) * 128], xn[c * 128:(c + 1) * 128])
                    tt("dve", HT[:, i * 128:(i + 1) * 128], ps.reshape(8, 128),
                       COL[colbase:colbase + 8].bcast_last(128), ALU.mult)
                if slot is None:
                    part2()
                else:
                    P.deferred.append(part2)

            def set_defer(tl):
                n = len(tl)
                a_ = (n + 1) // 2
                P.defer_map = {tl[a_ + j]: j for j in range(n - a_)}

            def run_deferred():
                for f_ in P.deferred:
                    f_()
                P.deferred = []
                P.defer_map = {}

            def prenorm_T(tl, colbase, stats=True):
                P.tag = P.tag.split("/")[0] + "/prenorm"
                if stats:
                    for i in tl:
                        pre_stats(i)
                for i in tl:
                    pre_emit(i, colbase)

            class Pipe:
                def __init__(self, rev=True):
                    self.q = []
                    self.rev = rev

                def push(self, *stages):
                    self.q.append(list(stages))
                    self.step()

                def step(self):
                    n = len(self.q)
                    lags = list(range(len(self.q[-1]) if self.q else 0))
                    if self.rev:
                        lags.reverse()
                    for lag in lags:
                        j = n - 1 - lag
                        if j >= 0 and self.q[j][lag] is not None:
                            f = self.q[j][lag]
                            self.q[j][lag] = None
                            f()

                def flush(self):
                    ns = max((len(x) for x in self.q), default=0)
                    for _ in range(ns):
                        self.q.append([None] * ns)
                        self.step()

            def tail_stages(i, evac_fn, ob_fn, ssv_fn, gpost, next_col, final_out):
                def s1():
                    evac_fn()
                    r = ssv_fn()
                    rstd_from_ss(r, C_EPS_RMS)
                    obs = ob_fn()
                    if isinstance(obs, View):
                        obs = [(X[i], obs)]
                    for xs, ov in obs:
                        stt("dve", xs, ov, r, xs, ALU.mult, ALU.add)
                    if final_out is not None:
                        dma_out(final_out, X[i])

                def s2():
                    if next_col is not None:
                        pre_stats(i)

                def s3():
                    if next_col is not None:
                        pre_emit(i, next_col)
                return s1, s2, s3

            def lin1(wsrc, nblk, tl, src, evac):
                P.tag = P.tag.split("/")[0] + "/lin1"
                sgs = subgroups(tl)
                for blk in range(nblk):
                    slot = w_o1(wsrc, blk * 256)
                    for j in range(2):
                        for (s0, n) in sgs:
                            ps = psum(1)
                            for k in range(8):
                                mm(ps[0:n], slot[k, j * 128:(j + 1) * 128], src[k, s0:s0 + n], k == 0, k == 7)
                            evac(ps[0:n], blk * 2 + j, s0, n)

            def lin2(wsrc, c0, tl, src, evac):
                P.tag = P.tag.split("/")[0] + "/lin2"
                for half in range(2):
                    sA = w_o2(wsrc, 0, c0 + half * 512)
                    sB = w_o2(wsrc, 1, c0 + half * 512)
                    for i in tl:
                        ps = psum(1)
                        for k in range(8):
                            mm(ps, src[k, i * 128:(i + 1) * 128], (sA if k < 4 else sB)[k % 4], k == 0, k == 7)
                        evac(ps, i, half)

            SAMPW = BBIG.v(16 * T, ((1, 4096),))
            WSTV2 = SAMPW[0:2048].reshape(2, 8, 128)
            BSBV2 = SAMPW[2048:4096].reshape(2, 1024)

            def gmlp(L, tl, gtile, next_col):
                P.tag = "gmlp%d/ln" % L
                lng = load_row(R_LNG + L)
                lnb = load_row(R_LNB + L)
                kinds = sorted(set(1 if gtile[i] == 17 else 0 for i in tl))
                for kd in kinds:
                    dma(SAMPW[kd * 1024:(kd + 1) * 1024], wst[L, kd])
                    dma(BSBV2[kd], bsb[L, kd:kd + 1, :].partition_broadcast(128))
                    for g in range(8):
                        tt("pool", WSTV2[kd, g], WSTV2[kd, g], TRIL, ALU.mult)
                _ph2(0)
                pipe = Pipe()

                def ln_stages(i):
                    vbi = VB[i]
                    st = {}

                    def s1():
                        stats = stat(12)
                        a0, a1 = stats[0:6].ap(), stats[6:12].ap()
                        v0, v1 = vbi[0:512], vbi[512:1024]
                        x0, x1 = v0.ap(), v1.ap()
                        P.add("dve", lambda e: e.bn_stats(a0, x0), [v0], [stats[0:6]])
                        P.add("dve", lambda e: e.bn_stats(a1, x1), [v1], [stats[6:12]])
                        mv = stat(2)
                        mva, a2r = mv.ap(), stats.reshape(2, 6).ap()
                        P.add("dve", lambda e: e.bn_aggr(mva, a2r), [stats], [mv])
                        rs = stat(1)
                        act(rs, mv[1:2], AF.Sqrt, bias=COL[C_EPS_LN:C_EPS_LN + 1], scale=1.0)
                        st["mv"], st["rs"] = mv, rs

                    def s2():
                        rs, mv = st["rs"], st["mv"]
                        rso = rs.ap()
                        P.add("dve", lambda e: e.reciprocal(rso, rso), [rs], [rs])
                        ts("dve", vbi, vbi, mv[0:1], rs, ALU.subtract, ALU.mult)

                    def s3():
                        tt("dve", vbi, vbi, lng, ALU.mult)
                        tt("dve", vbi, vbi, lnb, ALU.add)
                        if gtile[i] == 16:
                            dma_out(sv_o[L, 0], vbi)
                        elif gtile[i] == 17:
                            dma_out(sv_o[L, 1], vbi)
                    return s1, s2, s3

                def evac_v(ps, i, half):
                    act(VB[i, half * 512:(half + 1) * 512], ps, AF.Gelu)
                    if half == 1:
                        pipe.push(*ln_stages(i))
                P.tag = "gmlp%d" % L
                lin2(w_in[L], D, tl, HT, evac_v)
                pipe.flush()
                _ph2(1)
                lin1(w_in[L], 4, tl, HT, lambda ps, c, s0, n: act(UT[c, s0:s0 + n], ps, AF.Gelu))
                P.tag = "gmlp%d/mix" % L
                preload_sqrt()
                _ph2(2)
                for i in tl:
                    kd = 1 if gtile[i] == 17 else 0
                    ps = psum(2)
                    for g in range(8):
                        mm(ps[g * 128:(g + 1) * 128], VB[i, g * 128:(g + 1) * 128], WSTV2[kd, g], True, True)
                    tmp = temp()
                    tt("dve", tmp, ps, BSBV2[kd], ALU.add)
                    yv = UT[:, i * 128:(i + 1) * 128]
                    tt("pool", yv, tmp.reshape(8, 128), yv, ALU.mult)
                P.tag = "gmlp%d" % L
                _ph2(3)
                gpost = load_row(R_SGPOST + L)
                sss = {i: stat(3) for i in tl}
                pipe2 = Pipe()
                set_defer(tl)

                def evac(ps, i, half):
                    ob = VB[i, half * 512:(half + 1) * 512]
                    junk = talloc(512)
                    xk = [("PSRD",) + tuple(ps.keys())]
                    act(junk, ps, AF.Square, accum=sss[i][half:half + 1], xw=xk)
                    tt("dve", ob, ps, gpost[half * 512:(half + 1) * 512], ALU.mult, xw=xk)
                    if half == 1:
                        def ssv():
                            tt("dve", sss[i][2:3], sss[i][0:1], sss[i][1:2], ALU.add)
                            return sss[i][2:3]
                        pipe2.push(*tail_stages(i, lambda: None, lambda: VB[i], ssv, gpost, next_col, None))
                lin2(w_out[L], 0, tl, UT, evac)
                pipe2.flush()

            def ffn(L, tl, gtile, final, next_col):
                P.tag = "ffn%d/gateup" % L
                sgs = subgroups(tl)

                def gu(sg_, su_, blk, j, s0, n):
                    f = blk * 2 + j
                    pa = psum(1)
                    pb = psum(1)
                    for k in range(8):
                        mm(pa[0:n], sg_[k, j * 128:(j + 1) * 128], HT[k, s0:s0 + n], k == 0, k == 7)
                    for k in range(8):
                        mm(pb[0:n], su_[k, j * 128:(j + 1) * 128], HT[k, s0:s0 + n], k == 0, k == 7)
                    tmp = talloc(n)
                    act(tmp, pa[0:n], AF.Silu)
                    tt("dve", BIG[f, s0:s0 + n], tmp, pb[0:n], ALU.mult)

                first = 0
                if P.deferred and len(sgs) == 2:
                    wsl = []
                    for blk in range(2):
                        g_ = wget(wg[L].rearrange("(k p) c -> p k c", p=128)[:, :, blk * 256:blk * 256 + 256], (8, 256), look=1)
                        u_ = wget(wu[L].rearrange("(k p) c -> p k c", p=128)[:, :, blk * 256:blk * 256 + 256], (8, 256), look=1)
                        wsl.append((g_, u_))
                    for blk in range(2):
                        for j in range(2):
                            gu(wsl[blk][0], wsl[blk][1], blk, j, *sgs[0])
                    run_deferred()
                    for blk in range(2):
                        for j in range(2):
                            gu(wsl[blk][0], wsl[blk][1], blk, j, *sgs[1])
                    first = 2
                else:
                    run_deferred()
                for blk in range(first, 11):
                    sg_ = w_o1(wg[L], blk * 256)
                    su_ = w_o1(wu[L], blk * 256)
                    for j in range(2):
                        for (s0, n) in sgs:
                            gu(sg_, su_, blk, j, s0, n)
                P.tag = "ffn%d/down" % L
                preload_sqrt()
                gpost = load_row(R_FPOST + L)
                wdl = wd[L].rearrange("(f p) c -> p f c", p=128)
                pipe = Pipe(rev=False)
                sss = {i: stat(3) for i in tl}
                stash = {}
                for idx, i in enumerate(tl):
                    stash[i] = BPTB.v(idx * 512, ((1, 512),)) if idx < 4 else BSTASH.v((idx - 4) * 512, ((1, 512),))
                for half in range(2):
                    accs = {}
                    for i in tl:
                        accs[i] = psum(1)
                        P.live.add(bank_of(accs[i]))

                    def getslot(fb, look=None):
                        f0 = fb * 4
                        nf = min(4, NF - f0)
                        return wget(wdl[:, f0:f0 + nf, half * 512:(half + 1) * 512], (nf, 512), look=look), f0, nf

                    def evac1(i):
                        acc = accs[i]
                        if half == 0:
                            xk = [("PSRD",) + tuple(acc.keys())]
                            junk = talloc(512)
                            act(junk, acc, AF.Square, accum=sss[i][0:1], xw=xk)
                            tt("dve", stash[i], acc, gpost[0:512], ALU.mult, xw=xk)
                            P.live.discard(bank_of(acc))
                        else:
                            def mk(i=i, acc=acc):
                                ob1 = talloc(512)

                                def ev():
                                    xk = [("PSRD",) + tuple(acc.keys())]
                                    junk = talloc(512)
                                    act(junk, acc, AF.Square, accum=sss[i][1:2], xw=xk)
                                    tt("dve", ob1, acc, gpost[512:1024], ALU.mult, xw=xk)
                                    P.live.discard(bank_of(acc))

                                def ssv():
                                    tt("dve", sss[i][2:3], sss[i][0:1], sss[i][1:2], ALU.add)
                                    return sss[i][2:3]
                                s1, s2, s3 = tail_stages(i, ev, lambda: [(X[i, 0:512], stash[i]), (X[i, 512:1024], ob1)],
                                                         ssv, gpost, next_col, y_o[gtile[i] - 1] if final else None)
                                return s1, s2, None, None, s3
                            pipe.push(*mk())

                    nfb_major = 3
                    for fb in range(nfb_major):
                        slot, f0, nf = getslot(fb)
                        for ff in range(nf):
                            f = f0 + ff
                            for i in tl:
                                mm(accs[i], BIG[f, i * 128:(i + 1) * 128], slot[ff], f == 0, f == NF - 1)
                    if True:
                        sl3 = getslot(3, look=2)
                        sl4 = getslot(4, look=2)
                        sl5 = getslot(5, look=2)
                        for i in tl:
                            for (slot, f0, nf) in (sl3, sl4, sl5):
                                for ff in range(nf):
                                    f = f0 + ff
                                    mm(accs[i], BIG[f, i * 128:(i + 1) * 128], slot[ff], f == 0, f == NF - 1)
                            evac1(i)
                    else:
                        for i in tl:
                            evac1(i)
                pipe.flush()

            def kvslot(gi, i):
                return i if gi == 0 else i + 1

            def kvproj(gi, tl, gtile):
                P.tag = "kv"
                slot = wget(w_kv.rearrange("(k p) c -> p k c", p=128), (8, 256))
                bkv = load_row(R_BKV, 256)
                for i in tl:
                    ps = psum(1)
                    for k in range(8):
                        mm(ps[0:256], HT[k, i * 128:(i + 1) * 128], slot[k], k == 0, k == 7)
                    kvt = temp()
                    tt("dve", kvt[0:256], ps[0:256], bkv[0:256], ALU.add)
                    s = kvslot(gi, i)
                    cp("pool", VVd(s), kvt[128:256].reshape(2, 64))
                    if gtile[i] == 16:
                        dma_out(kv_o[0], kvt[0:256])
                    elif gtile[i] == 17:
                        dma_out(kv_o[1], kvt[0:256])
                    pt = psum(1)
                    tr(pt[0:128], kvt[0:128])
                    act(KTZ[0, s].part(0, 64), pt[0:128].part(0, 64), AF.Copy)
                    act(KTZ[1, s].part(64, 64), pt[0:128].part(64, 64), AF.Copy)

            def attn_prompt_all(L2, gi, tiles, gtile):
                P.tag = "attn%d/core" % L2
                items = [(i, g) for i in tiles for g in range(8)]
                st = {}

                def stA(n):
                    i, g = items[n]
                    s = kvslot(gi, i)
                    mask = MASKH if gtile[i] == 1 else MASKP
                    ps = bank(n % 3, 1)
                    for k in range(2):
                        o = ps[k * 256:(k + 1) * 256]
                        keys = BKT.v(k * 896 + (s - 1) * 128, ((1, 256),))
                        mm(o, ATT_Q[g, i * 128:(i + 1) * 128], keys, True, False)
                        mm(o, IDENT, mask, False, True)
                    rmax = stat(1)
                    reduce_max(rmax, ps[0:512])
                    negm = stat(1)
                    sk0 = L2 * 16 + 2 * g
                    pidx = 76 + L2 * 8 + g
                    stt("dve", negm, rmax, -0.125, SK[pidx:pidx + 1], ALU.mult, ALU.min)
                    pb = talloc(512)
                    act(pb, ps[0:512], AF.Exp, bias=negm, scale=0.125)
                    tq = stat(2)
                    tt("dve", tq, SK[sk0:sk0 + 2], negm.bcast_flat(2), ALU.add)
                    act(tq, tq, AF.Exp)
                    st[n] = {"pb": pb, "tq": tq}

                def stB(n):
                    pb = st[n]["pb"]
                    pt = bank(3 + n % 2, 1)
                    for k in range(2):
                        for kt in range(2):
                            c = k * 2 + kt
                            tr(pt[c * 128:(c + 1) * 128], pb[k * 256 + kt * 128:k * 256 + (kt + 1) * 128])
                    ptb = palloc(512)
                    cp("act_copy", ptb, pt)
                    st[n]["ptb"] = ptb

                def stC(n):
                    i, g = items[n]
                    s = kvslot(gi, i)
                    ptb = st[n]["ptb"]
                    tq = st[n]["tq"]
                    po = BPS.v(5 * 512, ((1, 132),))
                    for k in range(2):
                        for kt in range(2):
                            c = k * 2 + kt
                            mm(po[k * 66:(k + 1) * 66], ptb[c * 128:(c + 1) * 128],
                               VV2[s - 1 + kt, k * 66:(k + 1) * 66], kt == 0, kt == 1)
                    tt("dve", tq, tq, View(BPS, 0, 128, 5 * 512 + 64, ((66, 2),)), ALU.add)
                    rden = stat(2)
                    rdo, rdi = rden.ap(), tq.ap()
                    P.add("dve", lambda e: e.reciprocal(rdo, rdi), [tq], [rden])
                    att = talloc(128)
                    tt("dve", att.reshape(2, 64), View(BPS, 0, 128, 5 * 512, ((66, 2), (1, 64))),
                       rden.bcast_last(64), ALU.mult)
                    st[n]["att"] = att

                def stD(n):
                    i, g = items[n]
                    pt2 = BPS.v(6 * 512, ((1, 128),))
                    tr(pt2, st[n]["att"])
                    cp("act_copy", ATT_T[g, i * 128:(i + 1) * 128], pt2)
                    del st[n]

                N = len(items)
                for n in range(N + 3):
                    if n < N:
                        stA(n)
                    if 0 <= n - 1 < N:
                        stB(n - 1)
                    if 0 <= n - 2 < N:
                        stC(n - 2)
                    if 0 <= n - 3 < N:
                        stD(n - 3)

            def attn_sample(L2, gi, i):
                s = kvslot(gi, i)
                s0 = i * 128
                for hb in range(2):
                    stg = temp()
                    dma(stg.reshape(8, 128), ck[hb * 8:(hb + 1) * 8].rearrange("b r c -> r b c"))
                    pt = psum(2)
                    for b in range(8):
                        tr(pt[b * 128:(b + 1) * 128], stg[b * 128:(b + 1) * 128])
                    cp("act_copy", BBIG.v(16 * T + 2048 + hb * 1024, ((1, 1024),)), pt)
                dma(BHT.v(0, ((1, 2048),)).reshape(16, 128), cv.rearrange("b r c -> r b c"))
                qbd_all = BBIG.v(16 * T, ((1, 2048),))
                real_cp("pool", qbd_all.reshape(16, 128), View(BCON, 0, 128, K_ZERO, ((0, 16), (1, 128))))
                for k in range(2):
                    src = View(BBIG, k * 64, 64, 0 * T + s0, ((T, 8), (8, 16), (1, 8)))
                    dst = View(BBIG, k * 64, 64, 16 * T + k * 64, ((8, 8), (128, 16), (1, 8)))
                    cp("pool", dst, src)
                negsk = SK[64 + 2 + L2:64 + 2 + L2 + 1]
                sk = COL[C_SINKS + L2:C_SINKS + L2 + 1]
                st = {}

                def sA(b):
                    ps = bank(b % 3, 1)
                    mm(ps[0:128], QBD[b], KTC[b], True, False)
                    mm(ps[0:128], IDENT, MASKSC, False, True)
                    mm(ps[128:256], QBD[b], KTZ[0, s], True, False)
                    mm(ps[128:256], QBD[b], KTZ[1, s], False, False)
                    mm(ps[128:256], IDENT, MWIDE[(15 - b) * 8:(15 - b) * 8 + 128], False, True)
                    rmax = stat(1)
                    reduce_max(rmax, ps[0:256])
                    negm = stat(1)
                    stt("dve", negm, rmax, -0.125, negsk, ALU.mult, ALU.min)
                    rowsum = stat(1)
                    pb = talloc(256)
                    act(pb, ps[0:256], AF.Exp, bias=negm, scale=0.125, accum=rowsum)
                    tq = stat(1)
                    tt("dve", tq, negm, sk, ALU.add)
                    act(tq, tq, AF.Exp)
                    st[b] = {"pb": pb, "tq": tq, "rowsum": rowsum}

                def sB(b):
                    pb = st[b]["pb"]
                    pt = bank(3 + b % 2, 1)
                    tr(pt[0:128], pb[0:128])
                    tr(pt[128:256], pb[128:256])
                    ptb = palloc(256)
                    cp("dve", ptb, pt[0:256])
                    st[b]["ptb"] = ptb

                def sC(b):
                    tq, rowsum, ptb = st[b]["tq"], st[b]["rowsum"], st[b]["ptb"]
                    tt("dve", tq, tq, rowsum, ALU.add)
                    rden = stat(1)
                    rdo, rdi = rden.ap(), tq.ap()
                    P.add("dve", lambda e: e.reciprocal(rdo, rdi), [tq], [rden])
                    po = bank(5, 1)
                    mm(po[0:128], ptb[0:128], VC[b], True, False)
                    mm(po[0:128].reshape(2, 64), ptb[128:256], VVd(s), False, True)
                    ob_ = talloc(128)
                    act(ob_, po[0:128], AF.Identity, scale=rden)
                    st[b]["ob"] = ob_

                def sD(b):
                    pt3 = bank(6, 1)
                    tr(pt3[0:128], st[b]["ob"])
                    for k in range(2):
                        src = View(BPS, k * 64, 64, 6 * 512 + k * 64, ((8, 8), (1, 8)))
                        dst = View(BBIG, k * 64, 64, 8 * T + s0 + b * 8, ((T, 8), (1, 8)))
                        cp("act_copy", dst, src)
                    del st[b]

                for n in range(16 + 3):
                    if n < 16:
                        sA(n)
                    if 0 <= n - 1 < 16:
                        sB(n - 1)
                    if 0 <= n - 2 < 16:
                        sC(n - 2)
                    if 0 <= n - 3 < 16:
                        sD(n - 3)

            def attn_layer(L2, gi, tl, gtile, next_col):
                P.tag = "attn%d" % L2
                lin1(w_q[L2], 4, tl, HT,
                     lambda ps, c, s0, n: act(ATT_Q[c, s0:s0 + n], ps, AF.Identity,
                                              bias=COL[C_BQ + 8 * L2 + c:C_BQ + 8 * L2 + c + 1]))
                attn_prompt_all(L2, gi, [i for i in tl if gtile[i] != 17], gtile)
                for i in tl:
                    if gtile[i] == 17:
                        P.tag = "attn%d/sample" % L2
                        attn_sample(L2, gi, i)
                P.tag = "attn%d" % L2
                bo = load_row(R_BO + L2)
                gpost = load_row(R_SWPOST + L2)
                sss = {i: stat(3) for i in tl}
                OBQ = BBIG.v(0, ((1, GT * D),)).reshape(GT, D)
                pipe = Pipe()
                set_defer(tl)

                def evac(ps, i, half):
                    ob = OBQ[i, half * 512:(half + 1) * 512]
                    tt("dve", ob, ps, bo[half * 512:(half + 1) * 512], ALU.add)
                    junk = talloc(512)
                    act(junk, ob, AF.Square, accum=sss[i][half:half + 1])
                    tt("dve", ob, ob, gpost[half * 512:(half + 1) * 512], ALU.mult)
                    if half == 1:
                        def ssv():
                            tt("dve", sss[i][2:3], sss[i][0:1], sss[i][1:2], ALU.add)
                            return sss[i][2:3]
                        pipe.push(*tail_stages(i, lambda: None, lambda: OBQ[i], ssv, gpost, next_col, None))
                lin2(w_o[L2], 0, tl, ATT_T, evac)
                pipe.flush()

            ATT_Q = UT

            real_cp = cp

            def cp(eng, out, in_):
                if eng == "act_copy":
                    act(out, in_, AF.Copy)
                else:
                    real_cp(eng, out, in_)

            dma(BCON.v(), consts)
            real_cp("pool", BKT.v().reshape(14, 128), View(BCON, 0, 128, K_ZERO, ((0, 14), (1, 128))))
            real_cp("pool", VV2, View(BCON, 0, 128, K_VPAT, ((0, 7), (1, 132))))
            dma(MASKH, maskh)
            dma(COL, colv)
            dma(SK[0:32], rowv[R_SINK:R_SINK + 1, 0:32].partition_broadcast(128))
            ts("dve", SK[32:64], SK[0:32], -1.0, None, ALU.mult)
            ts("dve", SK[66:68], COL[C_SINKS:C_SINKS + 2], -1.0, None, ALU.mult)
            tt("dve", SK[76:92], View(BSK, 0, 128, 32, ((2, 16),)), View(BSK, 0, 128, 33, ((2, 16),)), ALU.min)

            import os as _os
            _stop = int(_os.environ.get("KSTOP", "999"))

            def _ph(k):
                if k >= _stop:
                    raise _Stop()
            try:
              for gi in range(3):
                gtile = [gi * GT + j for j in range(GT)]
                tl = list(range(GT))
                tl2 = [i for i in tl if gtile[i] != 0]
                for i in tl:
                    dma(X[i], xin[gtile[i]], q="pool")
                P.tag = "gmlp0"
                prenorm_T(tl, C_SGPRE)
                _ph(gi * 100 + 0)
                gmlp(0, tl, gtile, C_FPRE)
                _ph(gi * 100 + 1)
                ffn(0, tl, gtile, False, C_SGPRE + 8)
                _ph(gi * 100 + 2)
                gmlp(1, tl, gtile, C_FPRE + 8)
                _ph(gi * 100 + 3)
                ffn(1, tl, gtile, False, C_KVN)
                _ph(gi * 100 + 4)
                kvproj(gi, tl, gtile)
                P.tag = "attn0"
                prenorm_T(tl2, C_SWPRE, stats=False)
                _ph(gi * 100 + 5)
                attn_layer(0, gi, tl2, gtile, C_FPRE + 16)
                _ph(gi * 100 + 6)
                ffn(2, tl2, gtile, False, C_SWPRE + 8)
                _ph(gi * 100 + 7)
                attn_layer(1, gi, tl2, gtile, C_FPRE + 24)
                _ph(gi * 100 + 8)
                ffn(3, tl2, gtile, True, None)
                if gi < 2:
                    last = kvslot(gi, GT - 1)
                    real_cp("pool", KTZ[0, 0], KTZ[0, last])
                    real_cp("pool", KTZ[1, 0], KTZ[1, last])
                    real_cp("pool", VV2[0], VV2[last])
            except _Stop:
                pass
            P.add("sp", lambda e: e.nop(), list(P.outkeys), [])

        P0 = Prog(plan=None)
        run_body(P0)
        P = Prog(plan=P0.wdesc)
        run_body(P)
        assert P.wn == len(P.plan) and P.wissued == len(P.plan), (P.wn, P.wissued, len(P.plan))
        for m, pos in P.blk_dmapos.items():
            if m - NSLOT >= 0:
                assert P.blk_lastread.get(m - NSLOT, -1) < pos, ("slot reuse hazard", m)

        ins = P.ins
        last_w = {}
        readers = {}
        lane_last = {}
        lane_rr = 0
        lane_rr_sp = 0
        for it in ins:
            deps = set()
            for k in it.rk:
                w = last_w.get(k)
                if w is not None:
                    deps.add(w)
            for k in it.wk:
                w = last_w.get(k)
                if w is not None:
                    deps.add(w)
                for r in readers.get(k, ()):
                    deps.add(r)
            if it.dma:
                if it.eng == "sp":
                    it.lane = lane_rr_sp % 4
                    lane_rr_sp += 1
                else:
                    it.lane = 4 + lane_rr % (NLANE - 4)
                    lane_rr += 1
                pl = lane_last.get(it.lane)
                if pl is not None:
                    deps.add(pl)
                lane_last[it.lane] = it.idx
            deps.discard(it.idx)
            if it.eng == "pe":
                deps = {d for d in deps if ins[d].eng != "pe"}
            it.deps = deps
            for k in it.wk:
                last_w[k] = it.idx
                readers[k] = []
            for k in it.rk:
                readers.setdefault(k, []).append(it.idx)
        for it in ins:
            it.ms = False
        for it in ins:
            for d in it.deps:
                ins[d].ms = True
        cnt = {e: 0 for e in ENGS}
        lane_cnt = [0] * NLANE
        for it in ins:
            if it.dma:
                lane_cnt[it.lane] += 1
                it.ev = (lanes[it.lane], 16 * lane_cnt[it.lane])
            elif it.ms:
                cnt[it.eng] += 1
                it.ev = (sems[it.eng], cnt[it.eng])
            else:
                it.ev = None

        per_eng = {e: [] for e in ENGS}
        for it in ins:
            per_eng[it.eng].append(it)

        def emit(ename, e):
            seen = {}
            for it in per_eng[ename]:
                need = {}
                for d in it.deps:
                    sem, val = ins[d].ev
                    key = id(sem)
                    if seen.get(key, 0) >= val:
                        continue
                    if key not in need or need[key][1] < val:
                        need[key] = (sem, val)
                items = list(need.items())
                emb = None
                if items and not it.dma and ename != "sp":
                    emb = items.pop()
                for key, (sem, val) in items:
                    e.wait_ge(sem, val)
                    seen[key] = val
                bi = it.fn(e)
                if emb is not None:
                    key, (sem, val) = emb
                    bi._wait_ge(sem, val)
                    seen[key] = val
                if it.ev is not None:
                    bi.then_inc(it.ev[0], 16 if it.dma else 1)

        with nc.Block() as block:
            @block.sync
            def _(e):
                emit("sp", e)

            @block.tensor
            def _(e):
                emit("pe", e)

            @block.scalar
            def _(e):
                emit("act", e)

            @block.vector
            def _(e):
                emit("dve", e)

            @block.gpsimd
            def _(e):
                emit("pool", e)

    build_program.stats = {e: len(per_eng[e]) for e in ENGS}
    build_program.semmax = (dict(cnt), list(lane_cnt))
    build_program.tags = {e: [it.tag for it in per_eng[e]] for e in ENGS}
    return nc


def _prep_inputs(inp):
    f = lambda a: np.ascontiguousarray(np.asarray(a, dtype=np.float32))
    xp = f(inp["x_prompt"])
    xs = f(inp["x_sample"])
    ckf = f(inp["cache_k"]).reshape(128, 128, 128)
    cvf = f(inp["cache_v"]).reshape(128, 128, 128)
    L = 2
    w_q = f(inp["sw_w_q"]).reshape(L, D, 2, 8, 64).transpose(0, 1, 3, 2, 4).reshape(L, D, D)
    w_o = f(inp["sw_w_o"]).reshape(L, 2, 8, 64, D).transpose(0, 2, 1, 3, 4).reshape(L, D, D)
    b_q = f(inp["sw_b_q"]).reshape(L, 2, 8, 64).transpose(0, 2, 1, 3).reshape(L, D)
    sinks = f(inp["sw_sinks"])
    sinks_p = sinks.reshape(L, 2, 8).transpose(0, 2, 1).reshape(L, 16)
    w_s = f(inp["sg_w_s"])
    b_s = f(inp["sg_b_s"])
    wst = np.zeros((2, 2, 128, 8, 128), np.float32)
    wst[:, 0] = w_s.transpose(0, 3, 1, 2)
    ws8 = w_s[:, :, :8, :8].transpose(0, 3, 1, 2)
    for b in range(16):
        wst[:, 1, b * 8:(b + 1) * 8, :, b * 8:(b + 1) * 8] = ws8
    wst = wst.reshape(2, 2, 128, 1024)
    bsb = np.zeros((2, 2, 8, 128), np.float32)
    bsb[:, 0] = b_s
    bsb[:, 1] = np.tile(b_s[:, :, :8], (1, 1, 16))
    bsb = bsb.reshape(2, 2, 1024)

    def colize(v):
        return v.reshape(8, 128).T

    colv = np.zeros((128, NCOL), np.float32)
    for l in range(2):
        colv[:, C_SGPRE + 8 * l:C_SGPRE + 8 * l + 8] = colize(f(inp["sg_norm_pre"])[l])
        colv[:, C_SWPRE + 8 * l:C_SWPRE + 8 * l + 8] = colize(f(inp["sw_norm_pre"])[l])
        colv[:, C_BQ + 8 * l:C_BQ + 8 * l + 8] = colize(b_q[l])
        colv[:, C_SINKS + l] = np.repeat(sinks[l], 8)
    colv[:, C_KVN:C_KVN + 8] = colize(f(inp["kv_norm"]))
    colv[:, C_EPS_RMS] = RMS_EPS
    colv[:, C_EPS_LN] = LN_EPS
    for l in range(4):
        colv[:, C_FPRE + 8 * l:C_FPRE + 8 * l + 8] = colize(f(inp["f_norm_pre"])[l])
    rowv = np.zeros((NROW, D), np.float32)
    rowv[R_SGPOST:R_SGPOST + 2] = f(inp["sg_norm_post"])
    rowv[R_SWPOST:R_SWPOST + 2] = f(inp["sw_norm_post"])
    rowv[R_FPOST:R_FPOST + 4] = f(inp["f_norm_post"])
    rowv[R_LNG:R_LNG + 2] = f(inp["sg_ln_g"])
    rowv[R_LNB:R_LNB + 2] = f(inp["sg_ln_b"])
    rowv[R_BO:R_BO + 2] = f(inp["sw_b_o"])
    rowv[R_BKV, :256] = f(inp["b_kv"])
    rowv[R_SINK, :32] = sinks_p.reshape(32)

    consts = np.zeros((128, NCONST), np.float32)
    consts[:, K_ID:K_ID + 128] = np.eye(128, dtype=np.float32)
    p = np.arange(128)[:, None]
    j = np.arange(128)[None, :]
    consts[:, K_TRIL:K_TRIL + 128] = (p <= j).astype(np.float32)
    consts[:, K_MASKP:K_MASKP + 128] = np.where(j > p, 0.0, NEG)
    consts[:, K_MASKP + 128:K_MASKP + 256] = np.where(j <= p, 0.0, NEG)
    t_row = (np.arange(128) % 8)[:, None]
    consts[:, K_MASKSC:K_MASKSC + 128] = np.where(j > t_row, 0.0, NEG)
    mw = np.full((128, 248), NEG, np.float32)
    tp = np.arange(8)[None, :]
    mw[:, 120:128] = np.where(tp <= t_row, 0.0, NEG)
    consts[:, K_MWIDE:K_MWIDE + 248] = mw
    consts[:, K_VPAT + 64] = 1.0
    consts[:, K_VPAT + 130] = 1.0
    maskh_norm = consts[:, K_MASKP:K_MASKP + 256].copy()
    maskh_first = maskh_norm.copy()
    maskh_first[:, :128] = NEG

    shared = {
        "w_in": f(inp["sg_w_in"]), "w_out": f(inp["sg_w_out"]), "wst": wst, "bsb": bsb,
        "w_kv": f(inp["w_kv"]), "w_q": np.ascontiguousarray(w_q), "w_o": np.ascontiguousarray(w_o),
        "wg": f(inp["f_w_gate"]), "wu": f(inp["f_w_up"]), "wd": f(inp["f_w_down"]),
        "colv": colv, "rowv": rowv, "consts": consts,
    }
    in_maps = []
    for c in range(8):
        b, q = c // 4, c % 4
        s0 = q * 2048
        xin = np.empty((NTILE, 128, D), np.float32)
        if q == 0:
            xin[0] = xp[b, 0:128]
        else:
            xin[0] = xp[b, s0 - 128:s0]
        xin[1:17] = xp[b, s0:s0 + 2048].reshape(16, 128, D)
        xin[17] = xs[c * 16:(c + 1) * 16].reshape(128, D)
        m = dict(shared)
        m["xin"] = xin
        m["ck"] = np.ascontiguousarray(ckf[c * 16:(c + 1) * 16])
        m["cv"] = np.ascontiguousarray(cvf[c * 16:(c + 1) * 16])
        m["maskh"] = maskh_first if q == 0 else maskh_norm
        in_maps.append(m)
    return in_maps


_NC_CACHE = {}


def kernel(**inputs):
    in_maps = _prep_inputs(inputs)
    if "nc" not in _NC_CACHE:
        _NC_CACHE["nc"] = build_program()
    nc = _NC_CACHE["nc"]
    res = run_bass_kernel_spmd(nc, in_maps, core_ids=list(range(8)))
    rs = res.results
    y_prompt = np.empty((2, 8192, D), np.float32)
    y_sample = np.empty((128, 8, D), np.float32)
    sv_p = np.empty((2, 2, 128, D), np.float32)
    sv_s = np.empty((2, 128, 8, D), np.float32)
    k_p = np.empty((2, 128, 2, 64), np.float32)
    v_p = np.empty((2, 128, 2, 64), np.float32)
    k_s = np.empty((128, 8, 2, 64), np.float32)
    v_s = np.empty((128, 8, 2, 64), np.float32)
    for c in range(8):
        b, q = c // 4, c % 4
        y = np.asarray(rs[c]["y"])
        sv = np.asarray(rs[c]["sv"])
        kvo = np.asarray(rs[c]["kvo"])
        y_prompt[b, q * 2048:(q + 1) * 2048] = y[0:16].reshape(2048, D)
        y_sample[c * 16:(c + 1) * 16] = y[16].reshape(16, 8, D)
        sv_s[:, c * 16:(c + 1) * 16] = sv[:, 1].reshape(2, 16, 8, D)
        k_s[c * 16:(c + 1) * 16] = kvo[1][:, 0:128].reshape(16, 8, 2, 64)
        v_s[c * 16:(c + 1) * 16] = kvo[1][:, 128:256].reshape(16, 8, 2, 64)
        if q == 3:
            sv_p[:, b] = sv[:, 0]
            k_p[b] = kvo[0][:, 0:128].reshape(128, 2, 64)
            v_p[b] = kvo[0][:, 128:256].reshape(128, 2, 64)
    return (y_prompt, y_sample, sv_p, sv_s, k_p, v_p, k_s, v_s)
```
